# Optimizing a Trainium2 kernel written in Bass

```python
import math
import jax, jax.numpy as jnp
from jax import lax
import numpy as np

D_MODEL = 1024
BATCH = 8
SEQ = 2048
DEPTH = 1

CONV_DIM = D_MODEL // 2
CONV_WIDTH = 3
N_HEADS = 8
HEAD_DIM = 64
N_KV_GROUPS = 2
HEADS_PER_GROUP = N_HEADS // N_KV_GROUPS
NSA_DIM = N_HEADS * HEAD_DIM
KV_DIM = N_KV_GROUPS * HEAD_DIM
ROPE_DIM = HEAD_DIM // 4
ROPE_THETA = 500000.0
CMP_BLOCK = 32
CMP_STRIDE = 16
CMP_HIDDEN = 2 * HEAD_DIM
SEL_BLOCK = 64
N_SEL = 16
WINDOW = 512
Q_BLOCK = 128
SEL_Q_CHUNK = 64
N_NSA_BRANCHES = 3
D_FF = 4 * D_MODEL
ALPHA = (2 * DEPTH) ** 0.25
BETA = (8 * DEPTH) ** -0.25
LN_EPS = 1e-5
NEG = -1e30
FORCE = 1e9

IN_SPLITS = (CONV_DIM, CONV_DIM, CONV_DIM, NSA_DIM,
             KV_DIM, KV_DIM, KV_DIM, KV_DIM, KV_DIM, KV_DIM,
             N_HEADS * N_NSA_BRANCHES, D_MODEL, D_MODEL)
N_IN = sum(IN_SPLITS)

kernel_name = "hybrid_conv_nsa_deepnorm_block"


def layer_norm(x, g, b):
    xf = x.astype(jnp.float32)
    mu = xf.mean(-1, keepdims=True)
    var = jnp.square(xf - mu).mean(-1, keepdims=True)
    y = (xf - mu) * lax.rsqrt(var + LN_EPS)
    return (y * g.astype(jnp.float32) + b.astype(jnp.float32)).astype(x.dtype)


def masked_softmax(s, mask):
    s = jnp.where(mask, s.astype(jnp.float32), NEG)
    m = s.max(-1, keepdims=True)
    e = jnp.where(mask, jnp.exp(s - m), 0.0)
    return e / jnp.maximum(e.sum(-1, keepdims=True), 1e-30)


def rope_tables(seq):
    inv = ROPE_THETA ** (-jnp.arange(0, ROPE_DIM, 2, dtype=jnp.float32) / ROPE_DIM)
    ang = jnp.arange(seq, dtype=jnp.float32)[:, None] * inv[None, :]
    return jnp.cos(ang), jnp.sin(ang)


def partial_rope(x, cos, sin):
    half = ROPE_DIM // 2
    shape = (cos.shape[0],) + (1,) * (x.ndim - 3) + (half,)
    c = cos.reshape(shape).astype(x.dtype)
    s = sin.reshape(shape).astype(x.dtype)
    x1, x2, rest = x[..., :half], x[..., half:ROPE_DIM], x[..., ROPE_DIM:]
    return jnp.concatenate([x1 * c - x2 * s, x2 * c + x1 * s, rest], axis=-1)


def short_conv_mixer(h, b_gate, c_gate, conv_w):
    u = c_gate * h
    y = lax.conv_general_dilated(u, conv_w, window_strides=(1,),
                                 padding=[(CONV_WIDTH - 1, 0)],
                                 dimension_numbers=('NWC', 'WIO', 'NWC'),
                                 feature_group_count=CONV_DIM)
    return b_gate * y


def compress(kv, pe, w1, w2):
    nc = (kv.shape[1] - CMP_BLOCK) // CMP_STRIDE + 1
    idx = jnp.arange(nc)[:, None] * CMP_STRIDE + jnp.arange(CMP_BLOCK)[None, :]
    blocks = kv[:, idx] + pe[:, None, :].astype(kv.dtype)
    hid = jax.nn.gelu(jnp.einsum('bnlgd,ldh->bngh', blocks, w1))
    return jnp.einsum('bngh,hd->bngd', hid, w2)


def cmp_to_sel_matrix(nc, ns):
    start = np.arange(nc)[:, None] * CMP_STRIDE
    s0 = np.arange(ns)[None, :] * SEL_BLOCK
    return jnp.asarray(((start < s0 + SEL_BLOCK) & (start + CMP_BLOCK > s0)).astype(np.float32))


def nsa_attention(q, k_cmp, v_cmp, k_sel, v_sel, k_win, v_win, gate_logits,
                  pe_k, wk1, wk2, pe_v, wv1, wv2, cos, sin):
    B, S = q.shape[0], q.shape[1]
    G, Z, Dh = N_KV_GROUPS, HEADS_PER_GROUP, HEAD_DIM
    scale = Dh ** -0.5
    t = jnp.arange(S)
    q = q.reshape(B, S, G, Z, Dh)
    k_cmp, v_cmp, k_sel, v_sel, k_win, v_win = [
        a.reshape(B, S, G, Dh) for a in (k_cmp, v_cmp, k_sel, v_sel, k_win, v_win)]
    q_rot = partial_rope(q, cos, sin)
    k_sel = partial_rope(k_sel, cos, sin)
    k_win = partial_rope(k_win, cos, sin)

    kc = compress(k_cmp, pe_k, wk1, wk2)
    vc = compress(v_cmp, pe_v, wv1, wv2)
    nc = kc.shape[1]
    s_c = jnp.einsum('bsgzd,bngd->bgzsn', q, kc) * scale
    blk_end = jnp.arange(nc) * CMP_STRIDE + CMP_BLOCK - 1
    p_c = masked_softmax(s_c, blk_end[None, :] <= t[:, None])
    o_cmp = jnp.einsum('bgzsn,bngd->bsgzd', p_c, vc)

    ns = S // SEL_BLOCK
    n_sel = min(N_SEL, ns)
    imp = jnp.einsum('bgzsn,nj->bgsj', p_c, cmp_to_sel_matrix(nc, ns))
    jblk = jnp.arange(ns)[None, :]
    cur = (t // SEL_BLOCK)[:, None]
    forced = (jblk == 0) | (jblk == cur) | (jblk == cur - 1)
    imp = jnp.where(jblk <= cur, jnp.where(forced, FORCE, imp), NEG)
    vals, sel_idx = lax.top_k(imp, n_sel)
    sel_ok = vals > 0.5 * NEG

    kb = k_sel.reshape(B, ns, SEL_BLOCK, G, Dh).transpose(0, 3, 1, 2, 4)
    vb = v_sel.reshape(B, ns, SEL_BLOCK, G, Dh).transpose(0, 3, 1, 2, 4)
    nq = S // SEL_Q_CHUNK
    gather = jax.vmap(jax.vmap(lambda a, i: a[i]))

    def sel_chunk(args):
        qc, ic, okc, tc = args
        kg = gather(kb, ic)
        vg = gather(vb, ic)
        s = jnp.einsum('bcgzd,bgcnld->bgzcnl', qc, kg) * scale
        kpos = ic[..., None] * SEL_BLOCK + jnp.arange(SEL_BLOCK)
        mask = okc[..., None] & (kpos <= tc[:, None, None])
        nk = n_sel * SEL_BLOCK
        p = masked_softmax(s.reshape(B, G, Z, SEL_Q_CHUNK, nk),
                           mask.reshape(B, G, 1, SEL_Q_CHUNK, nk))
        return jnp.einsum('bgzck,bgckd->bcgzd', p, vg.reshape(B, G, SEL_Q_CHUNK, nk, Dh))

    o_sel = lax.map(sel_chunk, (
        q_rot.reshape(B, nq, SEL_Q_CHUNK, G, Z, Dh).transpose(1, 0, 2, 3, 4, 5),
        sel_idx.reshape(B, G, nq, SEL_Q_CHUNK, n_sel).transpose(2, 0, 1, 3, 4),
        sel_ok.reshape(B, G, nq, SEL_Q_CHUNK, n_sel).transpose(2, 0, 1, 3, 4),
        t.reshape(nq, SEL_Q_CHUNK)))
    o_sel = o_sel.transpose(1, 0, 2, 3, 4, 5).reshape(B, S, G, Z, Dh)

    nqb = S // Q_BLOCK
    nb = WINDOW // Q_BLOCK + 1

    def band(a):
        ap = jnp.pad(a, ((0, 0), (WINDOW, 0), (0, 0), (0, 0)))
        ab = ap.reshape(B, (S + WINDOW) // Q_BLOCK, Q_BLOCK, G, Dh)
        return jnp.concatenate([ab[:, j:j + nqb] for j in range(nb)], axis=2)

    kw = band(k_win)
    vw = band(v_win)
    qw = q_rot.reshape(B, nqb, Q_BLOCK, G, Z, Dh)
    s_w = jnp.einsum('biqgzd,bikgd->bgziqk', qw, kw) * scale
    qpos = t.reshape(nqb, Q_BLOCK)
    kpos = jnp.arange(nqb)[:, None] * Q_BLOCK - WINDOW + jnp.arange(nb * Q_BLOCK)[None, :]
    diff = qpos[:, :, None] - kpos[:, None, :]
    wmask = (kpos[:, None, :] >= 0) & (diff >= 0) & (diff < WINDOW)
    p_w = masked_softmax(s_w, wmask)
    o_win = jnp.einsum('bgziqk,bikgd->biqgzd', p_w, vw).reshape(B, S, G, Z, Dh)

    g = jax.nn.sigmoid(gate_logits.astype(jnp.float32)).reshape(B, S, G, Z, N_NSA_BRANCHES)
    o = g[..., 0:1] * o_cmp + g[..., 1:2] * o_sel + g[..., 2:3] * o_win
    return o.reshape(B, S, NSA_DIM).astype(q.dtype)


def hybrid_layer(x, w_in, conv_w, w_conv_out, pe_k_cmp, w_k_cmp1, w_k_cmp2,
                 pe_v_cmp, w_v_cmp1, w_v_cmp2, w_nsa_out, w_o, ln1_g, ln1_b,
                 w_up, w_down, ln2_g, ln2_b, cos, sin):
    proj = x @ w_in
    offs = np.cumsum(IN_SPLITS)[:-1].tolist()
    (h, b_gate, c_gate, q, k_cmp, v_cmp, k_sel, v_sel, k_win, v_win,
     nsa_gates, g_conv, g_nsa) = jnp.split(proj, offs, axis=-1)
    y_conv = short_conv_mixer(h, b_gate, c_gate, conv_w) @ w_conv_out
    y_nsa = nsa_attention(q, k_cmp, v_cmp, k_sel, v_sel, k_win, v_win, nsa_gates,
                          pe_k_cmp, w_k_cmp1, w_k_cmp2, pe_v_cmp, w_v_cmp1, w_v_cmp2,
                          cos, sin) @ w_nsa_out
    mixed = jax.nn.sigmoid(g_conv) * y_conv + jax.nn.sigmoid(g_nsa) * y_nsa
    x = layer_norm(ALPHA * x + mixed @ w_o, ln1_g, ln1_b)
    ff = jnp.square(jax.nn.relu(x @ w_up)) @ w_down
    return layer_norm(ALPHA * x + ff, ln2_g, ln2_b)


def setup_inputs(seed: int = 0) -> dict:
    key = jax.random.key(seed)
    ks = jax.random.split(key, 20)
    nrm = lambda k, shape, s: jax.random.normal(k, shape, jnp.float32) * s
    L = DEPTH
    return {
        "x": nrm(ks[0], (BATCH, SEQ, D_MODEL), 1.0),
        "w_in": nrm(ks[1], (L, D_MODEL, N_IN), D_MODEL ** -0.5),
        "conv_w": nrm(ks[2], (L, CONV_WIDTH, 1, CONV_DIM), CONV_WIDTH ** -0.5),
        "w_conv_out": nrm(ks[3], (L, CONV_DIM, D_MODEL), CONV_DIM ** -0.5),
        "pe_k_cmp": nrm(ks[4], (L, CMP_BLOCK, HEAD_DIM), 0.5),
        "w_k_cmp1": nrm(ks[5], (L, CMP_BLOCK, HEAD_DIM, CMP_HIDDEN), (CMP_BLOCK * HEAD_DIM) ** -0.5),
        "w_k_cmp2": nrm(ks[6], (L, CMP_HIDDEN, HEAD_DIM), CMP_HIDDEN ** -0.5),
        "pe_v_cmp": nrm(ks[7], (L, CMP_BLOCK, HEAD_DIM), 0.5),
        "w_v_cmp1": nrm(ks[8], (L, CMP_BLOCK, HEAD_DIM, CMP_HIDDEN), (CMP_BLOCK * HEAD_DIM) ** -0.5),
        "w_v_cmp2": nrm(ks[9], (L, CMP_HIDDEN, HEAD_DIM), CMP_HIDDEN ** -0.5),
        "w_nsa_out": nrm(ks[10], (L, NSA_DIM, D_MODEL), NSA_DIM ** -0.5),
        "w_o": nrm(ks[11], (L, D_MODEL, D_MODEL), BETA * D_MODEL ** -0.5),
        "ln1_g": 1.0 + nrm(ks[12], (L, D_MODEL), 0.02),
        "ln1_b": nrm(ks[13], (L, D_MODEL), 0.02),
        "w_up": nrm(ks[14], (L, D_MODEL, D_FF), D_MODEL ** -0.5),
        "w_down": nrm(ks[15], (L, D_FF, D_MODEL), BETA * D_FF ** -0.5),
        "ln2_g": 1.0 + nrm(ks[16], (L, D_MODEL), 0.02),
        "ln2_b": nrm(ks[17], (L, D_MODEL), 0.02),
    }


def reference(x, w_in, conv_w, w_conv_out, pe_k_cmp, w_k_cmp1, w_k_cmp2,
              pe_v_cmp, w_v_cmp1, w_v_cmp2, w_nsa_out, w_o, ln1_g, ln1_b,
              w_up, w_down, ln2_g, ln2_b):
    cos, sin = rope_tables(x.shape[1])
    for l in range(DEPTH):
        x = hybrid_layer(x, w_in[l], conv_w[l], w_conv_out[l], pe_k_cmp[l], w_k_cmp1[l],
                         w_k_cmp2[l], pe_v_cmp[l], w_v_cmp1[l], w_v_cmp2[l], w_nsa_out[l],
                         w_o[l], ln1_g[l], ln1_b[l], w_up[l], w_down[l], ln2_g[l], ln2_b[l],
                         cos, sin)
    return x
```

```python
import numpy as np
import ml_dtypes
import concourse.bass as bass
import concourse.mybir as mybir
from concourse.bass_utils import run_bass_kernel_spmd

F32 = mybir.dt.float32
BF16 = mybir.dt.bfloat16
AF = mybir.ActivationFunctionType
ALU = mybir.AluOpType

S = 2048
D = 1024
NCORES = 8
ALPHA = 2.0 ** 0.25
LN_EPS = 1e-5
BIG = 32768.0
ROPE_THETA = 500000.0

ENGS = ["pe", "act", "dve", "pool", "sp"]
N_DMA_SEMS = 24
SB_BASE = 16512
SB_END = 229376 - 64

KEEP_WARM = 0
DEBUG_STAGE = None


class Buf:
    __slots__ = ("name", "w", "r", "excl")

    def __init__(self, name="", excl=False):
        self.name = name
        self.w = None
        self.r = {}
        self.excl = excl


class Prog:
    def __init__(self, nc):
        self.nc = nc
        self.ops = {e: [] for e in ENGS}
        self.cnt = {e: 0 for e in ENGS}
        self.waited = {e: {} for e in ENGS}
        self.sems = {}
        self._ctx = []
        for e in ["pe", "act", "dve", "pool"]:
            self.sems[e] = self._sem("c_" + e)
        self.dma_cnt = []
        for i in range(N_DMA_SEMS):
            self.sems[("dma", i)] = self._sem("d_%d" % i)
            self.dma_cnt.append(0)
        self.dma_rr = 0
        self.dma_rr_sw = 0
        self.out_tokens = []

    def _sem(self, name):
        cm = self.nc.semaphore(name)
        s = cm.__enter__()
        self._ctx.append(cm)
        return s

    def close(self):
        for cm in reversed(self._ctx):
            cm.__exit__(None, None, None)

    def _filter(self, eng, deps):
        out = []
        wd = self.waited[eng]
        for k, v in deps.items():
            if eng == "pe" and k == "pe":
                continue
            if wd.get(k, 0) >= v:
                continue
            wd[k] = v
            out.append((k, v))
        return out

    def _deps(self, eng, reads, writes):
        deps = {}

        def add(k, v):
            if deps.get(k, 0) < v:
                deps[k] = v

        for b in reads:
            if b.w is not None:
                add(*b.w)
            if b.excl:
                for k, v in b.r.items():
                    if k != eng:
                        add(k, v)
        for b in writes:
            if b.w is not None:
                add(*b.w)
            for k, v in b.r.items():
                add(k, v)
        return self._filter(eng, deps)

    def _mark(self, tok, reads, writes):
        for b in reads:
            if b.r.get(tok[0], 0) < tok[1]:
                b.r[tok[0]] = tok[1]
        for b in writes:
            b.w = tok
            b.r = {}

    def op(self, eng, fn, reads=(), writes=()):
        waits = self._deps(eng, reads, writes)
        self.cnt[eng] += 1
        tok = (eng, self.cnt[eng])
        self.ops[eng].append((waits, fn, (eng, 1)))
        self._mark(tok, reads, writes)
        return tok

    def dma(self, q, fn, reads=(), writes=(), is_output=False):
        half = N_DMA_SEMS // 2
        if q == "pool":
            i = half + self.dma_rr_sw
            self.dma_rr_sw = (self.dma_rr_sw + 1) % half
        else:
            i = self.dma_rr
            self.dma_rr = (self.dma_rr + 1) % half
        k = ("dma", i)
        waits = self._deps(q, reads, writes)
        prev = self.dma_cnt[i]
        if prev > 0 and self.waited[q].get(k, 0) < prev:
            self.waited[q][k] = prev
            waits.append((k, prev))
        self.dma_cnt[i] += 16
        tok = (k, self.dma_cnt[i])
        self.ops[q].append((waits, fn, (k, 16)))
        self._mark(tok, reads, writes)
        if is_output:
            self.out_tokens.append(tok)
        return tok

    def barrier(self):
        deps = {}
        for e in ["pe", "act", "dve", "pool"]:
            if self.cnt[e] > 0:
                deps[e] = self.cnt[e]
        for i in range(N_DMA_SEMS):
            if self.dma_cnt[i] > 0:
                deps[("dma", i)] = self.dma_cnt[i]
        for e in ENGS:
            d = dict(deps)
            waits = []
            wd = self.waited[e]
            for k, v in d.items():
                if wd.get(k, 0) >= v:
                    continue
                wd[k] = v
                waits.append((k, v))
            if waits:
                self.ops[e].append((waits, None, None))

    def finish(self):
        waits = []
        for k, v in self.out_tokens:
            if self.waited["sp"].get(k, 0) < v:
                self.waited["sp"][k] = v
                waits.append((k, v))
        self.ops["sp"].append((waits, None, None))

    def _plan_signals(self):
        needed = {e: set() for e in ["pe", "act", "dve", "pool"]}
        for e in ENGS:
            for waits, fn, inc in self.ops[e]:
                for k, v in waits:
                    if k in needed:
                        needed[k].add(v)
        self.sigmap = {}
        for e in needed:
            m = {}
            c = 0
            for idx in range(1, self.cnt[e] + 1):
                if idx in needed[e]:
                    c += 1
                    m[idx] = c
            self.sigmap[e] = m

    def replay(self, eng, e):
        idx = 0
        for waits, fn, inc in self.ops[eng]:
            for k, v in waits:
                if k in self.sigmap:
                    v = self.sigmap[k][v]
                e.wait_ge(self.sems[k], v)
            if fn is None:
                continue
            ins = fn(e)
            if inc[0] in self.sigmap:
                idx += 1
                if idx in self.sigmap[inc[0]]:
                    ins.then_inc(self.sems[inc[0]], 1)
            else:
                ins.then_inc(self.sems[inc[0]], inc[1])

    def run_block(self):
        self.finish()
        self._plan_signals()
        with self.nc.Block() as block:
            @block.tensor
            def _(e):
                self.replay("pe", e)

            @block.scalar
            def _(e):
                self.replay("act", e)

            @block.vector
            def _(e):
                self.replay("dve", e)

            @block.gpsimd
            def _(e):
                self.replay("pool", e)

            @block.sync
            def _(e):
                self.replay("sp", e)


class Arena:
    def __init__(self, nc, lo, hi, tag):
        self.nc, self.lo, self.hi, self.cur, self.tag = nc, lo, hi, lo, tag
        self.n = 0

    def take(self, name, shape, dt):
        per = 1
        for s_ in shape[1:]:
            per *= s_
        nbytes = per * (2 if dt == BF16 else 4)
        self.cur = (self.cur + 31) // 32 * 32
        assert self.cur + nbytes <= self.hi, (self.tag, name, self.cur, nbytes, self.hi)
        t = self.nc.alloc_sbuf_tensor_at("%s_%s" % (self.tag, name), list(shape), dt, offset=self.cur)
        self.cur += nbytes
        return t


def pe_fn(mms):
    def fn(e):
        last = None
        for (o, l, r, st, sp) in mms:
            last = e.matmul(o, lhsT=l, rhs=r, start=st, stop=sp)
        return last
    return fn


def copy_fn(eng, out, in_):
    if eng == "act":
        return lambda e: e.copy(out, in_)
    return lambda e: e.tensor_copy(out, in_)


def act_fn(out, in_, func, scale=1.0, bias=None):
    if bias is None:
        return lambda e: e.activation(out, in_, func, scale=scale)
    return lambda e: e.activation(out, in_, func, bias=bias, scale=scale)


def tt_fn(out, a, b, op):
    return lambda e: e.tensor_tensor(out, a, b, op)


def ts_fn(out, a, s1, s2, op0, op1=None):
    if op1 is None:
        return lambda e: e.tensor_scalar(out, a, s1, None, op0=op0)
    return lambda e: e.tensor_scalar(out, a, s1, s2, op0=op0, op1=op1)


def stt_fn(out, a, s, b, op0, op1):
    return lambda e: e.scalar_tensor_tensor(out, a, s, b, op0=op0, op1=op1)


def dma_fn(out, in_):
    return lambda e: e.dma_start(out=out, in_=in_)


def build_program(debug_stage=None):
    nc = bass.Bass("TRN2", target_bir_lowering=False)

    def din(name, shape, dt=F32):
        return nc.dram_tensor(name, list(shape), dt, kind="ExternalInput").ap()

    x_d = din("x", [S, D])
    w_hbc_d = din("w_hbc", [128, 8, 4, 384])
    w_att_d = din("w_att", [128, 8, 2, 640])
    w_gate_d = din("w_gate", [128, 8, 24])
    w_g2_d = din("w_g2", [128, 8, 8, 256])
    cw_d = din("cw", [128, 4, 3])
    wco_d = din("wco", [128, 4, 1024])
    wno_d = din("wno", [128, 4, 1024])
    wo_d = din("wo", [128, 8, 1024])
    wup_d = din("wup", [128, 8, 4096])
    wdn_d = din("wdn", [128, 32, 1024])
    w1_d = [din("w1k", [64, 32, 128]), din("w1v", [64, 32, 128])]
    w2_d = [din("w2k", [128, 64]), din("w2v", [128, 64])]
    peT_d = [din("peTk", [64, 32]), din("peTv", [64, 32])]
    lnp_d = din("lnp", [128, 4, 1024])
    ident_d = din("c_ident", [128, 128], BF16)
    tri_d = din("c_tri", [128, 128], BF16)
    anti_d = din("c_anti", [128, 128], BF16)
    cmask_d = din("c_cmask", [128, S], BF16)
    e30_d = din("c_e30", [32, S], BF16)
    maug_d = din("c_maug", [128, 33], BF16)
    perm_d = din("c_perm", [64, 64], BF16)
    selall_d = din("c_selall", [56, 24, 64], BF16)
    keep_d = din("c_keep", [128, 8, 32])
    forceb_d = din("c_forceb", [128, 8, 32])
    ct_d = din("c_ct", [64, S])
    st_d = din("c_st", [64, S])

    y_d = nc.dram_tensor("y", [S, D], F32, kind="ExternalOutput").ap()
    dbg_d = None
    if debug_stage is not None:
        dbg_d = nc.dram_tensor("dbg", [128, 16384], F32, kind="ExternalOutput").ap()

    P = Prog(nc)
    K = 1024
    base = SB_BASE

    pb = [nc.alloc_psum_tensor("pb%d" % i, [128, 512], F32) for i in range(8)]
    PB = [Buf("pb%d" % i, excl=True) for i in range(8)]

    class Rot:
        def __init__(self, ids):
            self.ids, self.i = ids, 0

        def __call__(self):
            b = self.ids[self.i]
            self.i = (self.i + 1) % len(self.ids)
            return b

    A0 = Arena(nc, base, base + 3 * K, "p")
    ident = A0.take("ident", [128, 128], BF16)
    triM = A0.take("tri", [128, 128], BF16)
    antiM = A0.take("anti", [128, 128], BF16)
    permM = A0.take("perm", [64, 64], BF16)
    maug = A0.take("maug", [128, 33], BF16)
    kcT = A0.take("kcT", [64, 2, 128], BF16)
    vcaug = A0.take("vcaug", [128, 2, 128], BF16)
    cw = A0.take("cw", [128, 4, 3], F32)
    B_const = Buf("const")
    B_kcT = [Buf("kcT0"), Buf("kcT1")]
    B_vc = [Buf("vc0"), Buf("vc1")]
    base = SB_BASE - 3 * K
    AX = Arena(nc, base + 6 * K, base + 38 * K, "x")
    xT = AX.take("xT", [128, 8, S], BF16)
    B_xT = [Buf("xT%d" % n) for n in range(16)]

    for (t, d) in [(ident, ident_d), (triM, tri_d), (antiM, anti_d), (permM, perm_d), (maug, maug_d), (cw, cw_d)]:
        P.dma("sp", dma_fn(t[:], d), writes=[B_const])
    P.op("dve", lambda e: e.memset(vcaug[:, :, 64:128], 1.0), writes=B_vc)

    TAc = Arena(nc, base + 70 * K, base + 112 * K, "tac")
    KA = TAc.take("KA", [96, S], BF16)
    sgT = TAc.take("sgT", [56, S], BF16)
    cmask = TAc.take("cmask", [128, S], BF16)
    CT = TAc.take("CT", [64, S], F32)
    ST = TAc.take("ST", [64, S], F32)
    selall = TAc.take("selall", [56, 24, 64], BF16)
    keep = TAc.take("keep", [128, 8, 32], F32)
    forceb = TAc.take("forceb", [128, 8, 32], F32)
    wA = TAc.take("wA", [128, 8, 512], BF16)
    wg = TAc.take("wg", [128, 8, 24], BF16)
    B_catt = Buf("catt")
    B_wA, B_wg = Buf(), Buf()
    for (t, d) in [(cmask, cmask_d), (selall, selall_d), (keep, keep_d), (forceb, forceb_d), (CT, ct_d), (ST, st_d)]:
        P.dma("sp", dma_fn(t[:], d), writes=[B_catt])
    P.dma("sp", dma_fn(KA[64:96, :], e30_d), writes=[B_catt])

    def dump(ap_list):
        col = 0
        for (ap, bufs, dt) in ap_list:
            p, n = ap.shape
            if dt == F32:
                P.dma("sp", dma_fn(dbg_d[0:p, col:col + n], ap), reads=bufs, is_output=True)
            else:
                P.dma("pool", dma_fn(dbg_d[0:p, col:col + n], ap), reads=bufs, is_output=True)
            col += n
        P.run_block()
        P.close()
        return nc

    T0 = Arena(nc, base + 38 * K, base + 100 * K, "t0")
    xb = T0.take("xb", [128, 16, D], BF16)
    B_xb = [Buf("xb%d" % c) for c in range(4)]
    x_v = x_d.rearrange("(n p) d -> p n d", p=128)
    for c in range(4):
        P.dma("pool", dma_fn(xb[:, 4 * c:4 * c + 4, :], x_v[:, 4 * c:4 * c + 4, :]), writes=[B_xb[c]])
    TC = Arena(nc, SB_END - 42 * K, SB_END, "tc")
    kcmpT = TC.take("kcmpT", [64, 4, S], BF16)
    w1 = [TC.take("w1k", [64, 32, 128], BF16), TC.take("w1v", [64, 32, 128], BF16)]
    w2 = [TC.take("w2k", [128, 64], BF16), TC.take("w2v", [128, 64], BF16)]
    peT = [TC.take("peTk", [64, 32], BF16), TC.take("peTv", [64, 32], BF16)]
    wck = [TC.take("wck0", [128, 8, 128], BF16), TC.take("wck1", [128, 8, 128], BF16)]
    B_w1 = [Buf(), Buf()]
    B_wck = [Buf(), Buf()]
    for g in range(2):
        P.dma("pool", dma_fn(wck[g][:], w_att_d[:, :, g, 512:640]), writes=[B_wck[g]])
    for kind in range(2):
        P.dma("pool", dma_fn(w1[kind][:], w1_d[kind]), writes=[B_w1[kind]])
        P.dma("pool", dma_fn(w2[kind][:], w2_d[kind]), writes=[B_w1[kind]])
        P.dma("pool", dma_fn(peT[kind][:], peT_d[kind]), writes=[B_w1[kind]])
    P.dma("pool", dma_fn(wg[:], w_gate_d), writes=[B_wg])
    P.dma("pool", dma_fn(wA[:], w_att_d[:, :, 0, 0:512]), writes=[B_wA])
    rot_all = Rot([0, 1, 2, 3, 4, 5, 6, 7])
    alt = 0
    for n in range(16):
        for half in range(2):
            b = rot_all()
            mms = []
            for j in range(4):
                kc = half * 4 + j
                mms.append((pb[b][:, j * 128:(j + 1) * 128], xb[:, n, kc * 128:(kc + 1) * 128], ident[:], True, True))
            P.op("pe", pe_fn(mms), reads=[B_xb[n // 4], B_const], writes=[PB[b]])
            eng = ["act", "dve"][alt % 2]
            alt += 1
            P.op(eng, copy_fn(eng, xT[:, half * 4:half * 4 + 4, n * 128:(n + 1) * 128],
                              pb[b][:, :].rearrange("p (a b) -> p a b", a=4)),
                 reads=[PB[b]], writes=[B_xT[n]])
    if debug_stage == "xT":
        tmpf = T0.take("dbgf", [128, 4096], F32)
        Bt = Buf()
        for kc in range(2):
            P.op("dve", copy_fn("dve", tmpf[:, kc * 2048:(kc + 1) * 2048], xT[:, kc, :]), reads=B_xT, writes=[Bt])
        return dump([(tmpf[:, :], [Bt], F32)])

    def proj_fm(w_tile, wcol0, m, tb, bank, B_w):
        mms = []
        for kc in range(8):
            mms.append((pb[bank][0:m, :], w_tile[:, kc, wcol0:wcol0 + m], xT[:, kc, tb * 512:(tb + 1) * 512],
                        kc == 0, kc == 7))
        P.op("pe", pe_fn(mms), reads=[B_w] + B_xT[4 * tb:4 * tb + 4], writes=[PB[bank]])

    bias_sb = TC.take("bias", [128, 1], F32)
    gx = TC.take("gx", [128, 128], F32)
    gx2 = TC.take("gx2", [128, 128], F32)
    gz = TC.take("gz", [128, 128], F32)
    gs = TC.take("gs", [128, 128], F32)
    hid = TC.take("hid", [128, 128], BF16)
    B_kcmp = [Buf() for _ in range(4)]
    B_bias, B_gx, B_gx2, B_gz, B_gs, B_hid = Buf(), Buf(), Buf(), Buf(), Buf(), Buf()
    rot4 = Rot([0, 1, 2, 3])
    alt = 0
    for g in range(2):
        for kind in range(2):
            idx = kind * 2 + g
            for tb in range(4):
                b = rot4()
                proj_fm(wck[g], kind * 64, 64, tb, b, B_wck[g])
                eng = ["act", "dve"][alt % 2]
                alt += 1
                P.op(eng, copy_fn(eng, kcmpT[:, idx, tb * 512:(tb + 1) * 512], pb[b][0:64, :]),
                     reads=[PB[b]], writes=[B_kcmp[idx]])
    for kind in range(2):
        for g in range(2):
            idx = kind * 2 + g
            bh, bbias, bo = 4, 5, 6
            mms = []
            for l in range(32):
                mms.append((pb[bh][:, 0:127], w1[kind][:, l, :], kcmpT[:, idx, l:l + 16 * 126 + 1:16], l == 0, l == 31))
            P.op("pe", pe_fn(mms), reads=[B_w1[kind], B_kcmp[idx]], writes=[PB[bh]])
            mms = []
            for l in range(32):
                mms.append((pb[bbias][:, 0:1], w1[kind][:, l, :], peT[kind][:, l:l + 1], l == 0, l == 31))
            P.op("pe", pe_fn(mms), reads=[B_w1[kind]], writes=[PB[bbias]])
            P.op("dve", copy_fn("dve", bias_sb[:, :], pb[bbias][:, 0:1]), reads=[PB[bbias]], writes=[B_bias])
            P.op("dve", ts_fn(gx[:, 0:127], pb[bh][:, 0:127], bias_sb[:, 0:1], None, ALU.add),
                 reads=[PB[bh], B_bias], writes=[B_gx])
            P.op("dve", tt_fn(gx2[:, 0:127], gx[:, 0:127], gx[:, 0:127], ALU.mult), reads=[B_gx], writes=[B_gx2])
            P.op("dve", ts_fn(gx2[:, 0:127], gx2[:, 0:127], 0.044715, 1.0, ALU.mult, ALU.add),
                 reads=[B_gx2], writes=[B_gx2])
            P.op("dve", tt_fn(gz[:, 0:127], gx2[:, 0:127], gx[:, 0:127], ALU.mult), reads=[B_gx2, B_gx], writes=[B_gz])
            P.op("act", act_fn(gs[:, 0:127], gz[:, 0:127], AF.Sigmoid, scale=1.5957691216057308),
                 reads=[B_gz], writes=[B_gs])
            P.op("dve", tt_fn(hid[:, 0:127], gx[:, 0:127], gs[:, 0:127], ALU.mult), reads=[B_gx, B_gs], writes=[B_hid])
            if kind == 0:
                P.op("pe", pe_fn([(pb[bo][0:64, 0:127], w2[0][:, :], hid[:, 0:127], True, True)]),
                     reads=[B_w1[0], B_hid], writes=[PB[bo]])
                P.op("dve", copy_fn("dve", kcT[:, g, 0:127], pb[bo][0:64, 0:127]), reads=[PB[bo]], writes=[B_kcT[g]])
            else:
                P.op("pe", pe_fn([(pb[bo][0:127, 0:64], hid[:, 0:127], w2[1][:, :], True, True)]),
                     reads=[B_w1[1], B_hid], writes=[PB[bo]])
                P.op("dve", copy_fn("dve", vcaug[0:127, g, 0:64], pb[bo][0:127, 0:64]), reads=[PB[bo]], writes=[B_vc[g]])
    if debug_stage == "cmp":
        tmpf = TC.take("dbgf", [128, 512], F32)
        Bt = Buf()
        P.op("dve", lambda e: e.memset(tmpf[:, :], 0.0), writes=[Bt])
        for g in range(2):
            P.op("dve", copy_fn("dve", tmpf[0:64, g * 128:(g + 1) * 128], kcT[:, g, :]), reads=[B_kcT[g]], writes=[Bt])
            P.op("dve", copy_fn("dve", tmpf[:, 256 + g * 128:256 + (g + 1) * 128], vcaug[:, g, :]), reads=[B_vc[g]], writes=[Bt])
        return dump([(tmpf[:, :], [Bt], F32)])
    P.barrier()

    AO = Arena(nc, base + 38 * K, base + 70 * K, "o")
    oT = AO.take("oT", [128, 4, S], BF16)
    B_oT = [Buf("oT%d" % h) for h in range(8)]
    TA = Arena(nc, base + 112 * K, SB_END, "ta")
    QA = TA.take("QA", [96, 4, S], BF16)
    qraw = TA.take("qraw", [64, 4, S], BF16)
    pcT = TA.take("pcT", [128, 4, S], BF16)
    kwin = TA.take("kwin", [64, S], BF16)
    Vtok = TA.take("Vtok", [128, 16, 2, 128], BF16)
    sgf = TA.take("sgf", [24, 512], F32)
    kraw = [TA.take("kraw%d" % i, [64, 512], BF16) for i in range(2)]
    rt1 = [TA.take("rt1_%d" % i, [64, 512], F32) for i in range(2)]
    rt2 = [TA.take("rt2_%d" % i, [64, 512], F32) for i in range(2)]
    pT = [TA.take("pT%d" % i, [128, 512], BF16) for i in range(3)]
    rden = TA.take("rden", [64, 512], F32)
    lnb = [TA.take("lnb%d" % i, [64, 512], F32) for i in range(2)]
    rdenA = [TA.take("rdenA%d" % i, [64, 512], F32) for i in range(2)]
    B_lnb, B_rdenA = [Buf(), Buf()], [Buf(), Buf()]
    dent = TA.take("dent", [64, 512], F32)
    nt1 = TA.take("nt1", [64, 512], F32)
    nt2 = TA.take("nt2", [64, 512], F32)
    nacc = [TA.take("nacc%d" % i, [64, 512], F32) for i in range(2)]
    rec4 = TA.take("rec4", [128, 4], F32)
    iacc = TA.take("iacc", [128, 32], F32)
    impm = TA.take("impm", [128, 32], F32)
    itmp = TA.take("itmp", [128, 32], F32)
    mx8 = TA.take("mx8", [128, 8], F32)
    m1 = TA.take("m1", [128, 32], BF16)

    B_QAq = [[Buf() for _ in range(4)] for _ in range(4)]
    B_QAm = [[Buf() for _ in range(4)] for _ in range(4)]
    B_qraw = [[Buf() for _ in range(4)] for _ in range(4)]
    B_pcT = [[Buf() for _ in range(4)] for _ in range(4)]
    B_KAk = [Buf() for _ in range(4)]
    B_kwin = [Buf() for _ in range(4)]
    B_Vtok = [Buf() for _ in range(4)]
    B_sg = [Buf() for _ in range(4)]
    B_sgf = Buf()
    B_kraw = [Buf(), Buf()]
    B_rt1 = [Buf(), Buf()]
    B_rt2 = [Buf(), Buf()]
    B_pT = [Buf(), Buf(), Buf()]
    B_rden, B_dent, B_nt1, B_nt2 = Buf(), Buf(), Buf(), Buf()
    B_nacc = [Buf(), Buf()]
    B_rec4, B_iacc, B_impm, B_itmp, B_mx8, B_m1 = Buf(), Buf(), Buf(), Buf(), Buf(), Buf()

    P.op("pool", lambda e: e.memset(QA[64:96, :, 0:1024], 0.0), writes=[B_QAm[hh][tb] for hh in range(4) for tb in range(2)])
    P.op("pool", lambda e: e.memset(Vtok[:, :, :, 64:128], 1.0), writes=B_Vtok)

    P.op("pool", lambda e: e.memset(sgT[:, :], 0.0), writes=B_sg)
    for tb in range(4):
        tbs = slice(tb * 512, (tb + 1) * 512)
        b = rot4()
        proj_fm(wg, 0, 24, tb, b, B_wg)
        P.op("act", act_fn(sgf[:, :], pb[b][0:24, :], AF.Sigmoid), reads=[PB[b]], writes=[B_sgf])
        P.op("dve", copy_fn("dve", sgT[0:24, tbs], sgf[:, :]), reads=[B_sgf], writes=[B_sg[tb]])
        P.op("dve", tt_fn(sgT[32:56, tbs], sgf[:, :], sgT[0:24, tbs], ALU.subtract), reads=[B_sgf, B_sg[tb]], writes=[B_sg[tb]])

    if debug_stage == "gates":
        tmpf = Arena(nc, base + 38 * K, base + 70 * K, "dbg").take("dbgf", [128, 4096], F32)
        Bt = Buf()
        P.op("dve", lambda e: e.memset(tmpf[:, :], 0.0), writes=[Bt])
        P.op("dve", copy_fn("dve", tmpf[0:24, 0:2048], sgT[0:24, :]), reads=B_sg, writes=[Bt])
        P.op("dve", copy_fn("dve", tmpf[0:24, 2048:4096], sgT[32:56, :]), reads=B_sg, writes=[Bt])
        return dump([(tmpf[:, :], [Bt], F32)])
    rot3 = Rot([0, 1] if KEEP_WARM else [0, 1, 2])
    rotG = Rot([6, 7])
    state = {"rope": 0, "pt": 0, "acc": 0, "rda": 0}

    def rope_a(job):
        wcol0, tb, raw_ap, B_raw, out_ap, B_out = job
        bq = rot_all()
        proj_fm(wA, wcol0, 64, tb, bq, B_wA)
        P.op("act", copy_fn("act", raw_ap, pb[bq][0:64, :]), reads=[PB[bq]], writes=[B_raw])
        return bq

    def rope_b(job, bq):
        wcol0, tb, raw_ap, B_raw, out_ap, B_out = job
        tbs = slice(tb * 512, (tb + 1) * 512)
        bs = rot_all()
        P.op("pe", pe_fn([(pb[bs][0:64, :], permM[:, :], raw_ap, True, True)]), reads=[B_const, B_raw], writes=[PB[bs]])
        i = state["rope"] % 2
        state["rope"] += 1
        P.op("dve", tt_fn(rt1[i][:, :], pb[bq][0:64, :], CT[:, tbs], ALU.mult), reads=[PB[bq], B_catt], writes=[B_rt1[i]])
        P.op("dve", tt_fn(rt2[i][:, :], pb[bs][0:64, :], ST[:, tbs], ALU.mult), reads=[PB[bs], B_catt], writes=[B_rt2[i]])
        P.op("pool", tt_fn(out_ap, rt1[i][:, :], rt2[i][:, :], ALU.add), reads=[B_rt1[i], B_rt2[i]], writes=[B_out])

    def rope_jobs(jobs):
        prev = None
        for job in jobs:
            bq = rope_a(job)
            if prev is not None:
                rope_b(*prev)
            prev = (job, bq)
        rope_b(*prev)

    def normalize(bank, h, hh, tb, br, clamp):
        tbs = slice(tb * 512, (tb + 1) * 512)
        if clamp:
            P.op("dve", ts_fn(dent[:, :], pb[bank][64:128, :], 1e-30, None, ALU.max), reads=[PB[bank]], writes=[B_dent])
            P.op("dve", lambda e: e.reciprocal(rden[:, :], dent[:, :]), reads=[B_dent], writes=[B_rden])
            rd, B_rd = rden, B_rden
        elif br == 0 or (br == 1 and tb >= 2):
            P.op("dve", lambda e: e.reciprocal(rden[:, :], pb[bank][64:128, :]), reads=[PB[bank]], writes=[B_rden])
            rd, B_rd = rden, B_rden
        else:
            k = state["rda"] % 2
            state["rda"] += 1
            P.op("act", act_fn(lnb[k][:, :], pb[bank][64:128, :], AF.Ln), reads=[PB[bank]], writes=[B_lnb[k]])
            P.op("act", act_fn(rdenA[k][:, :], lnb[k][:, :], AF.Exp, scale=-1.0), reads=[B_lnb[k]], writes=[B_rdenA[k]])
            rd, B_rd = rdenA[k], B_rdenA[k]
        P.op("dve", tt_fn(nt1[:, :], pb[bank][0:64, :], rd[:, :], ALU.mult), reads=[PB[bank], B_rd], writes=[B_nt1])
        bg = rotG()
        j = h * 3 + br
        P.op("pe", pe_fn([(pb[bg][0:64, :], selall[:, j, :], sgT[:, tbs], True, True)]),
             reads=[B_catt, B_sg[tb]], writes=[PB[bg]])
        a = state["acc"] % 2
        if br == 0:
            P.op("dve", tt_fn(nacc[a][:, :], pb[bg][0:64, :], nt1[:, :], ALU.mult), reads=[PB[bg], B_nt1], writes=[B_nacc[a]])
        else:
            P.op("dve", tt_fn(nt2[:, :], pb[bg][0:64, :], nt1[:, :], ALU.mult), reads=[PB[bg], B_nt1], writes=[B_nt2])
            if br == 1:
                P.op("pool", tt_fn(nacc[a][:, :], nacc[a][:, :], nt2[:, :], ALU.add), reads=[B_nacc[a], B_nt2], writes=[B_nacc[a]])
            else:
                po = 64 * (h % 2)
                P.op("dve", tt_fn(oT[po:po + 64, h // 2, tbs], nacc[a][:, :], nt2[:, :], ALU.add),
                     reads=[B_nacc[a], B_nt2], writes=[B_oT[h]])
                state["acc"] += 1

    def attn_branch(items, acc_bank, q_of, k_of, v_of, krows, reads_q, reads_k, reads_v):
        n = len(items)
        sb_of = [None] * n
        pt_of = [None] * n

        def emit_s(ii):
            kt, lo, hi, mtile, mlo = items[ii]
            b = rot3()
            sb_of[ii] = b
            mms = [(pb[b][:, lo:hi], k_of(kt), q_of(lo, hi), True, mtile is None)]
            if mtile is not None:
                mms.append((pb[b][:, mlo:mlo + 128], ident[:, :], mtile[:, :], False, True))
            P.op("pe", pe_fn(mms), reads=reads_q + [reads_k(kt), B_const], writes=[PB[b]])

        emit_s(0)
        if n > 1:
            emit_s(1)
        for ii in range(n):
            kt, lo, hi, mtile, mlo = items[ii]
            b = sb_of[ii]
            pi = state["pt"] % 3
            state["pt"] += 1
            P.op("act", act_fn(pT[pi][:, lo:hi], pb[b][:, lo:hi], AF.Exp, scale=0.125), reads=[PB[b]], writes=[B_pT[pi]])
            if ii + 2 < n:
                emit_s(ii + 2)
            mm_list = [(pb[acc_bank][:, lo:hi], v_of(kt), pT[pi][:, lo:hi], ii == 0, ii == n - 1)]
            if KEEP_WARM:
                mm_list.append((pb[2][:, 0:KEEP_WARM], ident[:, :], cmask[:, 0:KEEP_WARM], True, True))
            P.op("pe", pe_fn(mm_list), reads=[B_pT[pi], reads_v(kt), B_const, B_catt],
                 writes=[PB[acc_bank]] + ([PB[2]] if KEEP_WARM else []))

    for g in range(2):
        if g == 1:
            P.dma("pool", dma_fn(wA[:], w_att_d[:, :, g, 0:512]), writes=[B_wA])
        jobs = []
        for hh in range(4):
            for tb in range(4):
                tbs = slice(tb * 512, (tb + 1) * 512)
                jobs.append((hh * 64, tb, qraw[:, hh, tbs], B_qraw[hh][tb], QA[0:64, hh, tbs], B_QAq[hh][tb]))
        for tb in range(4):
            tbs = slice(tb * 512, (tb + 1) * 512)
            jobs.append((256, tb, kraw[tb % 2][:, :], B_kraw[tb % 2], KA[0:64, tbs], B_KAk[tb]))
        for tb in range(4):
            tbs = slice(tb * 512, (tb + 1) * 512)
            jobs.append((320, tb, kraw[tb % 2][:, :], B_kraw[tb % 2], kwin[:, tbs], B_kwin[tb]))
        rope_jobs(jobs)
        for t4 in range(4):
            b = rot4()
            mms = []
            for i in range(4):
                tt = 4 * t4 + i
                for kc in range(8):
                    mms.append((pb[b][:, i * 128:(i + 1) * 128], xT[:, kc, tt * 128:(tt + 1) * 128], wA[:, kc, 384:512],
                                kc == 0, kc == 7))
            P.op("pe", pe_fn(mms), reads=[B_wA] + B_xT[4 * t4:4 * t4 + 4], writes=[PB[b]])
            P.op("act", copy_fn("act", Vtok[:, 4 * t4:4 * t4 + 4, :, 0:64],
                                pb[b][:, :].rearrange("p (a s c) -> p a s c", a=4, s=2)),
                 reads=[PB[b]], writes=[B_Vtok[t4]])
        if debug_stage == "proj" and g == 0:
            tmpf = Arena(nc, base + 38 * K, base + 70 * K, "dbg").take("dbgf", [128, 8192], F32)
            Bt = Buf()
            P.op("dve", lambda e: e.memset(tmpf[:, :], 0.0), writes=[Bt])
            allq = [B_QAq[hh][tb] for hh in range(4) for tb in range(4)]
            P.op("dve", copy_fn("dve", tmpf[0:64, 0:2048], QA[0:64, 1, :]), reads=allq, writes=[Bt])
            P.op("dve", copy_fn("dve", tmpf[0:64, 2048:4096], KA[0:64, :]), reads=B_KAk, writes=[Bt])
            P.op("dve", copy_fn("dve", tmpf[0:64, 4096:6144], kwin[:, :]), reads=B_kwin, writes=[Bt])
            P.op("dve", copy_fn("dve", tmpf[:, 6144:8192].rearrange("p (a c) -> p a c", a=16), Vtok[:, :, 0, :]),
                 reads=B_Vtok, writes=[Bt])
            return dump([(tmpf[:, :], [Bt], F32)])

        for hh in range(4):
            for tb in range(4):
                tbs = slice(tb * 512, (tb + 1) * 512)
                b = rot3()
                P.op("pe", pe_fn([(pb[b][0:127, :], kcT[:, g, 0:127], qraw[:, hh, tbs], True, False),
                                  (pb[b][0:127, :], ident[0:127, 0:127], cmask[0:127, tbs], False, True)]),
                     reads=[B_kcT[g], B_qraw[hh][tb], B_const, B_catt], writes=[PB[b]])
                P.op("act", act_fn(pcT[0:127, hh, tbs], pb[b][0:127, :], AF.Exp, scale=0.125),
                     reads=[PB[b]], writes=[B_pcT[hh][tb]])
        def topk_a(tt):
            tb = tt // 4
            bi = 6
            mms = []
            for hh in range(4):
                mms.append((pb[bi][:, hh * 33:(hh + 1) * 33], pcT[0:127, hh, tt * 128:(tt + 1) * 128], maug[0:127, :], True, True))
            P.op("pe", pe_fn(mms), reads=[B_pcT[hh][tb] for hh in range(4)] + [B_const], writes=[PB[bi]])
            P.op("dve", lambda e: e.reciprocal(rec4[:, :], pb[6][:, 32:132:33]), reads=[PB[bi]], writes=[B_rec4])
            P.op("dve", ts_fn(iacc[:, :], pb[bi][:, 0:32], rec4[:, 0:1], None, ALU.mult), reads=[PB[bi], B_rec4], writes=[B_iacc])
            for hh in range(1, 4):
                P.op("dve", stt_fn(iacc[:, :], pb[bi][:, hh * 33:hh * 33 + 32], rec4[:, hh:hh + 1], iacc[:, :], ALU.mult, ALU.add),
                     reads=[PB[bi], B_rec4, B_iacc], writes=[B_iacc])
            P.op("dve", tt_fn(impm[:, :], iacc[:, :], keep[:, tt - 8, :], ALU.mult), reads=[B_iacc, B_catt], writes=[B_impm])
            P.op("dve", tt_fn(impm[:, :], impm[:, :], forceb[:, tt - 8, :], ALU.add), reads=[B_impm, B_catt], writes=[B_impm])
            P.op("dve", lambda e: e.max(out=mx8[:, :], in_=impm[:, :]), reads=[B_impm], writes=[B_mx8])
            P.op("dve", lambda e: e.match_replace(out=itmp[:, :], in_to_replace=mx8[:, :], in_values=impm[:, :], imm_value=-1.0),
                 reads=[B_impm, B_mx8], writes=[B_itmp])
            P.op("dve", lambda e: e.max(out=mx8[:, :], in_=itmp[:, :]), reads=[B_itmp], writes=[B_mx8])
            P.op("dve", ts_fn(m1[:, :], impm[:, :], mx8[:, 7:8], 1.0, ALU.is_ge, ALU.subtract), reads=[B_impm, B_mx8], writes=[B_m1])

        def topk_b(tt):
            tb = tt // 4
            bm = 5
            tts = slice(tt * 128, (tt + 1) * 128)
            P.op("pe", pe_fn([(pb[bm][64:96, 0:128], m1[:, :], ident[:, :], True, True)]),
                 reads=[B_m1, B_const], writes=[PB[bm]])
            for hh in range(4):
                P.op("dve", copy_fn("dve", QA[64:96, hh, tts], pb[bm][64:96, 0:128]), reads=[PB[bm]], writes=[B_QAm[hh][tb]])

        if debug_stage == "topk":
            for tt in range(8, 16):
                topk_a(tt)
                topk_b(tt)
        if debug_stage == "topk" and g == 0:
            tmpf = Arena(nc, base + 38 * K, base + 70 * K, "dbg").take("dbgf", [128, 4096], F32)
            Bt = Buf()
            P.op("dve", lambda e: e.memset(tmpf[:, :], 0.0), writes=[Bt])
            P.op("dve", copy_fn("dve", tmpf[64:96, 0:2048], QA[64:96, 2, :]),
                 reads=[B_QAm[2][tb] for tb in range(4)], writes=[Bt])
            P.op("dve", copy_fn("dve", tmpf[0:127, 2048:4096], pcT[0:127, 1, :]),
                 reads=[B_pcT[1][tb] for tb in range(4)], writes=[Bt])
            return dump([(tmpf[:, :], [Bt], F32)])

        def unit(hh, tb):
            h = 4 * g + hh
            if True:
                tbs = slice(tb * 512, (tb + 1) * 512)
                q0 = tb * 512
                P.op("pe", pe_fn([(pb[5][:, :], vcaug[0:127, g, :], pcT[0:127, hh, tbs], True, True)]),
                     reads=[B_vc[g], B_pcT[hh][tb]], writes=[PB[5]])
                normalize(5, h, hh, tb, 0, clamp=(tb == 0))
                items = []
                for kt in range(0, 4 * tb + 4):
                    i = kt - 4 * tb
                    if i < 0:
                        items.append((kt, 0, 512, None, 0))
                    else:
                        items.append((kt, i * 128, 512, triM, i * 128))
                attn_branch(items, 3,
                            q_of=lambda lo, hi, hh=hh, q0=q0: QA[0:96, hh, q0 + lo:q0 + hi],
                            k_of=lambda kt: KA[0:96, kt * 128:(kt + 1) * 128],
                            v_of=lambda kt: Vtok[:, kt, 0, :],
                            krows=96,
                            reads_q=[B_QAq[hh][tb], B_QAm[hh][tb], B_catt],
                            reads_k=lambda kt: B_KAk[kt // 4],
                            reads_v=lambda kt: B_Vtok[kt // 4])
                normalize(3, h, hh, tb, 1, clamp=False)
                items = []
                for i in range(4):
                    items.append((4 * tb + i, i * 128, 512, triM, i * 128))
                if tb > 0:
                    for i in range(4):
                        items.append((4 * tb - 4 + i, 0, (i + 1) * 128, antiM, i * 128))
                attn_branch(items, 4,
                            q_of=lambda lo, hi, hh=hh, q0=q0: QA[0:64, hh, q0 + lo:q0 + hi],
                            k_of=lambda kt: kwin[:, kt * 128:(kt + 1) * 128],
                            v_of=lambda kt: Vtok[:, kt, 1, :],
                            krows=64,
                            reads_q=[B_QAq[hh][tb]],
                            reads_k=lambda kt: B_kwin[kt // 4],
                            reads_v=lambda kt: B_Vtok[kt // 4])
                normalize(4, h, hh, tb, 2, clamp=False)

        early = [(hh, tb) for tb in range(2) for hh in range(4)]
        for idx, tt in enumerate(range(8, 16)):
            topk_a(tt)
            unit(*early[idx])
            topk_b(tt)
        for tb in range(2, 4):
            for hh in range(4):
                unit(hh, tb)
    if debug_stage == "att":
        tmpf = Arena(nc, base + 70 * K, SB_END, "dbg").take("dbgf", [128, 16384], F32)
        Bt = Buf()
        P.barrier()
        P.op("dve", lambda e: e.memset(tmpf[:, :], 0.0), writes=[Bt])
        for h in range(8):
            P.op("dve", copy_fn("dve", tmpf[0:64, h * 2048:(h + 1) * 2048], oT[64 * (h % 2):64 * (h % 2) + 64, h // 2, :]), reads=[B_oT[h]], writes=[Bt])
        return dump([(tmpf[:, :], [Bt], F32)])
    P.barrier()

    AY = Arena(nc, base + 70 * K, base + 86 * K, "y")
    ycT = AY.take("ycT", [128, 4, S], BF16)
    B_yc = [Buf() for _ in range(4)]
    TM = Arena(nc, base + 118 * K, SB_END, "tm")
    whbc = [TM.take("whbc%d" % i, [128, 8, 384], BF16) for i in range(2)]
    B_whbc = [Buf(), Buf()]
    h_sb = TM.take("h_sb", [128, S], F32)
    u_sb = TM.take("u_sb", [128, S + 2], F32)
    y_sb = TM.take("y_sb", [128, S], F32)
    B_h, B_u, B_y = Buf(), Buf(), Buf()
    P.op("pool", lambda e: e.memset(u_sb[:, 0:2], 0.0), writes=[B_u])
    rot6 = Rot([0, 1, 2, 3, 4, 5])
    P.dma("pool", dma_fn(whbc[0][:], w_hbc_d[:, :, 0, :]), writes=[B_whbc[0]])
    TB = Arena(nc, base + 156 * K, SB_END, "tb")
    wco = TB.take("wco", [128, 4, D], BF16)
    wno = TB.take("wno", [128, 4, D], BF16)
    wg2 = [TB.take("wg2_%d" % i, [128, 8, 256], BF16) for i in range(2)]
    B_wco, B_wno = Buf(), Buf()
    B_wg2 = [Buf(), Buf()]
    P.dma("pool", dma_fn(whbc[1][:], w_hbc_d[:, :, 1, :]), writes=[B_whbc[1]])
    P.dma("pool", dma_fn(wco[:], wco_d), writes=[B_wco])
    P.dma("pool", dma_fn(wno[:], wno_d), writes=[B_wno])
    P.dma("pool", dma_fn(wg2[0][:], w_g2_d[:, :, 0, :]), writes=[B_wg2[0]])
    for j in range(4):
        wt = whbc[j % 2]
        Bw = B_whbc[j % 2]
        if 1 <= j and j + 1 < 4:
            P.dma("pool", dma_fn(whbc[(j + 1) % 2][:], w_hbc_d[:, :, j + 1, :]), writes=[B_whbc[(j + 1) % 2]])
        for tb in range(4):
            tbs = slice(tb * 512, (tb + 1) * 512)
            b = rot6()
            proj_fm(wt, 0, 128, tb, b, Bw)
            P.op("act", copy_fn("act", h_sb[:, tbs], pb[b][:, :]), reads=[PB[b]], writes=[B_h])
        for tb in range(4):
            tbs = slice(tb * 512, (tb + 1) * 512)
            b = rot6()
            proj_fm(wt, 256, 128, tb, b, Bw)
            P.op("dve", tt_fn(u_sb[:, 2 + tb * 512:2 + (tb + 1) * 512], pb[b][:, :], h_sb[:, tbs], ALU.mult),
                 reads=[PB[b], B_h], writes=[B_u])
        P.op("act", act_fn(y_sb[:, :], u_sb[:, 2:S + 2], AF.Identity, scale=cw[:, j, 2:3]), reads=[B_u, B_const], writes=[B_y])
        P.op("dve", stt_fn(y_sb[:, :], u_sb[:, 1:S + 1], cw[:, j, 1:2], y_sb[:, :], ALU.mult, ALU.add),
             reads=[B_u, B_const, B_y], writes=[B_y])
        P.op("dve", stt_fn(y_sb[:, :], u_sb[:, 0:S], cw[:, j, 0:1], y_sb[:, :], ALU.mult, ALU.add),
             reads=[B_u, B_const, B_y], writes=[B_y])
        for tb in range(4):
            tbs = slice(tb * 512, (tb + 1) * 512)
            b = rot6()
            proj_fm(wt, 128, 128, tb, b, Bw)
            P.op("dve", tt_fn(ycT[:, j, tbs], pb[b][:, :], y_sb[:, tbs], ALU.mult), reads=[PB[b], B_y], writes=[B_yc[j]])
    if debug_stage == "conv":
        tmpf = Arena(nc, base + 38 * K, base + 70 * K, "dbg").take("dbgf", [128, 8192], F32)
        Bt = Buf()
        P.barrier()
        for j in range(4):
            P.op("dve", copy_fn("dve", tmpf[:, j * 2048:(j + 1) * 2048], ycT[:, j, :]), reads=[B_yc[j]], writes=[Bt])
        return dump([(tmpf[:, :], [Bt], F32)])
    P.barrier()

    AM = Arena(nc, base + 86 * K, base + 118 * K, "m")
    mixT = AM.take("mixT", [128, 8, S], BF16)
    B_mix = [Buf() for _ in range(4)]
    sgc = [TB.take("sgc%d" % i, [128, 512], F32) for i in range(2)]
    sgn = [TB.take("sgn%d" % i, [128, 512], F32) for i in range(2)]
    ma = [TB.take("ma%d" % i, [128, 512], F32) for i in range(2)]
    mb = [TB.take("mb%d" % i, [128, 512], F32) for i in range(2)]
    B_sgc, B_sgn, B_ma, B_mb = [Buf(), Buf()], [Buf(), Buf()], [Buf(), Buf()], [Buf(), Buf()]
    rot8 = Rot([0, 1, 2, 3, 4, 5, 6, 7])
    it = 0
    for fc in range(8):
        wt = wg2[fc % 2]
        Bw = B_wg2[fc % 2]
        if fc + 1 < 8:
            P.dma("pool", dma_fn(wg2[(fc + 1) % 2][:], w_g2_d[:, :, fc + 1, :]), writes=[B_wg2[(fc + 1) % 2]])
        fcs = slice(fc * 128, (fc + 1) * 128)
        for tb in range(4):
            tbs = slice(tb * 512, (tb + 1) * 512)
            i = it % 2
            it += 1
            b1, b2, b3, b4 = rot8(), rot8(), rot8(), rot8()
            proj_fm(wt, 0, 128, tb, b1, Bw)
            P.op("act", act_fn(sgc[i][:, :], pb[b1][:, :], AF.Sigmoid), reads=[PB[b1]], writes=[B_sgc[i]])
            proj_fm(wt, 128, 128, tb, b2, Bw)
            P.op("act", act_fn(sgn[i][:, :], pb[b2][:, :], AF.Sigmoid), reads=[PB[b2]], writes=[B_sgn[i]])
            mms = [(pb[b3][:, :], wco[:, kc, fcs], ycT[:, kc, tbs], kc == 0, kc == 3) for kc in range(4)]
            P.op("pe", pe_fn(mms), reads=[B_wco] + B_yc, writes=[PB[b3]])
            mms = [(pb[b4][:, :], wno[:, hp, fcs], oT[:, hp, tbs], hp == 0, hp == 3) for hp in range(4)]
            P.op("pe", pe_fn(mms), reads=[B_wno] + B_oT, writes=[PB[b4]])
            P.op("dve", tt_fn(ma[i][:, :], pb[b3][:, :], sgc[i][:, :], ALU.mult), reads=[PB[b3], B_sgc[i]], writes=[B_ma[i]])
            P.op("dve", tt_fn(mb[i][:, :], pb[b4][:, :], sgn[i][:, :], ALU.mult), reads=[PB[b4], B_sgn[i]], writes=[B_mb[i]])
            P.op("pool", tt_fn(mixT[:, fc, tbs], ma[i][:, :], mb[i][:, :], ALU.add), reads=[B_ma[i], B_mb[i]], writes=[B_mix[tb]])
    if debug_stage == "mix":
        tmpf = Arena(nc, base + 38 * K, base + 70 * K, "dbg").take("dbgf", [128, 8192], F32)
        Bt = Buf()
        P.barrier()
        for j in range(4):
            P.op("dve", copy_fn("dve", tmpf[:, j * 2048:(j + 1) * 2048], mixT[:, j, :]), reads=B_mix, writes=[Bt])
        return dump([(tmpf[:, :], [Bt], F32)])
    P.barrier()

    AR = Arena(nc, base + 118 * K, base + 182 * K, "r")
    resid = AR.take("resid", [128, 16, D], F32)
    B_res = [Buf() for _ in range(16)]
    TCm = Arena(nc, base + 38 * K, base + 86 * K, "tcm")
    wo = TCm.take("wo", [128, 8, D], BF16)
    B_wo = Buf()
    P.dma("pool", dma_fn(wo[:], wo_d), writes=[B_wo])
    TL = Arena(nc, base + 182 * K, SB_END, "tl")
    lnp = TL.take("lnp", [128, 2, D], F32)
    B_lnp = Buf()
    P.dma("sp", dma_fn(lnp[:], lnp_d[:, 0:2, :]), writes=[B_lnp])
    xin = [TCm.take("xin%d" % i, [128, D], F32) for i in range(4)]
    B_xin = [Buf() for _ in range(4)]
    rbuf = [TCm.take("rbuf%d" % i, [128, D], F32) for i in range(2)]
    B_rbuf = [Buf(), Buf()]
    xn = [TL.take("xn%d" % i, [128, D], F32) for i in range(2)]
    B_xn = [Buf(), Buf()]
    obuf = [TL.take("obuf%d" % i, [128, D], F32) for i in range(2)]
    B_obuf = [Buf(), Buf()]
    rl = [TL.take("rl%d" % i, [128, 512], F32) for i in range(2)]
    B_rl = [Buf(), Buf()]
    x1b = [TCm.take("x1b%d" % i, [128, D], BF16) for i in range(2)]
    B_x1b = [Buf(), Buf()]
    stats4 = [TL.take("stats4_%d" % i, [128, 4, 12], F32) for i in range(2)]
    mv4 = [TL.take("mv4_%d" % i, [128, 4, 2], F32) for i in range(2)]
    rstd4 = [TL.take("rstd4_%d" % i, [128, 4], F32) for i in range(2)]
    nmr4 = [TL.take("nmr4_%d" % i, [128, 4], F32) for i in range(2)]
    ssum4 = [TL.take("ssum4_%d" % i, [128, 4], F32) for i in range(2)]
    ssq4 = [TL.take("ssq4_%d" % i, [128, 4], F32) for i in range(2)]
    B_ssum = [Buf(), Buf()]
    B_stats, B_mv, B_rstd, B_nmr = [Buf(), Buf()], [Buf(), Buf()], [Buf(), Buf()], [Buf(), Buf()]
    state["ln"] = 0

    def ln_stats_a(tiles, on_act=False):
        n = len(tiles)
        s_ = state["ln"] % 2
        state["ln"] += 1
        if on_act:
            for j, (src_ap, B_src, dst_ap, B_dst, xi, variant) in enumerate(tiles):
                P.op("act", lambda e, j=j, src_ap=src_ap, xi=xi: e.activation(xn[xi][:, :], src_ap, AF.Identity, accum_out=ssum4[s_][:, j:j + 1]),
                     reads=[B_src], writes=[B_xn[xi], B_ssum[s_]])
                P.op("act", lambda e, j=j, src_ap=src_ap, xi=xi: e.activation(xn[xi][:, :], src_ap, AF.Square, accum_out=ssq4[s_][:, j:j + 1]),
                     reads=[B_src], writes=[B_xn[xi], B_ssum[s_]])
            inv = 1.0 / D
            P.op("dve", ts_fn(mv4[s_][:, 0:n, 0], ssum4[s_][:, 0:n], inv, None, ALU.mult), reads=[B_ssum[s_]], writes=[B_mv[s_]])
            P.op("dve", tt_fn(ssum4[s_][:, 0:n], mv4[s_][:, 0:n, 0], mv4[s_][:, 0:n, 0], ALU.mult), reads=[B_mv[s_]], writes=[B_ssum[s_]])
            P.op("dve", ts_fn(rstd4[s_][:, 0:n], ssq4[s_][:, 0:n], inv, LN_EPS, ALU.mult, ALU.add), reads=[B_ssum[s_]], writes=[B_rstd[s_]])
            P.op("dve", tt_fn(rstd4[s_][:, 0:n], rstd4[s_][:, 0:n], ssum4[s_][:, 0:n], ALU.subtract),
                 reads=[B_rstd[s_], B_ssum[s_]], writes=[B_rstd[s_]])
            return s_
        for j, (src_ap, B_src, dst_ap, B_dst, xi, variant) in enumerate(tiles):
            P.op("dve", lambda e, j=j, src_ap=src_ap: e.bn_stats(stats4[s_][:, j, 0:6], src_ap[:, 0:512]),
                 reads=[B_src], writes=[B_stats[s_]])
            P.op("dve", lambda e, j=j, src_ap=src_ap: e.bn_stats(stats4[s_][:, j, 6:12], src_ap[:, 512:1024]),
                 reads=[B_src], writes=[B_stats[s_]])
            P.op("dve", lambda e, j=j: e.bn_aggr(mv4[s_][:, j, :], stats4[s_][:, j, :]), reads=[B_stats[s_]], writes=[B_mv[s_]])
        P.op("dve", ts_fn(rstd4[s_][:, 0:n], mv4[s_][:, 0:n, 1], LN_EPS, None, ALU.add), reads=[B_mv[s_]], writes=[B_rstd[s_]])
        return s_

    def ln_stats_b(tiles, s_):
        n = len(tiles)
        P.op("act", act_fn(rstd4[s_][:, 0:n], rstd4[s_][:, 0:n], AF.Sqrt), reads=[B_rstd[s_]], writes=[B_rstd[s_]])
        P.op("dve", lambda e: e.reciprocal(rstd4[s_][:, 0:n], rstd4[s_][:, 0:n]), reads=[B_rstd[s_]], writes=[B_rstd[s_]])
        if any(t[5] == "actpool" for t in tiles):
            P.op("dve", stt_fn(nmr4[s_][:, 0:n], mv4[s_][:, 0:n, 0], -1.0, rstd4[s_][:, 0:n], ALU.mult, ALU.mult),
                 reads=[B_mv[s_], B_rstd[s_]], writes=[B_nmr[s_]])
        return s_

    def ln_apply(tiles, s_, after=None):
        for j, (src_ap, B_src, dst_ap, B_dst, xi, variant) in enumerate(tiles):
            if variant == "dve":
                P.op("dve", stt_fn(xn[xi][:, :], src_ap, mv4[s_][:, j, 0:1], lnp[:, 0, :], ALU.subtract, ALU.mult),
                     reads=[B_src, B_mv[s_], B_lnp], writes=[B_xn[xi]])
                P.op("dve", stt_fn(dst_ap, xn[xi][:, :], rstd4[s_][:, j:j + 1], lnp[:, 1, :], ALU.mult, ALU.add),
                     reads=[B_xn[xi], B_rstd[s_], B_lnp], writes=[B_dst])
            else:
                P.op("act", act_fn(xn[xi][:, :], src_ap, AF.Identity, scale=rstd4[s_][:, j:j + 1], bias=nmr4[s_][:, j:j + 1]),
                     reads=[B_src, B_rstd[s_], B_nmr[s_]], writes=[B_xn[xi]])
                P.op("pool", tt_fn(xn[xi][:, :], xn[xi][:, :], lnp[:, 0, :], ALU.mult), reads=[B_xn[xi], B_lnp], writes=[B_xn[xi]])
                P.op("pool", tt_fn(dst_ap, xn[xi][:, :], lnp[:, 1, :], ALU.add), reads=[B_xn[xi], B_lnp], writes=[B_dst])
            if after is not None:
                after(j)

    def ln_stats(tiles):
        return ln_stats_b(tiles, ln_stats_a(tiles))

    def ln_block(tiles, after=None):
        ln_apply(tiles, ln_stats(tiles), after)

    x_t = x_d.rearrange("(n p) d -> n p d", p=128)

    wo_banks = {}
    rotW = Rot([0, 1, 2, 3, 4, 5])
    rotT = Rot([6, 7])

    def mixc_pe(tt):
        i = tt % 2
        tts = slice(tt * 128, (tt + 1) * 128)
        P.dma("sp", dma_fn(xin[tt % 4][:, :], x_t[tt]), writes=[B_xin[tt % 4]])
        wo_banks[tt] = []
        for half in range(2):
            hs = slice(half * 512, (half + 1) * 512)
            b = rotW()
            wo_banks[tt].append(b)
            mms = [(pb[b][:, :], mixT[:, kc, tts], wo[:, kc, hs], kc == 0, kc == 7) for kc in range(8)]
            P.op("pe", pe_fn(mms), reads=[B_wo] + B_mix, writes=[PB[b]])

    def mixc_evac(tt):
        i = tt % 2
        for half in range(2):
            hs = slice(half * 512, (half + 1) * 512)
            b = wo_banks[tt][half]
            P.op("dve", stt_fn(rbuf[i][:, hs], xin[tt % 4][:, hs], ALPHA, pb[b][:, :], ALU.mult, ALU.add),
                 reads=[B_xin[tt % 4], PB[b]], writes=[B_rbuf[i]])

    def mixc_tiles(tt):
        i = tt % 2
        return [(rbuf[i][:, :], B_rbuf[i], rbuf[i][:, :], B_rbuf[i], i, "dve")]

    def mixc_tail(tt):
        i = tt % 2
        tts = slice(tt * 128, (tt + 1) * 128)
        P.op("act", lambda e: e.mul(resid[:, tt, :], rbuf[i][:, :], ALPHA), reads=[B_rbuf[i]], writes=[B_res[tt]])
        P.op("act", copy_fn("act", x1b[i][:, :], rbuf[i][:, :]), reads=[B_rbuf[i]], writes=[B_x1b[i]])
        for half in range(2):
            b = rotT()
            mms = []
            for j in range(4):
                kc = half * 4 + j
                mms.append((pb[b][:, j * 128:(j + 1) * 128], x1b[i][:, kc * 128:(kc + 1) * 128], ident[:], True, True))
            P.op("pe", pe_fn(mms), reads=[B_x1b[i], B_const], writes=[PB[b]])
            P.op("act", copy_fn("act", xT[:, half * 4:half * 4 + 4, tts], pb[b][:, :].rearrange("p (a b) -> p a b", a=4)),
                 reads=[PB[b]], writes=[B_xT[tt]])

    TF = Arena(nc, base + 38 * K, base + 118 * K, "tf")
    wu = [TF.take("wu%d" % i, [128, 8, 1024], BF16) for i in range(2)]
    wd = [TF.take("wd%d" % i, [128, 8, 1024], BF16) for i in range(2)]
    hT = [TF.take("hT%d" % i, [128, 8, 512], BF16) for i in range(2)]
    B_wu, B_wd, B_hT = [Buf(), Buf()], [Buf(), Buf()], [Buf(), Buf()]

    mixc_pe(0)
    mixc_pe(1)
    mixc_pe(2)
    mixc_evac(0)
    ln_block(mixc_tiles(0))
    for tt in range(16):
        if tt + 3 < 16:
            mixc_pe(tt + 3)
            if tt + 3 == 15:
                P.dma("pool", dma_fn(wu[0][:], wup_d[:, :, 0:1024]), writes=[B_wu[0], B_wo])
        if tt + 1 < 16:
            mixc_evac(tt + 1)
            s_next = ln_stats(mixc_tiles(tt + 1))
        mixc_tail(tt)
        if tt + 1 < 16:
            ln_apply(mixc_tiles(tt + 1), s_next)
    if debug_stage == "ln1":
        P.barrier()
        return dump([(resid[:, tt, :], [B_res[tt]], F32) for tt in range(16)])
    P.barrier()

    P.dma("sp", dma_fn(lnp[:], lnp_d[:, 2:4, :]), writes=[B_lnp])
    rotU = Rot([0, 1, 2, 3])
    rotD = Rot([4, 5, 6, 7])

    def ln2_tiles(tb_):
        tiles = []
        for t4 in range(4):
            tt = 4 * tb_ + t4
            tiles.append((resid[:, tt, :], B_res[tt], obuf[tt % 2][:, :], B_obuf[tt % 2], tt % 2, "dve"))

        def store(j):
            tt = 4 * tb_ + j
            P.dma("sp", dma_fn(y_d[tt * 128:(tt + 1) * 128, :], obuf[tt % 2][:, :]), reads=[B_obuf[tt % 2]], is_output=True)

        return tiles, store

    def load_wu(q):
        P.dma("pool", dma_fn(wu[q % 2][:], wup_d[:, :, q * 1024:(q + 1) * 1024]), writes=[B_wu[q % 2]])

    def load_wd(q):
        P.dma("pool", dma_fn(wd[q % 2][:], wdn_d[:, q * 8:(q + 1) * 8, :]), writes=[B_wd[q % 2]])

    def up_group(blk, f):
        q, tb = divmod(blk, 4)
        i = blk % 2
        b = rotU()
        mms = [(pb[b][:, :], wu[q % 2][:, kc, f * 128:(f + 1) * 128], xT[:, kc, tb * 512:(tb + 1) * 512], kc == 0, kc == 7)
               for kc in range(8)]
        P.op("pe", pe_fn(mms), reads=[B_wu[q % 2]] + B_xT[4 * tb:4 * tb + 4], writes=[PB[b]])
        r = f % 2
        P.op("act", act_fn(rl[r][:, :], pb[b][:, :], AF.Relu), reads=[PB[b]], writes=[B_rl[r]])
        P.op("pool", tt_fn(hT[i][:, f, :], rl[r][:, :], rl[r][:, :], ALU.mult), reads=[B_rl[r]], writes=[B_hT[i]])

    def down_group(blk, j):
        q, tb = divmod(blk, 4)
        i = blk % 2
        t4, half = divmod(j, 2)
        tt = 4 * tb + t4
        hs = slice(half * 512, (half + 1) * 512)
        b = rotD()
        mms = [(pb[b][:, :], hT[i][:, f, t4 * 128:(t4 + 1) * 128], wd[q % 2][:, f, hs], f == 0, f == 7) for f in range(8)]
        P.op("pe", pe_fn(mms), reads=[B_hT[i], B_wd[q % 2]], writes=[PB[b]])
        P.op("dve", tt_fn(resid[:, tt, hs], pb[b][:, :], resid[:, tt, hs], ALU.add),
             reads=[PB[b], B_res[tt]], writes=[B_res[tt]])

    load_wd(0)
    pending = None
    for s_blk in range(17):
        if s_blk < 16:
            q, tb = divmod(s_blk, 4)
            if tb == 0 and q + 1 < 4:
                load_wu(q + 1)
            if tb == 1 and q + 1 < 4:
                load_wd(q + 1)
        for j in range(8):
            if s_blk < 16:
                up_group(s_blk, j)
            if s_blk >= 1:
                down_group(s_blk - 1, j)
        if pending is not None:
            tiles_p, s_p, store_p = pending
            ln_apply(tiles_p, ln_stats_b(tiles_p, s_p), after=store_p)
            pending = None
        if s_blk >= 1 and (s_blk - 1) // 4 == 3:
            tiles_p, store_p = ln2_tiles((s_blk - 1) % 4)
            pending = (tiles_p, ln_stats_a(tiles_p, on_act=True), store_p)
    tiles_p, s_p, store_p = pending
    ln_apply(tiles_p, ln_stats_b(tiles_p, s_p), after=store_p)

    P.run_block()
    P.close()
    return nc


def _constants():
    bf = ml_dtypes.bfloat16
    c = {}
    c["c_ident"] = np.eye(128, dtype=np.float32).astype(bf)
    kl = np.arange(128)[:, None]
    ql = np.arange(128)[None, :]
    c["c_tri"] = np.where(kl <= ql, 0.0, -BIG).astype(np.float32).astype(bf)
    c["c_anti"] = np.where(kl > ql, 0.0, -BIG).astype(np.float32).astype(bf)
    n = np.arange(128)[:, None]
    t = np.arange(S)[None, :]
    c["c_cmask"] = np.where(16 * n + 31 <= t, 0.0, -BIG).astype(np.float32).astype(bf)
    j = np.arange(32)[:, None]
    c["c_e30"] = np.where((t // 64) == j, BIG, 0.0).astype(np.float32).astype(bf)
    nn = np.arange(128)[:, None] * 16
    jj = np.arange(32)[None, :] * 64
    m = ((nn < jj + 64) & (nn + 32 > jj)).astype(np.float32)
    m[127, :] = 0.0
    maug = np.concatenate([m, np.ones((128, 1), np.float32)], axis=1)
    c["c_maug"] = maug.astype(bf)
    perm = np.zeros((64, 64), np.float32)
    for mm_ in range(16):
        partner = mm_ + 8 if mm_ < 8 else mm_ - 8
        perm[partner, mm_] = 1.0
    c["c_perm"] = perm.astype(bf)
    sel = np.zeros((56, 24, 64), np.float32)
    for r in range(24):
        sel[r, r, :] = 1.0
        sel[32 + r, r, :] = 1.0
    c["c_selall"] = sel.astype(bf)
    keep = np.zeros((128, 8, 32), np.float32)
    forceb = np.zeros((128, 8, 32), np.float32)
    for i in range(8):
        tt = 8 + i
        tq = tt * 128 + np.arange(128)
        cur = tq // 64
        for p in range(128):
            cu = cur[p]
            for jb in range(32):
                if jb == 0:
                    forceb[p, i, jb] = 3e9
                elif jb == cu:
                    forceb[p, i, jb] = 2e9
                elif jb == cu - 1:
                    forceb[p, i, jb] = 1e9
                elif jb > cu:
                    forceb[p, i, jb] = -1.0
                else:
                    keep[p, i, jb] = 1.0
    c["c_keep"] = keep
    c["c_forceb"] = forceb
    inv = ROPE_THETA ** (-np.arange(0, 16, 2, dtype=np.float32) / np.float32(16.0))
    ang = np.arange(S, dtype=np.float32)[:, None] * inv[None, :].astype(np.float32)
    cos = np.cos(ang).astype(np.float32).T
    sin = np.sin(ang).astype(np.float32).T
    ct = np.ones((64, S), np.float32)
    st = np.zeros((64, S), np.float32)
    ct[0:8] = cos
    ct[8:16] = cos
    st[0:8] = -sin
    st[8:16] = sin
    c["c_ct"] = ct
    c["c_st"] = st
    return c


def _prep_weights(inp):
    f = lambda a: np.ascontiguousarray(a, dtype=np.float32)
    w_in = np.asarray(inp["w_in"])[0]
    wr = w_in.reshape(8, 128, 4888).transpose(1, 0, 2)
    out = {}
    hbc = np.stack([np.concatenate([wr[:, :, j * 128:(j + 1) * 128],
                                    wr[:, :, 512 + j * 128:512 + (j + 1) * 128],
                                    wr[:, :, 1024 + j * 128:1024 + (j + 1) * 128]], axis=2) for j in range(4)], axis=2)
    out["w_hbc"] = f(hbc)
    att = []
    for g in range(2):
        parts = [wr[:, :, 1536 + g * 256:1536 + (g + 1) * 256],
                 wr[:, :, 2304 + g * 64:2304 + (g + 1) * 64],
                 wr[:, :, 2560 + g * 64:2560 + (g + 1) * 64],
                 wr[:, :, 2432 + g * 64:2432 + (g + 1) * 64],
                 wr[:, :, 2688 + g * 64:2688 + (g + 1) * 64],
                 wr[:, :, 2048 + g * 64:2048 + (g + 1) * 64],
                 wr[:, :, 2176 + g * 64:2176 + (g + 1) * 64]]
        att.append(np.concatenate(parts, axis=2))
    out["w_att"] = f(np.stack(att, axis=2))
    out["w_gate"] = f(wr[:, :, 2816:2840])
    g2 = np.stack([np.concatenate([wr[:, :, 2840 + fc * 128:2840 + (fc + 1) * 128],
                                   wr[:, :, 3864 + fc * 128:3864 + (fc + 1) * 128]], axis=2) for fc in range(8)], axis=2)
    out["w_g2"] = f(g2)
    conv_w = np.asarray(inp["conv_w"])[0][:, 0, :]
    out["cw"] = f(conv_w.reshape(3, 4, 128).transpose(2, 1, 0))
    out["wco"] = f(np.asarray(inp["w_conv_out"])[0].reshape(4, 128, 1024).transpose(1, 0, 2))
    out["wno"] = f(np.asarray(inp["w_nsa_out"])[0].reshape(4, 128, 1024).transpose(1, 0, 2))
    out["wo"] = f(np.asarray(inp["w_o"])[0].reshape(8, 128, 1024).transpose(1, 0, 2))
    out["wup"] = f(np.asarray(inp["w_up"])[0].reshape(8, 128, 4096).transpose(1, 0, 2))
    out["wdn"] = f(np.asarray(inp["w_down"])[0].reshape(32, 128, 1024).transpose(1, 0, 2))
    out["w1k"] = f(np.asarray(inp["w_k_cmp1"])[0].transpose(1, 0, 2))
    out["w1v"] = f(np.asarray(inp["w_v_cmp1"])[0].transpose(1, 0, 2))
    out["w2k"] = f(np.asarray(inp["w_k_cmp2"])[0])
    out["w2v"] = f(np.asarray(inp["w_v_cmp2"])[0])
    out["peTk"] = f(np.asarray(inp["pe_k_cmp"])[0].T)
    out["peTv"] = f(np.asarray(inp["pe_v_cmp"])[0].T)
    lnp = np.stack([np.asarray(inp["ln1_g"])[0], np.asarray(inp["ln1_b"])[0],
                    np.asarray(inp["ln2_g"])[0], np.asarray(inp["ln2_b"])[0]], axis=0)
    out["lnp"] = f(np.broadcast_to(lnp[None], (128, 4, 1024)))
    return out


def kernel(**inputs):
    x = np.asarray(inputs["x"], dtype=np.float32)
    shared = _prep_weights(inputs)
    shared.update(_constants())
    nc = build_program(DEBUG_STAGE)
    in_maps = []
    for b in range(NCORES):
        m = dict(shared)
        m["x"] = np.ascontiguousarray(x[b])
        in_maps.append(m)
    res = run_bass_kernel_spmd(nc, in_maps, core_ids=list(range(NCORES)))
    if DEBUG_STAGE is not None:
        return np.stack([np.asarray(r["dbg"]) for r in res.results], axis=0)
    return np.stack([np.asarray(r["y"], dtype=np.float32) for r in res.results], axis=0)
```

```python
import numpy as np
import ml_dtypes
import concourse.bass as bass
import concourse.mybir as mybir
from concourse.bass_utils import run_bass_kernel_spmd

F32 = mybir.dt.float32
BF16 = mybir.dt.bfloat16
AF = mybir.ActivationFunctionType
ALU = mybir.AluOpType

S = 2048
D = 1024
NCORES = 8
ALPHA = 2.0 ** 0.25
LN_EPS = 1e-5
BIG = 32768.0
ROPE_THETA = 500000.0

ENGS = ["pe", "act", "dve", "pool", "sp"]
N_DMA_SEMS = 24
SB_BASE = 16512
SB_END = 229376 - 64

KEEP_WARM = 0
DEBUG_STAGE = None


class Buf:
    __slots__ = ("name", "w", "r", "excl")

    def __init__(self, name="", excl=False):
        self.name = name
        self.w = None
        self.r = {}
        self.excl = excl


class Prog:
    def __init__(self, nc):
        self.nc = nc
        self.ops = {e: [] for e in ENGS}
        self.cnt = {e: 0 for e in ENGS}
        self.waited = {e: {} for e in ENGS}
        self.sems = {}
        self._ctx = []
        for e in ["pe", "act", "dve", "pool"]:
            self.sems[e] = self._sem("c_" + e)
        self.dma_cnt = []
        for i in range(N_DMA_SEMS):
            self.sems[("dma", i)] = self._sem("d_%d" % i)
            self.dma_cnt.append(0)
        self.dma_rr = 0
        self.dma_rr_sw = 0
        self.out_tokens = []

    def _sem(self, name):
        cm = self.nc.semaphore(name)
        s = cm.__enter__()
        self._ctx.append(cm)
        return s

    def close(self):
        for cm in reversed(self._ctx):
            cm.__exit__(None, None, None)

    def _filter(self, eng, deps):
        out = []
        wd = self.waited[eng]
        for k, v in deps.items():
            if eng == "pe" and k == "pe":
                continue
            if wd.get(k, 0) >= v:
                continue
            wd[k] = v
            out.append((k, v))
        return out

    def _deps(self, eng, reads, writes):
        deps = {}

        def add(k, v):
            if deps.get(k, 0) < v:
                deps[k] = v

        for b in reads:
            if b.w is not None:
                add(*b.w)
            if b.excl:
                for k, v in b.r.items():
                    if k != eng:
                        add(k, v)
        for b in writes:
            if b.w is not None:
                add(*b.w)
            for k, v in b.r.items():
                add(k, v)
        return self._filter(eng, deps)

    def _mark(self, tok, reads, writes):
        for b in reads:
            if b.r.get(tok[0], 0) < tok[1]:
                b.r[tok[0]] = tok[1]
        for b in writes:
            b.w = tok
            b.r = {}

    def op(self, eng, fn, reads=(), writes=()):
        waits = self._deps(eng, reads, writes)
        self.cnt[eng] += 1
        tok = (eng, self.cnt[eng])
        self.ops[eng].append((waits, fn, (eng, 1)))
        self._mark(tok, reads, writes)
        return tok

    def dma(self, q, fn, reads=(), writes=(), is_output=False):
        half = N_DMA_SEMS // 2
        if q == "pool":
            i = half + self.dma_rr_sw
            self.dma_rr_sw = (self.dma_rr_sw + 1) % half
        else:
            i = self.dma_rr
            self.dma_rr = (self.dma_rr + 1) % half
        k = ("dma", i)
        waits = self._deps(q, reads, writes)
        prev = self.dma_cnt[i]
        if prev > 0 and self.waited[q].get(k, 0) < prev:
            self.waited[q][k] = prev
            waits.append((k, prev))
        self.dma_cnt[i] += 16
        tok = (k, self.dma_cnt[i])
        self.ops[q].append((waits, fn, (k, 16)))
        self._mark(tok, reads, writes)
        if is_output:
            self.out_tokens.append(tok)
        return tok

    def barrier(self):
        deps = {}
        for e in ["pe", "act", "dve", "pool"]:
            if self.cnt[e] > 0:
                deps[e] = self.cnt[e]
        for i in range(N_DMA_SEMS):
            if self.dma_cnt[i] > 0:
                deps[("dma", i)] = self.dma_cnt[i]
        for e in ENGS:
            d = dict(deps)
            waits = []
            wd = self.waited[e]
            for k, v in d.items():
                if wd.get(k, 0) >= v:
                    continue
                wd[k] = v
                waits.append((k, v))
            if waits:
                self.ops[e].append((waits, None, None))

    def finish(self):
        waits = []
        for k, v in self.out_tokens:
            if self.waited["sp"].get(k, 0) < v:
                self.waited["sp"][k] = v
                waits.append((k, v))
        self.ops["sp"].append((waits, None, None))

    def _plan_signals(self):
        needed = {e: set() for e in ["pe", "act", "dve", "pool"]}
        for e in ENGS:
            for waits, fn, inc in self.ops[e]:
                for k, v in waits:
                    if k in needed:
                        needed[k].add(v)
        self.sigmap = {}
        for e in needed:
            m = {}
            c = 0
            for idx in range(1, self.cnt[e] + 1):
                if idx in needed[e]:
                    c += 1
                    m[idx] = c
            self.sigmap[e] = m

    def replay(self, eng, e):
        idx = 0
        for waits, fn, inc in self.ops[eng]:
            for k, v in waits:
                if k in self.sigmap:
                    v = self.sigmap[k][v]
                e.wait_ge(self.sems[k], v)
            if fn is None:
                continue
            ins = fn(e)
            if inc[0] in self.sigmap:
                idx += 1
                if idx in self.sigmap[inc[0]]:
                    ins.then_inc(self.sems[inc[0]], 1)
            else:
                ins.then_inc(self.sems[inc[0]], inc[1])

    def run_block(self):
        self.finish()
        self._plan_signals()
        with self.nc.Block() as block:
            @block.tensor
            def _(e):
                self.replay("pe", e)

            @block.scalar
            def _(e):
                self.replay("act", e)

            @block.vector
            def _(e):
                self.replay("dve", e)

            @block.gpsimd
            def _(e):
                self.replay("pool", e)

            @block.sync
            def _(e):
                self.replay("sp", e)


class Arena:
    def __init__(self, nc, lo, hi, tag):
        self.nc, self.lo, self.hi, self.cur, self.tag = nc, lo, hi, lo, tag
        self.n = 0

    def take(self, name, shape, dt):
        per = 1
        for s_ in shape[1:]:
            per *= s_
        nbytes = per * (2 if dt == BF16 else 4)
        self.cur = (self.cur + 31) // 32 * 32
        assert self.cur + nbytes <= self.hi, (self.tag, name, self.cur, nbytes, self.hi)
        t = self.nc.alloc_sbuf_tensor_at("%s_%s" % (self.tag, name), list(shape), dt, offset=self.cur)
        self.cur += nbytes
        return t


def pe_fn(mms):
    def fn(e):
        last = None
        for (o, l, r, st, sp) in mms:
            last = e.matmul(o, lhsT=l, rhs=r, start=st, stop=sp)
        return last
    return fn


def copy_fn(eng, out, in_):
    if eng == "act":
        return lambda e: e.copy(out, in_)
    return lambda e: e.tensor_copy(out, in_)


def act_fn(out, in_, func, scale=1.0, bias=None):
    if bias is None:
        return lambda e: e.activation(out, in_, func, scale=scale)
    return lambda e: e.activation(out, in_, func, bias=bias, scale=scale)


def tt_fn(out, a, b, op):
    return lambda e: e.tensor_tensor(out, a, b, op)


def ts_fn(out, a, s1, s2, op0, op1=None):
    if op1 is None:
        return lambda e: e.tensor_scalar(out, a, s1, None, op0=op0)
    return lambda e: e.tensor_scalar(out, a, s1, s2, op0=op0, op1=op1)


def stt_fn(out, a, s, b, op0, op1):
    return lambda e: e.scalar_tensor_tensor(out, a, s, b, op0=op0, op1=op1)


def dma_fn(out, in_):
    return lambda e: e.dma_start(out=out, in_=in_)


def build_program(debug_stage=None):
    nc = bass.Bass("TRN2", target_bir_lowering=False)

    def din(name, shape, dt=F32):
        return nc.dram_tensor(name, list(shape), dt, kind="ExternalInput").ap()

    x_d = din("x", [S, D])
    w_hbc_d = din("w_hbc", [128, 8, 4, 384])
    w_att_d = din("w_att", [128, 8, 2, 640])
    w_gate_d = din("w_gate", [128, 8, 24])
    w_g2_d = din("w_g2", [128, 8, 8, 256])
    cw_d = din("cw", [128, 4, 3])
    wco_d = din("wco", [128, 4, 1024])
    wno_d = din("wno", [128, 4, 1024])
    wo_d = din("wo", [128, 8, 1024])
    wup_d = din("wup", [128, 8, 4096])
    wdn_d = din("wdn", [128, 32, 1024])
    w1_d = [din("w1k", [64, 32, 128]), din("w1v", [64, 32, 128])]
    w2_d = [din("w2k", [128, 64]), din("w2v", [128, 64])]
    peT_d = [din("peTk", [64, 32]), din("peTv", [64, 32])]
    lnp_d = din("lnp", [128, 4, 1024])
    ident_d = din("c_ident", [128, 128], BF16)
    tri_d = din("c_tri", [128, 128], BF16)
    anti_d = din("c_anti", [128, 128], BF16)
    cmask_d = din("c_cmask", [128, S], BF16)
    e30_d = din("c_e30", [32, S], BF16)
    maug_d = din("c_maug", [128, 33], BF16)
    perm_d = din("c_perm", [64, 64], BF16)
    selall_d = din("c_selall", [56, 24, 64], BF16)
    keep_d = din("c_keep", [128, 8, 32])
    forceb_d = din("c_forceb", [128, 8, 32])
    ct_d = din("c_ct", [64, S])
    st_d = din("c_st", [64, S])

    y_d = nc.dram_tensor("y", [S, D], F32, kind="ExternalOutput").ap()
    dbg_d = None
    if debug_stage is not None:
        dbg_d = nc.dram_tensor("dbg", [128, 16384], F32, kind="ExternalOutput").ap()

    P = Prog(nc)
    K = 1024
    base = SB_BASE

    pb = [nc.alloc_psum_tensor("pb%d" % i, [128, 512], F32) for i in range(8)]
    PB = [Buf("pb%d" % i, excl=True) for i in range(8)]

    class Rot:
        def __init__(self, ids):
            self.ids, self.i = ids, 0

        def __call__(self):
            b = self.ids[self.i]
            self.i = (self.i + 1) % len(self.ids)
            return b

    A0 = Arena(nc, base, base + 3 * K, "p")
    ident = A0.take("ident", [128, 128], BF16)
    triM = A0.take("tri", [128, 128], BF16)
    antiM = A0.take("anti", [128, 128], BF16)
    permM = A0.take("perm", [64, 64], BF16)
    maug = A0.take("maug", [128, 33], BF16)
    kcT = A0.take("kcT", [64, 2, 128], BF16)
    vcaug = A0.take("vcaug", [128, 2, 128], BF16)
    cw = A0.take("cw", [128, 4, 3], F32)
    B_const = Buf("const")
    B_kcT = [Buf("kcT0"), Buf("kcT1")]
    B_vc = [Buf("vc0"), Buf("vc1")]
    base = SB_BASE - 3 * K
    AX = Arena(nc, base + 6 * K, base + 38 * K, "x")
    xT = AX.take("xT", [128, 8, S], BF16)
    B_xT = [Buf("xT%d" % n) for n in range(16)]

    for (t, d) in [(ident, ident_d), (triM, tri_d), (antiM, anti_d), (permM, perm_d), (maug, maug_d), (cw, cw_d)]:
        P.dma("sp", dma_fn(t[:], d), writes=[B_const])
    P.op("dve", lambda e: e.memset(vcaug[:, :, 64:128], 1.0), writes=B_vc)

    TAc = Arena(nc, base + 70 * K, base + 112 * K, "tac")
    KA = TAc.take("KA", [96, S], BF16)
    sgT = TAc.take("sgT", [56, S], BF16)
    cmask = TAc.take("cmask", [128, S], BF16)
    CT = TAc.take("CT", [64, S], F32)
    ST = TAc.take("ST", [64, S], F32)
    selall = TAc.take("selall", [56, 24, 64], BF16)
    keep = TAc.take("keep", [128, 8, 32], F32)
    forceb = TAc.take("forceb", [128, 8, 32], F32)
    wA = TAc.take("wA", [128, 8, 512], BF16)
    wg = TAc.take("wg", [128, 8, 24], BF16)
    B_catt = Buf("catt")
    B_wA, B_wg = Buf(), Buf()
    for (t, d) in [(cmask, cmask_d), (selall, selall_d), (keep, keep_d), (forceb, forceb_d), (CT, ct_d), (ST, st_d)]:
        P.dma("sp", dma_fn(t[:], d), writes=[B_catt])
    P.dma("sp", dma_fn(KA[64:96, :], e30_d), writes=[B_catt])

    def dump(ap_list):
        col = 0
        for (ap, bufs, dt) in ap_list:
            p, n = ap.shape
            if dt == F32:
                P.dma("sp", dma_fn(dbg_d[0:p, col:col + n], ap), reads=bufs, is_output=True)
            else:
                P.dma("pool", dma_fn(dbg_d[0:p, col:col + n], ap), reads=bufs, is_output=True)
            col += n
        P.run_block()
        P.close()
        return nc

    T0 = Arena(nc, base + 38 * K, base + 100 * K, "t0")
    xb = T0.take("xb", [128, 16, D], BF16)
    B_xb = [Buf("xb%d" % c) for c in range(4)]
    x_v = x_d.rearrange("(n p) d -> p n d", p=128)
    for c in range(4):
        P.dma("pool", dma_fn(xb[:, 4 * c:4 * c + 4, :], x_v[:, 4 * c:4 * c + 4, :]), writes=[B_xb[c]])
    TC = Arena(nc, SB_END - 42 * K, SB_END, "tc")
    kcmpT = TC.take("kcmpT", [64, 4, S], BF16)
    w1 = [TC.take("w1k", [64, 32, 128], BF16), TC.take("w1v", [64, 32, 128], BF16)]
    w2 = [TC.take("w2k", [128, 64], BF16), TC.take("w2v", [128, 64], BF16)]
    peT = [TC.take("peTk", [64, 32], BF16), TC.take("peTv", [64, 32], BF16)]
    wck = [TC.take("wck0", [128, 8, 128], BF16), TC.take("wck1", [128, 8, 128], BF16)]
    B_w1 = [Buf(), Buf()]
    B_wck = [Buf(), Buf()]
    for g in range(2):
        P.dma("pool", dma_fn(wck[g][:], w_att_d[:, :, g, 512:640]), writes=[B_wck[g]])
    for kind in range(2):
        P.dma("pool", dma_fn(w1[kind][:], w1_d[kind]), writes=[B_w1[kind]])
        P.dma("pool", dma_fn(w2[kind][:], w2_d[kind]), writes=[B_w1[kind]])
        P.dma("pool", dma_fn(peT[kind][:], peT_d[kind]), writes=[B_w1[kind]])
    P.dma("pool", dma_fn(wg[:], w_gate_d), writes=[B_wg])
    P.dma("pool", dma_fn(wA[:], w_att_d[:, :, 0, 0:512]), writes=[B_wA])
    rot_all = Rot([0, 1, 2, 3, 4, 5, 6, 7])
    alt = 0
    for n in range(16):
        for half in range(2):
            b = rot_all()
            mms = []
            for j in range(4):
                kc = half * 4 + j
                mms.append((pb[b][:, j * 128:(j + 1) * 128], xb[:, n, kc * 128:(kc + 1) * 128], ident[:], True, True))
            P.op("pe", pe_fn(mms), reads=[B_xb[n // 4], B_const], writes=[PB[b]])
            eng = ["act", "dve"][alt % 2]
            alt += 1
            P.op(eng, copy_fn(eng, xT[:, half * 4:half * 4 + 4, n * 128:(n + 1) * 128],
                              pb[b][:, :].rearrange("p (a b) -> p a b", a=4)),
                 reads=[PB[b]], writes=[B_xT[n]])
    if debug_stage == "xT":
        tmpf = T0.take("dbgf", [128, 4096], F32)
        Bt = Buf()
        for kc in range(2):
            P.op("dve", copy_fn("dve", tmpf[:, kc * 2048:(kc + 1) * 2048], xT[:, kc, :]), reads=B_xT, writes=[Bt])
        return dump([(tmpf[:, :], [Bt], F32)])

    def proj_fm(w_tile, wcol0, m, tb, bank, B_w):
        mms = []
        for kc in range(8):
            mms.append((pb[bank][0:m, :], w_tile[:, kc, wcol0:wcol0 + m], xT[:, kc, tb * 512:(tb + 1) * 512],
                        kc == 0, kc == 7))
        P.op("pe", pe_fn(mms), reads=[B_w] + B_xT[4 * tb:4 * tb + 4], writes=[PB[bank]])

    bias_sb = TC.take("bias", [128, 1], F32)
    gx = TC.take("gx", [128, 128], F32)
    gx2 = TC.take("gx2", [128, 128], F32)
    gz = TC.take("gz", [128, 128], F32)
    gs = TC.take("gs", [128, 128], F32)
    hid = TC.take("hid", [128, 128], BF16)
    B_kcmp = [Buf() for _ in range(4)]
    B_bias, B_gx, B_gx2, B_gz, B_gs, B_hid = Buf(), Buf(), Buf(), Buf(), Buf(), Buf()
    rot4 = Rot([0, 1, 2, 3])
    alt = 0
    for g in range(2):
        for kind in range(2):
            idx = kind * 2 + g
            for tb in range(4):
                b = rot4()
                proj_fm(wck[g], kind * 64, 64, tb, b, B_wck[g])
                eng = ["act", "dve"][alt % 2]
                alt += 1
                P.op(eng, copy_fn(eng, kcmpT[:, idx, tb * 512:(tb + 1) * 512], pb[b][0:64, :]),
                     reads=[PB[b]], writes=[B_kcmp[idx]])
    for kind in range(2):
        for g in range(2):
            idx = kind * 2 + g
            bh, bbias, bo = 4, 5, 6
            mms = []
            for l in range(32):
                mms.append((pb[bh][:, 0:127], w1[kind][:, l, :], kcmpT[:, idx, l:l + 16 * 126 + 1:16], l == 0, l == 31))
            P.op("pe", pe_fn(mms), reads=[B_w1[kind], B_kcmp[idx]], writes=[PB[bh]])
            mms = []
            for l in range(32):
                mms.append((pb[bbias][:, 0:1], w1[kind][:, l, :], peT[kind][:, l:l + 1], l == 0, l == 31))
            P.op("pe", pe_fn(mms), reads=[B_w1[kind]], writes=[PB[bbias]])
            P.op("dve", copy_fn("dve", bias_sb[:, :], pb[bbias][:, 0:1]), reads=[PB[bbias]], writes=[B_bias])
            P.op("dve", ts_fn(gx[:, 0:127], pb[bh][:, 0:127], bias_sb[:, 0:1], None, ALU.add),
                 reads=[PB[bh], B_bias], writes=[B_gx])
            P.op("dve", tt_fn(gx2[:, 0:127], gx[:, 0:127], gx[:, 0:127], ALU.mult), reads=[B_gx], writes=[B_gx2])
            P.op("dve", ts_fn(gx2[:, 0:127], gx2[:, 0:127], 0.044715, 1.0, ALU.mult, ALU.add),
                 reads=[B_gx2], writes=[B_gx2])
            P.op("dve", tt_fn(gz[:, 0:127], gx2[:, 0:127], gx[:, 0:127], ALU.mult), reads=[B_gx2, B_gx], writes=[B_gz])
            P.op("act", act_fn(gs[:, 0:127], gz[:, 0:127], AF.Sigmoid, scale=1.5957691216057308),
                 reads=[B_gz], writes=[B_gs])
            P.op("dve", tt_fn(hid[:, 0:127], gx[:, 0:127], gs[:, 0:127], ALU.mult), reads=[B_gx, B_gs], writes=[B_hid])
            if kind == 0:
                P.op("pe", pe_fn([(pb[bo][0:64, 0:127], w2[0][:, :], hid[:, 0:127], True, True)]),
                     reads=[B_w1[0], B_hid], writes=[PB[bo]])
                P.op("dve", copy_fn("dve", kcT[:, g, 0:127], pb[bo][0:64, 0:127]), reads=[PB[bo]], writes=[B_kcT[g]])
            else:
                P.op("pe", pe_fn([(pb[bo][0:127, 0:64], hid[:, 0:127], w2[1][:, :], True, True)]),
                     reads=[B_w1[1], B_hid], writes=[PB[bo]])
                P.op("dve", copy_fn("dve", vcaug[0:127, g, 0:64], pb[bo][0:127, 0:64]), reads=[PB[bo]], writes=[B_vc[g]])
    if debug_stage == "cmp":
        tmpf = TC.take("dbgf", [128, 512], F32)
        Bt = Buf()
        P.op("dve", lambda e: e.memset(tmpf[:, :], 0.0), writes=[Bt])
        for g in range(2):
            P.op("dve", copy_fn("dve", tmpf[0:64, g * 128:(g + 1) * 128], kcT[:, g, :]), reads=[B_kcT[g]], writes=[Bt])
            P.op("dve", copy_fn("dve", tmpf[:, 256 + g * 128:256 + (g + 1) * 128], vcaug[:, g, :]), reads=[B_vc[g]], writes=[Bt])
        return dump([(tmpf[:, :], [Bt], F32)])
    P.barrier()

    AO = Arena(nc, base + 38 * K, base + 70 * K, "o")
    oT = AO.take("oT", [128, 4, S], BF16)
    B_oT = [Buf("oT%d" % h) for h in range(8)]
    TA = Arena(nc, base + 112 * K, SB_END, "ta")
    QA = TA.take("QA", [96, 4, S], BF16)
    qraw = TA.take("qraw", [64, 4, S], BF16)
    pcT = TA.take("pcT", [128, 4, S], BF16)
    kwin = TA.take("kwin", [64, S], BF16)
    Vtok = TA.take("Vtok", [128, 16, 2, 128], BF16)
    sgf = TA.take("sgf", [24, 512], F32)
    kraw = [TA.take("kraw%d" % i, [64, 512], BF16) for i in range(2)]
    rt1 = [TA.take("rt1_%d" % i, [64, 512], F32) for i in range(2)]
    rt2 = [TA.take("rt2_%d" % i, [64, 512], F32) for i in range(2)]
    pT = [TA.take("pT%d" % i, [128, 512], BF16) for i in range(3)]
    rden = TA.take("rden", [64, 512], F32)
    lnb = [TA.take("lnb%d" % i, [64, 512], F32) for i in range(2)]
    rdenA = [TA.take("rdenA%d" % i, [64, 512], F32) for i in range(2)]
    B_lnb, B_rdenA = [Buf(), Buf()], [Buf(), Buf()]
    dent = TA.take("dent", [64, 512], F32)
    nt1 = TA.take("nt1", [64, 512], F32)
    nt2 = TA.take("nt2", [64, 512], F32)
    nacc = [TA.take("nacc%d" % i, [64, 512], F32) for i in range(2)]
    rec4 = TA.take("rec4", [128, 4], F32)
    iacc = TA.take("iacc", [128, 32], F32)
    impm = TA.take("impm", [128, 32], F32)
    itmp = TA.take("itmp", [128, 32], F32)
    mx8 = TA.take("mx8", [128, 8], F32)
    m1 = TA.take("m1", [128, 32], BF16)

    B_QAq = [[Buf() for _ in range(4)] for _ in range(4)]
    B_QAm = [[Buf() for _ in range(4)] for _ in range(4)]
    B_qraw = [[Buf() for _ in range(4)] for _ in range(4)]
    B_pcT = [[Buf() for _ in range(4)] for _ in range(4)]
    B_KAk = [Buf() for _ in range(4)]
    B_kwin = [Buf() for _ in range(4)]
    B_Vtok = [Buf() for _ in range(4)]
    B_sg = [Buf() for _ in range(4)]
    B_sgf = Buf()
    B_kraw = [Buf(), Buf()]
    B_rt1 = [Buf(), Buf()]
    B_rt2 = [Buf(), Buf()]
    B_pT = [Buf(), Buf(), Buf()]
    B_rden, B_dent, B_nt1, B_nt2 = Buf(), Buf(), Buf(), Buf()
    B_nacc = [Buf(), Buf()]
    B_rec4, B_iacc, B_impm, B_itmp, B_mx8, B_m1 = Buf(), Buf(), Buf(), Buf(), Buf(), Buf()

    P.op("pool", lambda e: e.memset(QA[64:96, :, 0:1024], 0.0), writes=[B_QAm[hh][tb] for hh in range(4) for tb in range(2)])
    P.op("pool", lambda e: e.memset(Vtok[:, :, :, 64:128], 1.0), writes=B_Vtok)

    P.op("pool", lambda e: e.memset(sgT[:, :], 0.0), writes=B_sg)
    for tb in range(4):
        tbs = slice(tb * 512, (tb + 1) * 512)
        b = rot4()
        proj_fm(wg, 0, 24, tb, b, B_wg)
        P.op("act", act_fn(sgf[:, :], pb[b][0:24, :], AF.Sigmoid), reads=[PB[b]], writes=[B_sgf])
        P.op("dve", copy_fn("dve", sgT[0:24, tbs], sgf[:, :]), reads=[B_sgf], writes=[B_sg[tb]])
        P.op("dve", tt_fn(sgT[32:56, tbs], sgf[:, :], sgT[0:24, tbs], ALU.subtract), reads=[B_sgf, B_sg[tb]], writes=[B_sg[tb]])

    if debug_stage == "gates":
        tmpf = Arena(nc, base + 38 * K, base + 70 * K, "dbg").take("dbgf", [128, 4096], F32)
        Bt = Buf()
        P.op("dve", lambda e: e.memset(tmpf[:, :], 0.0), writes=[Bt])
        P.op("dve", copy_fn("dve", tmpf[0:24, 0:2048], sgT[0:24, :]), reads=B_sg, writes=[Bt])
        P.op("dve", copy_fn("dve", tmpf[0:24, 2048:4096], sgT[32:56, :]), reads=B_sg, writes=[Bt])
        return dump([(tmpf[:, :], [Bt], F32)])
    rot3 = Rot([0, 1] if KEEP_WARM else [0, 1, 2])
    rotG = Rot([6, 7])
    state = {"rope": 0, "pt": 0, "acc": 0, "rda": 0}

    def rope_a(job):
        wcol0, tb, raw_ap, B_raw, out_ap, B_out = job
        bq = rot_all()
        proj_fm(wA, wcol0, 64, tb, bq, B_wA)
        P.op("act", copy_fn("act", raw_ap, pb[bq][0:64, :]), reads=[PB[bq]], writes=[B_raw])
        return bq

    def rope_b(job, bq):
        wcol0, tb, raw_ap, B_raw, out_ap, B_out = job
        tbs = slice(tb * 512, (tb + 1) * 512)
        bs = rot_all()
        P.op("pe", pe_fn([(pb[bs][0:64, :], permM[:, :], raw_ap, True, True)]), reads=[B_const, B_raw], writes=[PB[bs]])
        i = state["rope"] % 2
        state["rope"] += 1
        P.op("dve", tt_fn(rt1[i][:, :], pb[bq][0:64, :], CT[:, tbs], ALU.mult), reads=[PB[bq], B_catt], writes=[B_rt1[i]])
        P.op("dve", tt_fn(rt2[i][:, :], pb[bs][0:64, :], ST[:, tbs], ALU.mult), reads=[PB[bs], B_catt], writes=[B_rt2[i]])
        P.op("pool", tt_fn(out_ap, rt1[i][:, :], rt2[i][:, :], ALU.add), reads=[B_rt1[i], B_rt2[i]], writes=[B_out])

    def rope_jobs(jobs):
        prev = None
        for job in jobs:
            bq = rope_a(job)
            if prev is not None:
                rope_b(*prev)
            prev = (job, bq)
        rope_b(*prev)

    def normalize(bank, h, hh, tb, br, clamp):
        tbs = slice(tb * 512, (tb + 1) * 512)
        if clamp:
            P.op("dve", ts_fn(dent[:, :], pb[bank][64:128, :], 1e-30, None, ALU.max), reads=[PB[bank]], writes=[B_dent])
            P.op("dve", lambda e: e.reciprocal(rden[:, :], dent[:, :]), reads=[B_dent], writes=[B_rden])
            rd, B_rd = rden, B_rden
        elif br == 0 or (br == 1 and tb >= 2):
            P.op("dve", lambda e: e.reciprocal(rden[:, :], pb[bank][64:128, :]), reads=[PB[bank]], writes=[B_rden])
            rd, B_rd = rden, B_rden
        else:
            k = state["rda"] % 2
            state["rda"] += 1
            P.op("act", act_fn(lnb[k][:, :], pb[bank][64:128, :], AF.Ln), reads=[PB[bank]], writes=[B_lnb[k]])
            P.op("act", act_fn(rdenA[k][:, :], lnb[k][:, :], AF.Exp, scale=-1.0), reads=[B_lnb[k]], writes=[B_rdenA[k]])
            rd, B_rd = rdenA[k], B_rdenA[k]
        P.op("dve", tt_fn(nt1[:, :], pb[bank][0:64, :], rd[:, :], ALU.mult), reads=[PB[bank], B_rd], writes=[B_nt1])
        bg = rotG()
        j = h * 3 + br
        P.op("pe", pe_fn([(pb[bg][0:64, :], selall[:, j, :], sgT[:, tbs], True, True)]),
             reads=[B_catt, B_sg[tb]], writes=[PB[bg]])
        a = state["acc"] % 2
        if br == 0:
            P.op("dve", tt_fn(nacc[a][:, :], pb[bg][0:64, :], nt1[:, :], ALU.mult), reads=[PB[bg], B_nt1], writes=[B_nacc[a]])
        else:
            P.op("dve", tt_fn(nt2[:, :], pb[bg][0:64, :], nt1[:, :], ALU.mult), reads=[PB[bg], B_nt1], writes=[B_nt2])
            if br == 1:
                P.op("pool", tt_fn(nacc[a][:, :], nacc[a][:, :], nt2[:, :], ALU.add), reads=[B_nacc[a], B_nt2], writes=[B_nacc[a]])
            else:
                po = 64 * (h % 2)
                P.op("dve", tt_fn(oT[po:po + 64, h // 2, tbs], nacc[a][:, :], nt2[:, :], ALU.add),
                     reads=[B_nacc[a], B_nt2], writes=[B_oT[h]])
                state["acc"] += 1

    def attn_branch(items, acc_bank, q_of, k_of, v_of, krows, reads_q, reads_k, reads_v):
        n = len(items)
        sb_of = [None] * n
        pt_of = [None] * n

        def emit_s(ii):
            kt, lo, hi, mtile, mlo = items[ii]
            b = rot3()
            sb_of[ii] = b
            mms = [(pb[b][:, lo:hi], k_of(kt), q_of(lo, hi), True, mtile is None)]
            if mtile is not None:
                mms.append((pb[b][:, mlo:mlo + 128], ident[:, :], mtile[:, :], False, True))
            P.op("pe", pe_fn(mms), reads=reads_q + [reads_k(kt), B_const], writes=[PB[b]])

        emit_s(0)
        if n > 1:
            emit_s(1)
        for ii in range(n):
            kt, lo, hi, mtile, mlo = items[ii]
            b = sb_of[ii]
            pi = state["pt"] % 3
            state["pt"] += 1
            P.op("act", act_fn(pT[pi][:, lo:hi], pb[b][:, lo:hi], AF.Exp, scale=0.125), reads=[PB[b]], writes=[B_pT[pi]])
            if ii + 2 < n:
                emit_s(ii + 2)
            mm_list = [(pb[acc_bank][:, lo:hi], v_of(kt), pT[pi][:, lo:hi], ii == 0, ii == n - 1)]
            if KEEP_WARM:
                mm_list.append((pb[2][:, 0:KEEP_WARM], ident[:, :], cmask[:, 0:KEEP_WARM], True, True))
            P.op("pe", pe_fn(mm_list), reads=[B_pT[pi], reads_v(kt), B_const, B_catt],
                 writes=[PB[acc_bank]] + ([PB[2]] if KEEP_WARM else []))

    for g in range(2):
        jobs = []
        for hh in range(4):
            for tb in range(4):
                tbs = slice(tb * 512, (tb + 1) * 512)
                jobs.append((hh * 64, tb, qraw[:, hh, tbs], B_qraw[hh][tb], QA[0:64, hh, tbs], B_QAq[hh][tb]))
        for tb in range(4):
            tbs = slice(tb * 512, (tb + 1) * 512)
            jobs.append((256, tb, kraw[tb % 2][:, :], B_kraw[tb % 2], KA[0:64, tbs], B_KAk[tb]))
        for tb in range(4):
            tbs = slice(tb * 512, (tb + 1) * 512)
            jobs.append((320, tb, kraw[tb % 2][:, :], B_kraw[tb % 2], kwin[:, tbs], B_kwin[tb]))
        rope_jobs(jobs)
        for t4 in range(4):
            b = rot4()
            mms = []
            for i in range(4):
                tt = 4 * t4 + i
                for kc in range(8):
                    mms.append((pb[b][:, i * 128:(i + 1) * 128], xT[:, kc, tt * 128:(tt + 1) * 128], wA[:, kc, 384:512],
                                kc == 0, kc == 7))
            P.op("pe", pe_fn(mms), reads=[B_wA] + B_xT[4 * t4:4 * t4 + 4], writes=[PB[b]])
            P.op("act", copy_fn("act", Vtok[:, 4 * t4:4 * t4 + 4, :, 0:64],
                                pb[b][:, :].rearrange("p (a s c) -> p a s c", a=4, s=2)),
                 reads=[PB[b]], writes=[B_Vtok[t4]])
        if g == 0:
            P.dma("pool", dma_fn(wA[:], w_att_d[:, :, 1, 0:512]), writes=[B_wA])
        if debug_stage == "proj" and g == 0:
            tmpf = Arena(nc, base + 38 * K, base + 70 * K, "dbg").take("dbgf", [128, 8192], F32)
            Bt = Buf()
            P.op("dve", lambda e: e.memset(tmpf[:, :], 0.0), writes=[Bt])
            allq = [B_QAq[hh][tb] for hh in range(4) for tb in range(4)]
            P.op("dve", copy_fn("dve", tmpf[0:64, 0:2048], QA[0:64, 1, :]), reads=allq, writes=[Bt])
            P.op("dve", copy_fn("dve", tmpf[0:64, 2048:4096], KA[0:64, :]), reads=B_KAk, writes=[Bt])
            P.op("dve", copy_fn("dve", tmpf[0:64, 4096:6144], kwin[:, :]), reads=B_kwin, writes=[Bt])
            P.op("dve", copy_fn("dve", tmpf[:, 6144:8192].rearrange("p (a c) -> p a c", a=16), Vtok[:, :, 0, :]),
                 reads=B_Vtok, writes=[Bt])
            return dump([(tmpf[:, :], [Bt], F32)])

        for hh in range(4):
            for tb in range(4):
                tbs = slice(tb * 512, (tb + 1) * 512)
                b = rot3()
                P.op("pe", pe_fn([(pb[b][0:127, :], kcT[:, g, 0:127], qraw[:, hh, tbs], True, False),
                                  (pb[b][0:127, :], ident[0:127, 0:127], cmask[0:127, tbs], False, True)]),
                     reads=[B_kcT[g], B_qraw[hh][tb], B_const, B_catt], writes=[PB[b]])
                P.op("act", act_fn(pcT[0:127, hh, tbs], pb[b][0:127, :], AF.Exp, scale=0.125),
                     reads=[PB[b]], writes=[B_pcT[hh][tb]])
        def topk_a(tt):
            tb = tt // 4
            bi = 6
            mms = []
            for hh in range(4):
                mms.append((pb[bi][:, hh * 33:(hh + 1) * 33], pcT[0:127, hh, tt * 128:(tt + 1) * 128], maug[0:127, :], True, True))
            P.op("pe", pe_fn(mms), reads=[B_pcT[hh][tb] for hh in range(4)] + [B_const], writes=[PB[bi]])
            P.op("dve", lambda e: e.reciprocal(rec4[:, :], pb[6][:, 32:132:33]), reads=[PB[bi]], writes=[B_rec4])
            P.op("dve", ts_fn(iacc[:, :], pb[bi][:, 0:32], rec4[:, 0:1], None, ALU.mult), reads=[PB[bi], B_rec4], writes=[B_iacc])
            for hh in range(1, 4):
                P.op("dve", stt_fn(iacc[:, :], pb[bi][:, hh * 33:hh * 33 + 32], rec4[:, hh:hh + 1], iacc[:, :], ALU.mult, ALU.add),
                     reads=[PB[bi], B_rec4, B_iacc], writes=[B_iacc])
            P.op("dve", tt_fn(impm[:, :], iacc[:, :], keep[:, tt - 8, :], ALU.mult), reads=[B_iacc, B_catt], writes=[B_impm])
            P.op("dve", tt_fn(impm[:, :], impm[:, :], forceb[:, tt - 8, :], ALU.add), reads=[B_impm, B_catt], writes=[B_impm])
            P.op("dve", lambda e: e.max(out=mx8[:, :], in_=impm[:, :]), reads=[B_impm], writes=[B_mx8])
            P.op("dve", lambda e: e.match_replace(out=itmp[:, :], in_to_replace=mx8[:, :], in_values=impm[:, :], imm_value=-1.0),
                 reads=[B_impm, B_mx8], writes=[B_itmp])
            P.op("dve", lambda e: e.max(out=mx8[:, :], in_=itmp[:, :]), reads=[B_itmp], writes=[B_mx8])
            P.op("dve", ts_fn(m1[:, :], impm[:, :], mx8[:, 7:8], 1.0, ALU.is_ge, ALU.subtract), reads=[B_impm, B_mx8], writes=[B_m1])

        def topk_b(tt):
            tb = tt // 4
            bm = 5
            tts = slice(tt * 128, (tt + 1) * 128)
            P.op("pe", pe_fn([(pb[bm][64:96, 0:128], m1[:, :], ident[:, :], True, True)]),
                 reads=[B_m1, B_const], writes=[PB[bm]])
            for hh in range(4):
                P.op("dve", copy_fn("dve", QA[64:96, hh, tts], pb[bm][64:96, 0:128]), reads=[PB[bm]], writes=[B_QAm[hh][tb]])

        if debug_stage == "topk":
            for tt in range(8, 16):
                topk_a(tt)
                topk_b(tt)
        if debug_stage == "topk" and g == 0:
            tmpf = Arena(nc, base + 38 * K, base + 70 * K, "dbg").take("dbgf", [128, 4096], F32)
            Bt = Buf()
            P.op("dve", lambda e: e.memset(tmpf[:, :], 0.0), writes=[Bt])
            P.op("dve", copy_fn("dve", tmpf[64:96, 0:2048], QA[64:96, 2, :]),
                 reads=[B_QAm[2][tb] for tb in range(4)], writes=[Bt])
            P.op("dve", copy_fn("dve", tmpf[0:127, 2048:4096], pcT[0:127, 1, :]),
                 reads=[B_pcT[1][tb] for tb in range(4)], writes=[Bt])
            return dump([(tmpf[:, :], [Bt], F32)])

        def unit(hh, tb):
            h = 4 * g + hh
            if True:
                tbs = slice(tb * 512, (tb + 1) * 512)
                q0 = tb * 512
                P.op("pe", pe_fn([(pb[5][:, :], vcaug[0:127, g, :], pcT[0:127, hh, tbs], True, True)]),
                     reads=[B_vc[g], B_pcT[hh][tb]], writes=[PB[5]])
                normalize(5, h, hh, tb, 0, clamp=(tb == 0))
                items = []
                for kt in range(0, 4 * tb + 4):
                    i = kt - 4 * tb
                    if i < 0:
                        items.append((kt, 0, 512, None, 0))
                    else:
                        items.append((kt, i * 128, 512, triM, i * 128))
                attn_branch(items, 3,
                            q_of=lambda lo, hi, hh=hh, q0=q0: QA[0:96, hh, q0 + lo:q0 + hi],
                            k_of=lambda kt: KA[0:96, kt * 128:(kt + 1) * 128],
                            v_of=lambda kt: Vtok[:, kt, 0, :],
                            krows=96,
                            reads_q=[B_QAq[hh][tb], B_QAm[hh][tb], B_catt],
                            reads_k=lambda kt: B_KAk[kt // 4],
                            reads_v=lambda kt: B_Vtok[kt // 4])
                normalize(3, h, hh, tb, 1, clamp=False)
                items = []
                for i in range(4):
                    items.append((4 * tb + i, i * 128, 512, triM, i * 128))
                if tb > 0:
                    for i in range(4):
                        items.append((4 * tb - 4 + i, 0, (i + 1) * 128, antiM, i * 128))
                attn_branch(items, 4,
                            q_of=lambda lo, hi, hh=hh, q0=q0: QA[0:64, hh, q0 + lo:q0 + hi],
                            k_of=lambda kt: kwin[:, kt * 128:(kt + 1) * 128],
                            v_of=lambda kt: Vtok[:, kt, 1, :],
                            krows=64,
                            reads_q=[B_QAq[hh][tb]],
                            reads_k=lambda kt: B_kwin[kt // 4],
                            reads_v=lambda kt: B_Vtok[kt // 4])
                normalize(4, h, hh, tb, 2, clamp=False)

        early = [(hh, tb) for tb in range(2) for hh in range(4)]
        for idx, tt in enumerate(range(8, 16)):
            topk_a(tt)
            unit(*early[idx])
            topk_b(tt)
        for tb in range(2, 4):
            for hh in range(4):
                unit(hh, tb)
    if debug_stage == "att":
        tmpf = Arena(nc, base + 70 * K, SB_END, "dbg").take("dbgf", [128, 16384], F32)
        Bt = Buf()
        P.barrier()
        P.op("dve", lambda e: e.memset(tmpf[:, :], 0.0), writes=[Bt])
        for h in range(8):
            P.op("dve", copy_fn("dve", tmpf[0:64, h * 2048:(h + 1) * 2048], oT[64 * (h % 2):64 * (h % 2) + 64, h // 2, :]), reads=[B_oT[h]], writes=[Bt])
        return dump([(tmpf[:, :], [Bt], F32)])
    P.barrier()

    AY = Arena(nc, base + 70 * K, base + 86 * K, "y")
    ycT = AY.take("ycT", [128, 4, S], BF16)
    B_yc = [Buf() for _ in range(4)]
    TM = Arena(nc, base + 118 * K, SB_END, "tm")
    whbc = [TM.take("whbc%d" % i, [128, 8, 384], BF16) for i in range(2)]
    B_whbc = [Buf(), Buf()]
    h_sb = TM.take("h_sb", [128, S], F32)
    u_sb = TM.take("u_sb", [128, S + 2], F32)
    y_sb = TM.take("y_sb", [128, S], F32)
    B_h, B_u, B_y = Buf(), Buf(), Buf()
    P.op("pool", lambda e: e.memset(u_sb[:, 0:2], 0.0), writes=[B_u])
    rot6 = Rot([0, 1, 2, 3, 4, 5])
    P.dma("pool", dma_fn(whbc[0][:], w_hbc_d[:, :, 0, :]), writes=[B_whbc[0]])
    TB = Arena(nc, base + 156 * K, SB_END, "tb")
    wco = TB.take("wco", [128, 4, D], BF16)
    wno = TB.take("wno", [128, 4, D], BF16)
    wg2 = [TB.take("wg2_%d" % i, [128, 8, 256], BF16) for i in range(2)]
    B_wco, B_wno = Buf(), Buf()
    B_wg2 = [Buf(), Buf()]
    P.dma("pool", dma_fn(whbc[1][:], w_hbc_d[:, :, 1, :]), writes=[B_whbc[1]])
    P.dma("pool", dma_fn(wco[:], wco_d), writes=[B_wco])
    P.dma("pool", dma_fn(wno[:], wno_d), writes=[B_wno])
    P.dma("pool", dma_fn(wg2[0][:], w_g2_d[:, :, 0, :]), writes=[B_wg2[0]])
    for j in range(4):
        wt = whbc[j % 2]
        Bw = B_whbc[j % 2]
        if 1 <= j and j + 1 < 4:
            P.dma("pool", dma_fn(whbc[(j + 1) % 2][:], w_hbc_d[:, :, j + 1, :]), writes=[B_whbc[(j + 1) % 2]])
        for tb in range(4):
            tbs = slice(tb * 512, (tb + 1) * 512)
            b = rot6()
            proj_fm(wt, 0, 128, tb, b, Bw)
            P.op("act", copy_fn("act", h_sb[:, tbs], pb[b][:, :]), reads=[PB[b]], writes=[B_h])
        for tb in range(4):
            tbs = slice(tb * 512, (tb + 1) * 512)
            b = rot6()
            proj_fm(wt, 256, 128, tb, b, Bw)
            P.op("dve", tt_fn(u_sb[:, 2 + tb * 512:2 + (tb + 1) * 512], pb[b][:, :], h_sb[:, tbs], ALU.mult),
                 reads=[PB[b], B_h], writes=[B_u])
        P.op("act", act_fn(y_sb[:, :], u_sb[:, 2:S + 2], AF.Identity, scale=cw[:, j, 2:3]), reads=[B_u, B_const], writes=[B_y])
        P.op("dve", stt_fn(y_sb[:, :], u_sb[:, 1:S + 1], cw[:, j, 1:2], y_sb[:, :], ALU.mult, ALU.add),
             reads=[B_u, B_const, B_y], writes=[B_y])
        P.op("dve", stt_fn(y_sb[:, :], u_sb[:, 0:S], cw[:, j, 0:1], y_sb[:, :], ALU.mult, ALU.add),
             reads=[B_u, B_const, B_y], writes=[B_y])
        for tb in range(4):
            tbs = slice(tb * 512, (tb + 1) * 512)
            b = rot6()
            proj_fm(wt, 128, 128, tb, b, Bw)
            P.op("dve", tt_fn(ycT[:, j, tbs], pb[b][:, :], y_sb[:, tbs], ALU.mult), reads=[PB[b], B_y], writes=[B_yc[j]])
    if debug_stage == "conv":
        tmpf = Arena(nc, base + 38 * K, base + 70 * K, "dbg").take("dbgf", [128, 8192], F32)
        Bt = Buf()
        P.barrier()
        for j in range(4):
            P.op("dve", copy_fn("dve", tmpf[:, j * 2048:(j + 1) * 2048], ycT[:, j, :]), reads=[B_yc[j]], writes=[Bt])
        return dump([(tmpf[:, :], [Bt], F32)])
    P.barrier()

    AM = Arena(nc, base + 86 * K, base + 118 * K, "m")
    mixT = AM.take("mixT", [128, 8, S], BF16)
    B_mix = [Buf() for _ in range(4)]
    sgc = [TB.take("sgc%d" % i, [128, 512], F32) for i in range(2)]
    sgn = [TB.take("sgn%d" % i, [128, 512], F32) for i in range(2)]
    ma = [TB.take("ma%d" % i, [128, 512], F32) for i in range(2)]
    mb = [TB.take("mb%d" % i, [128, 512], F32) for i in range(2)]
    B_sgc, B_sgn, B_ma, B_mb = [Buf(), Buf()], [Buf(), Buf()], [Buf(), Buf()], [Buf(), Buf()]
    rot8 = Rot([0, 1, 2, 3, 4, 5, 6, 7])
    it = 0
    for fc in range(8):
        wt = wg2[fc % 2]
        Bw = B_wg2[fc % 2]
        if fc + 1 < 8:
            P.dma("pool", dma_fn(wg2[(fc + 1) % 2][:], w_g2_d[:, :, fc + 1, :]), writes=[B_wg2[(fc + 1) % 2]])
        fcs = slice(fc * 128, (fc + 1) * 128)
        for tb in range(4):
            tbs = slice(tb * 512, (tb + 1) * 512)
            i = it % 2
            it += 1
            b1, b2, b3, b4 = rot8(), rot8(), rot8(), rot8()
            proj_fm(wt, 0, 128, tb, b1, Bw)
            P.op("act", act_fn(sgc[i][:, :], pb[b1][:, :], AF.Sigmoid), reads=[PB[b1]], writes=[B_sgc[i]])
            proj_fm(wt, 128, 128, tb, b2, Bw)
            P.op("act", act_fn(sgn[i][:, :], pb[b2][:, :], AF.Sigmoid), reads=[PB[b2]], writes=[B_sgn[i]])
            mms = [(pb[b3][:, :], wco[:, kc, fcs], ycT[:, kc, tbs], kc == 0, kc == 3) for kc in range(4)]
            P.op("pe", pe_fn(mms), reads=[B_wco] + B_yc, writes=[PB[b3]])
            mms = [(pb[b4][:, :], wno[:, hp, fcs], oT[:, hp, tbs], hp == 0, hp == 3) for hp in range(4)]
            P.op("pe", pe_fn(mms), reads=[B_wno] + B_oT, writes=[PB[b4]])
            P.op("dve", tt_fn(ma[i][:, :], pb[b3][:, :], sgc[i][:, :], ALU.mult), reads=[PB[b3], B_sgc[i]], writes=[B_ma[i]])
            P.op("dve", tt_fn(mb[i][:, :], pb[b4][:, :], sgn[i][:, :], ALU.mult), reads=[PB[b4], B_sgn[i]], writes=[B_mb[i]])
            P.op("pool", tt_fn(mixT[:, fc, tbs], ma[i][:, :], mb[i][:, :], ALU.add), reads=[B_ma[i], B_mb[i]], writes=[B_mix[tb]])
    if debug_stage == "mix":
        tmpf = Arena(nc, base + 38 * K, base + 70 * K, "dbg").take("dbgf", [128, 8192], F32)
        Bt = Buf()
        P.barrier()
        for j in range(4):
            P.op("dve", copy_fn("dve", tmpf[:, j * 2048:(j + 1) * 2048], mixT[:, j, :]), reads=B_mix, writes=[Bt])
        return dump([(tmpf[:, :], [Bt], F32)])
    P.barrier()

    AR = Arena(nc, base + 118 * K, base + 182 * K, "r")
    resid = AR.take("resid", [128, 16, D], F32)
    B_res = [Buf() for _ in range(16)]
    TCm = Arena(nc, base + 38 * K, base + 86 * K, "tcm")
    wo = TCm.take("wo", [128, 8, D], BF16)
    B_wo = Buf()
    P.dma("pool", dma_fn(wo[:], wo_d), writes=[B_wo])
    TL = Arena(nc, base + 182 * K, SB_END, "tl")
    lnp = TL.take("lnp", [128, 2, D], F32)
    B_lnp = Buf()
    P.dma("sp", dma_fn(lnp[:], lnp_d[:, 0:2, :]), writes=[B_lnp])
    xin = [TCm.take("xin%d" % i, [128, D], F32) for i in range(4)]
    B_xin = [Buf() for _ in range(4)]
    rbuf = [TCm.take("rbuf%d" % i, [128, D], F32) for i in range(2)]
    B_rbuf = [Buf(), Buf()]
    xn = [TL.take("xn%d" % i, [128, D], F32) for i in range(2)]
    B_xn = [Buf(), Buf()]
    obuf = [TL.take("obuf%d" % i, [128, D], F32) for i in range(2)]
    B_obuf = [Buf(), Buf()]
    rl = [TL.take("rl%d" % i, [128, 512], F32) for i in range(2)]
    B_rl = [Buf(), Buf()]
    x1b = [TCm.take("x1b%d" % i, [128, D], BF16) for i in range(2)]
    B_x1b = [Buf(), Buf()]
    stats4 = [TL.take("stats4_%d" % i, [128, 4, 12], F32) for i in range(2)]
    mv4 = [TL.take("mv4_%d" % i, [128, 4, 2], F32) for i in range(2)]
    rstd4 = [TL.take("rstd4_%d" % i, [128, 4], F32) for i in range(2)]
    nmr4 = [TL.take("nmr4_%d" % i, [128, 4], F32) for i in range(2)]
    ssum4 = [TL.take("ssum4_%d" % i, [128, 4], F32) for i in range(2)]
    ssq4 = [TL.take("ssq4_%d" % i, [128, 4], F32) for i in range(2)]
    B_ssum = [Buf(), Buf()]
    B_stats, B_mv, B_rstd, B_nmr = [Buf(), Buf()], [Buf(), Buf()], [Buf(), Buf()], [Buf(), Buf()]
    state["ln"] = 0

    def ln_stats_a(tiles, on_act=False):
        n = len(tiles)
        s_ = state["ln"] % 2
        state["ln"] += 1
        if on_act:
            for j, (src_ap, B_src, dst_ap, B_dst, xi, variant) in enumerate(tiles):
                P.op("act", lambda e, j=j, src_ap=src_ap, xi=xi: e.activation(xn[xi][:, :], src_ap, AF.Identity, accum_out=ssum4[s_][:, j:j + 1]),
                     reads=[B_src], writes=[B_xn[xi], B_ssum[s_]])
                P.op("act", lambda e, j=j, src_ap=src_ap, xi=xi: e.activation(xn[xi][:, :], src_ap, AF.Square, accum_out=ssq4[s_][:, j:j + 1]),
                     reads=[B_src], writes=[B_xn[xi], B_ssum[s_]])
            inv = 1.0 / D
            P.op("dve", ts_fn(mv4[s_][:, 0:n, 0], ssum4[s_][:, 0:n], inv, None, ALU.mult), reads=[B_ssum[s_]], writes=[B_mv[s_]])
            P.op("dve", tt_fn(ssum4[s_][:, 0:n], mv4[s_][:, 0:n, 0], mv4[s_][:, 0:n, 0], ALU.mult), reads=[B_mv[s_]], writes=[B_ssum[s_]])
            P.op("dve", ts_fn(rstd4[s_][:, 0:n], ssq4[s_][:, 0:n], inv, LN_EPS, ALU.mult, ALU.add), reads=[B_ssum[s_]], writes=[B_rstd[s_]])
            P.op("dve", tt_fn(rstd4[s_][:, 0:n], rstd4[s_][:, 0:n], ssum4[s_][:, 0:n], ALU.subtract),
                 reads=[B_rstd[s_], B_ssum[s_]], writes=[B_rstd[s_]])
            return s_
        for j, (src_ap, B_src, dst_ap, B_dst, xi, variant) in enumerate(tiles):
            P.op("dve", lambda e, j=j, src_ap=src_ap: e.bn_stats(stats4[s_][:, j, 0:6], src_ap[:, 0:512]),
                 reads=[B_src], writes=[B_stats[s_]])
            P.op("dve", lambda e, j=j, src_ap=src_ap: e.bn_stats(stats4[s_][:, j, 6:12], src_ap[:, 512:1024]),
                 reads=[B_src], writes=[B_stats[s_]])
            P.op("dve", lambda e, j=j: e.bn_aggr(mv4[s_][:, j, :], stats4[s_][:, j, :]), reads=[B_stats[s_]], writes=[B_mv[s_]])
        P.op("dve", ts_fn(rstd4[s_][:, 0:n], mv4[s_][:, 0:n, 1], LN_EPS, None, ALU.add), reads=[B_mv[s_]], writes=[B_rstd[s_]])
        return s_

    def ln_stats_b(tiles, s_):
        n = len(tiles)
        P.op("act", act_fn(rstd4[s_][:, 0:n], rstd4[s_][:, 0:n], AF.Sqrt), reads=[B_rstd[s_]], writes=[B_rstd[s_]])
        P.op("dve", lambda e: e.reciprocal(rstd4[s_][:, 0:n], rstd4[s_][:, 0:n]), reads=[B_rstd[s_]], writes=[B_rstd[s_]])
        if any(t[5] == "actpool" for t in tiles):
            P.op("dve", stt_fn(nmr4[s_][:, 0:n], mv4[s_][:, 0:n, 0], -1.0, rstd4[s_][:, 0:n], ALU.mult, ALU.mult),
                 reads=[B_mv[s_], B_rstd[s_]], writes=[B_nmr[s_]])
        return s_

    def ln_apply(tiles, s_, after=None):
        for j, (src_ap, B_src, dst_ap, B_dst, xi, variant) in enumerate(tiles):
            if variant == "dve":
                P.op("dve", stt_fn(xn[xi][:, :], src_ap, mv4[s_][:, j, 0:1], lnp[:, 0, :], ALU.subtract, ALU.mult),
                     reads=[B_src, B_mv[s_], B_lnp], writes=[B_xn[xi]])
                P.op("dve", stt_fn(dst_ap, xn[xi][:, :], rstd4[s_][:, j:j + 1], lnp[:, 1, :], ALU.mult, ALU.add),
                     reads=[B_xn[xi], B_rstd[s_], B_lnp], writes=[B_dst])
            else:
                P.op("act", act_fn(xn[xi][:, :], src_ap, AF.Identity, scale=rstd4[s_][:, j:j + 1], bias=nmr4[s_][:, j:j + 1]),
                     reads=[B_src, B_rstd[s_], B_nmr[s_]], writes=[B_xn[xi]])
                P.op("pool", tt_fn(xn[xi][:, :], xn[xi][:, :], lnp[:, 0, :], ALU.mult), reads=[B_xn[xi], B_lnp], writes=[B_xn[xi]])
                P.op("pool", tt_fn(dst_ap, xn[xi][:, :], lnp[:, 1, :], ALU.add), reads=[B_xn[xi], B_lnp], writes=[B_dst])
            if after is not None:
                after(j)

    def ln_stats(tiles):
        return ln_stats_b(tiles, ln_stats_a(tiles))

    def ln_block(tiles, after=None):
        ln_apply(tiles, ln_stats(tiles), after)

    x_t = x_d.rearrange("(n p) d -> n p d", p=128)

    wo_banks = {}
    rotW = Rot([0, 1, 2, 3, 4, 5])
    rotT = Rot([6, 7])

    def mixc_pe(tt):
        i = tt % 2
        tts = slice(tt * 128, (tt + 1) * 128)
        P.dma("sp", dma_fn(xin[tt % 4][:, :], x_t[tt]), writes=[B_xin[tt % 4]])
        wo_banks[tt] = []
        for half in range(2):
            hs = slice(half * 512, (half + 1) * 512)
            b = rotW()
            wo_banks[tt].append(b)
            mms = [(pb[b][:, :], mixT[:, kc, tts], wo[:, kc, hs], kc == 0, kc == 7) for kc in range(8)]
            P.op("pe", pe_fn(mms), reads=[B_wo] + B_mix, writes=[PB[b]])

    def mixc_evac(tt):
        i = tt % 2
        for half in range(2):
            hs = slice(half * 512, (half + 1) * 512)
            b = wo_banks[tt][half]
            P.op("dve", stt_fn(rbuf[i][:, hs], xin[tt % 4][:, hs], ALPHA, pb[b][:, :], ALU.mult, ALU.add),
                 reads=[B_xin[tt % 4], PB[b]], writes=[B_rbuf[i]])

    def mixc_tiles(tt):
        i = tt % 2
        return [(rbuf[i][:, :], B_rbuf[i], rbuf[i][:, :], B_rbuf[i], i, "dve")]

    def mixc_tail(tt):
        i = tt % 2
        tts = slice(tt * 128, (tt + 1) * 128)
        P.op("act", lambda e: e.mul(resid[:, tt, :], rbuf[i][:, :], ALPHA), reads=[B_rbuf[i]], writes=[B_res[tt]])
        P.op("act", copy_fn("act", x1b[i][:, :], rbuf[i][:, :]), reads=[B_rbuf[i]], writes=[B_x1b[i]])
        for half in range(2):
            b = rotT()
            mms = []
            for j in range(4):
                kc = half * 4 + j
                mms.append((pb[b][:, j * 128:(j + 1) * 128], x1b[i][:, kc * 128:(kc + 1) * 128], ident[:], True, True))
            P.op("pe", pe_fn(mms), reads=[B_x1b[i], B_const], writes=[PB[b]])
            P.op("act", copy_fn("act", xT[:, half * 4:half * 4 + 4, tts], pb[b][:, :].rearrange("p (a b) -> p a b", a=4)),
                 reads=[PB[b]], writes=[B_xT[tt]])

    TF = Arena(nc, base + 38 * K, base + 118 * K, "tf")
    wu = [TF.take("wu%d" % i, [128, 8, 1024], BF16) for i in range(2)]
    wd = [TF.take("wd%d" % i, [128, 8, 1024], BF16) for i in range(2)]
    hT = [TF.take("hT%d" % i, [128, 8, 512], BF16) for i in range(2)]
    B_wu, B_wd, B_hT = [Buf(), Buf()], [Buf(), Buf()], [Buf(), Buf()]

    mixc_pe(0)
    mixc_pe(1)
    mixc_pe(2)
    mixc_evac(0)
    ln_block(mixc_tiles(0))
    for tt in range(16):
        if tt + 3 < 16:
            mixc_pe(tt + 3)
            if tt + 3 == 15:
                P.dma("pool", dma_fn(wu[0][:], wup_d[:, :, 0:1024]), writes=[B_wu[0], B_wo])
        if tt + 1 < 16:
            mixc_evac(tt + 1)
            s_next = ln_stats(mixc_tiles(tt + 1))
        mixc_tail(tt)
        if tt + 1 < 16:
            ln_apply(mixc_tiles(tt + 1), s_next)
    if debug_stage == "ln1":
        P.barrier()
        return dump([(resid[:, tt, :], [B_res[tt]], F32) for tt in range(16)])
    P.barrier()

    P.dma("sp", dma_fn(lnp[:], lnp_d[:, 2:4, :]), writes=[B_lnp])
    rotU = Rot([0, 1, 2, 3])
    rotD = Rot([4, 5, 6, 7])

    def ln2_tiles(tb_):
        tiles = []
        for t4 in range(4):
            tt = 4 * tb_ + t4
            tiles.append((resid[:, tt, :], B_res[tt], obuf[tt % 2][:, :], B_obuf[tt % 2], tt % 2, "dve"))

        def store(j):
            tt = 4 * tb_ + j
            P.dma("sp", dma_fn(y_d[tt * 128:(tt + 1) * 128, :], obuf[tt % 2][:, :]), reads=[B_obuf[tt % 2]], is_output=True)

        return tiles, store

    def load_wu(q):
        P.dma("pool", dma_fn(wu[q % 2][:], wup_d[:, :, q * 1024:(q + 1) * 1024]), writes=[B_wu[q % 2]])

    def load_wd(q):
        P.dma("pool", dma_fn(wd[q % 2][:], wdn_d[:, q * 8:(q + 1) * 8, :]), writes=[B_wd[q % 2]])

    def up_group(blk, f):
        q, tb = divmod(blk, 4)
        i = blk % 2
        b = rotU()
        mms = [(pb[b][:, :], wu[q % 2][:, kc, f * 128:(f + 1) * 128], xT[:, kc, tb * 512:(tb + 1) * 512], kc == 0, kc == 7)
               for kc in range(8)]
        P.op("pe", pe_fn(mms), reads=[B_wu[q % 2]] + B_xT[4 * tb:4 * tb + 4], writes=[PB[b]])
        r = f % 2
        P.op("act", act_fn(rl[r][:, :], pb[b][:, :], AF.Relu), reads=[PB[b]], writes=[B_rl[r]])
        P.op("pool", tt_fn(hT[i][:, f, :], rl[r][:, :], rl[r][:, :], ALU.mult), reads=[B_rl[r]], writes=[B_hT[i]])

    def down_group(blk, j):
        q, tb = divmod(blk, 4)
        i = blk % 2
        t4, half = divmod(j, 2)
        tt = 4 * tb + t4
        hs = slice(half * 512, (half + 1) * 512)
        b = rotD()
        mms = [(pb[b][:, :], hT[i][:, f, t4 * 128:(t4 + 1) * 128], wd[q % 2][:, f, hs], f == 0, f == 7) for f in range(8)]
        P.op("pe", pe_fn(mms), reads=[B_hT[i], B_wd[q % 2]], writes=[PB[b]])
        P.op("dve", tt_fn(resid[:, tt, hs], pb[b][:, :], resid[:, tt, hs], ALU.add),
             reads=[PB[b], B_res[tt]], writes=[B_res[tt]])

    load_wd(0)
    pending = None
    for s_blk in range(17):
        if s_blk < 16:
            q, tb = divmod(s_blk, 4)
            if tb == 0 and q + 1 < 4:
                load_wu(q + 1)
            if tb == 1 and q + 1 < 4:
                load_wd(q + 1)
        for j in range(8):
            if s_blk < 16:
                up_group(s_blk, j)
            if s_blk >= 1:
                down_group(s_blk - 1, j)
        if pending is not None:
            tiles_p, s_p, store_p = pending
            ln_apply(tiles_p, ln_stats_b(tiles_p, s_p), after=store_p)
            pending = None
        if s_blk >= 1 and (s_blk - 1) // 4 == 3:
            tiles_p, store_p = ln2_tiles((s_blk - 1) % 4)
            pending = (tiles_p, ln_stats_a(tiles_p, on_act=True), store_p)
    tiles_p, s_p, store_p = pending
    ln_apply(tiles_p, ln_stats_b(tiles_p, s_p), after=store_p)

    P.run_block()
    P.close()
    return nc


def _constants():
    bf = ml_dtypes.bfloat16
    c = {}
    c["c_ident"] = np.eye(128, dtype=np.float32).astype(bf)
    kl = np.arange(128)[:, None]
    ql = np.arange(128)[None, :]
    c["c_tri"] = np.where(kl <= ql, 0.0, -BIG).astype(np.float32).astype(bf)
    c["c_anti"] = np.where(kl > ql, 0.0, -BIG).astype(np.float32).astype(bf)
    n = np.arange(128)[:, None]
    t = np.arange(S)[None, :]
    c["c_cmask"] = np.where(16 * n + 31 <= t, 0.0, -BIG).astype(np.float32).astype(bf)
    j = np.arange(32)[:, None]
    c["c_e30"] = np.where((t // 64) == j, BIG, 0.0).astype(np.float32).astype(bf)
    nn = np.arange(128)[:, None] * 16
    jj = np.arange(32)[None, :] * 64
    m = ((nn < jj + 64) & (nn + 32 > jj)).astype(np.float32)
    m[127, :] = 0.0
    maug = np.concatenate([m, np.ones((128, 1), np.float32)], axis=1)
    c["c_maug"] = maug.astype(bf)
    perm = np.zeros((64, 64), np.float32)
    for mm_ in range(16):
        partner = mm_ + 8 if mm_ < 8 else mm_ - 8
        perm[partner, mm_] = 1.0
    c["c_perm"] = perm.astype(bf)
    sel = np.zeros((56, 24, 64), np.float32)
    for r in range(24):
        sel[r, r, :] = 1.0
        sel[32 + r, r, :] = 1.0
    c["c_selall"] = sel.astype(bf)
    keep = np.zeros((128, 8, 32), np.float32)
    forceb = np.zeros((128, 8, 32), np.float32)
    for i in range(8):
        tt = 8 + i
        tq = tt * 128 + np.arange(128)
        cur = tq // 64
        for p in range(128):
            cu = cur[p]
            for jb in range(32):
                if jb == 0:
                    forceb[p, i, jb] = 3e9
                elif jb == cu:
                    forceb[p, i, jb] = 2e9
                elif jb == cu - 1:
                    forceb[p, i, jb] = 1e9
                elif jb > cu:
                    forceb[p, i, jb] = -1.0
                else:
                    keep[p, i, jb] = 1.0
    c["c_keep"] = keep
    c["c_forceb"] = forceb
    inv = ROPE_THETA ** (-np.arange(0, 16, 2, dtype=np.float32) / np.float32(16.0))
    ang = np.arange(S, dtype=np.float32)[:, None] * inv[None, :].astype(np.float32)
    cos = np.cos(ang).astype(np.float32).T
    sin = np.sin(ang).astype(np.float32).T
    ct = np.ones((64, S), np.float32)
    st = np.zeros((64, S), np.float32)
    ct[0:8] = cos
    ct[8:16] = cos
    st[0:8] = -sin
    st[8:16] = sin
    c["c_ct"] = ct
    c["c_st"] = st
    return c


def _prep_weights(inp):
    f = lambda a: np.ascontiguousarray(a, dtype=np.float32)
    w_in = np.asarray(inp["w_in"])[0]
    wr = w_in.reshape(8, 128, 4888).transpose(1, 0, 2)
    out = {}
    hbc = np.stack([np.concatenate([wr[:, :, j * 128:(j + 1) * 128],
                                    wr[:, :, 512 + j * 128:512 + (j + 1) * 128],
                                    wr[:, :, 1024 + j * 128:1024 + (j + 1) * 128]], axis=2) for j in range(4)], axis=2)
    out["w_hbc"] = f(hbc)
    att = []
    for g in range(2):
        parts = [wr[:, :, 1536 + g * 256:1536 + (g + 1) * 256],
                 wr[:, :, 2304 + g * 64:2304 + (g + 1) * 64],
                 wr[:, :, 2560 + g * 64:2560 + (g + 1) * 64],
                 wr[:, :, 2432 + g * 64:2432 + (g + 1) * 64],
                 wr[:, :, 2688 + g * 64:2688 + (g + 1) * 64],
                 wr[:, :, 2048 + g * 64:2048 + (g + 1) * 64],
                 wr[:, :, 2176 + g * 64:2176 + (g + 1) * 64]]
        att.append(np.concatenate(parts, axis=2))
    out["w_att"] = f(np.stack(att, axis=2))
    out["w_gate"] = f(wr[:, :, 2816:2840])
    g2 = np.stack([np.concatenate([wr[:, :, 2840 + fc * 128:2840 + (fc + 1) * 128],
                                   wr[:, :, 3864 + fc * 128:3864 + (fc + 1) * 128]], axis=2) for fc in range(8)], axis=2)
    out["w_g2"] = f(g2)
    conv_w = np.asarray(inp["conv_w"])[0][:, 0, :]
    out["cw"] = f(conv_w.reshape(3, 4, 128).transpose(2, 1, 0))
    out["wco"] = f(np.asarray(inp["w_conv_out"])[0].reshape(4, 128, 1024).transpose(1, 0, 2))
    out["wno"] = f(np.asarray(inp["w_nsa_out"])[0].reshape(4, 128, 1024).transpose(1, 0, 2))
    out["wo"] = f(np.asarray(inp["w_o"])[0].reshape(8, 128, 1024).transpose(1, 0, 2))
    out["wup"] = f(np.asarray(inp["w_up"])[0].reshape(8, 128, 4096).transpose(1, 0, 2))
    out["wdn"] = f(np.asarray(inp["w_down"])[0].reshape(32, 128, 1024).transpose(1, 0, 2))
    out["w1k"] = f(np.asarray(inp["w_k_cmp1"])[0].transpose(1, 0, 2))
    out["w1v"] = f(np.asarray(inp["w_v_cmp1"])[0].transpose(1, 0, 2))
    out["w2k"] = f(np.asarray(inp["w_k_cmp2"])[0])
    out["w2v"] = f(np.asarray(inp["w_v_cmp2"])[0])
    out["peTk"] = f(np.asarray(inp["pe_k_cmp"])[0].T)
    out["peTv"] = f(np.asarray(inp["pe_v_cmp"])[0].T)
    lnp = np.stack([np.asarray(inp["ln1_g"])[0], np.asarray(inp["ln1_b"])[0],
                    np.asarray(inp["ln2_g"])[0], np.asarray(inp["ln2_b"])[0]], axis=0)
    out["lnp"] = f(np.broadcast_to(lnp[None], (128, 4, 1024)))
    return out


def kernel(**inputs):
    x = np.asarray(inputs["x"], dtype=np.float32)
    shared = _prep_weights(inputs)
    shared.update(_constants())
    nc = build_program(DEBUG_STAGE)
    in_maps = []
    for b in range(NCORES):
        m = dict(shared)
        m["x"] = np.ascontiguousarray(x[b])
        in_maps.append(m)
    res = run_bass_kernel_spmd(nc, in_maps, core_ids=list(range(NCORES)))
    if DEBUG_STAGE is not None:
        return np.stack([np.asarray(r["dbg"]) for r in res.results], axis=0)
    return np.stack([np.asarray(r["y"], dtype=np.float32) for r in res.results], axis=0)
```

```python
import numpy as np
import ml_dtypes
import concourse.bass as bass
import concourse.mybir as mybir
from concourse.bass_utils import run_bass_kernel_spmd

F32 = mybir.dt.float32
BF16 = mybir.dt.bfloat16
AF = mybir.ActivationFunctionType
ALU = mybir.AluOpType

S = 2048
D = 1024
NCORES = 8
ALPHA = 2.0 ** 0.25
LN_EPS = 1e-5
BIG = 32768.0
ROPE_THETA = 500000.0

ENGS = ["pe", "act", "dve", "pool", "sp"]
N_DMA_SEMS = 24
SB_BASE = 16512
SB_END = 229376 - 64

KEEP_WARM = 0
DEBUG_STAGE = None


class Buf:
    __slots__ = ("name", "w", "r", "excl")

    def __init__(self, name="", excl=False):
        self.name = name
        self.w = None
        self.r = {}
        self.excl = excl


class Prog:
    def __init__(self, nc):
        self.nc = nc
        self.ops = {e: [] for e in ENGS}
        self.cnt = {e: 0 for e in ENGS}
        self.waited = {e: {} for e in ENGS}
        self.sems = {}
        self._ctx = []
        for e in ["pe", "act", "dve", "pool"]:
            self.sems[e] = self._sem("c_" + e)
        self.dma_cnt = []
        for i in range(N_DMA_SEMS):
            self.sems[("dma", i)] = self._sem("d_%d" % i)
            self.dma_cnt.append(0)
        self.dma_rr = 0
        self.dma_rr_sw = 0
        self.out_tokens = []

    def _sem(self, name):
        cm = self.nc.semaphore(name)
        s = cm.__enter__()
        self._ctx.append(cm)
        return s

    def close(self):
        for cm in reversed(self._ctx):
            cm.__exit__(None, None, None)

    def _filter(self, eng, deps):
        out = []
        wd = self.waited[eng]
        for k, v in deps.items():
            if eng == "pe" and k == "pe":
                continue
            if wd.get(k, 0) >= v:
                continue
            wd[k] = v
            out.append((k, v))
        return out

    def _deps(self, eng, reads, writes):
        deps = {}

        def add(k, v):
            if deps.get(k, 0) < v:
                deps[k] = v

        for b in reads:
            if b.w is not None:
                add(*b.w)
            if b.excl:
                for k, v in b.r.items():
                    if k != eng:
                        add(k, v)
        for b in writes:
            if b.w is not None:
                add(*b.w)
            for k, v in b.r.items():
                add(k, v)
        return self._filter(eng, deps)

    def _mark(self, tok, reads, writes):
        for b in reads:
            if b.r.get(tok[0], 0) < tok[1]:
                b.r[tok[0]] = tok[1]
        for b in writes:
            b.w = tok
            b.r = {}

    def op(self, eng, fn, reads=(), writes=()):
        waits = self._deps(eng, reads, writes)
        self.cnt[eng] += 1
        tok = (eng, self.cnt[eng])
        self.ops[eng].append((waits, fn, (eng, 1)))
        self._mark(tok, reads, writes)
        return tok

    def dma(self, q, fn, reads=(), writes=(), is_output=False):
        half = N_DMA_SEMS // 2
        if q == "pool":
            i = half + self.dma_rr_sw
            self.dma_rr_sw = (self.dma_rr_sw + 1) % half
        else:
            i = self.dma_rr
            self.dma_rr = (self.dma_rr + 1) % half
        k = ("dma", i)
        waits = self._deps(q, reads, writes)
        prev = self.dma_cnt[i]
        if prev > 0 and self.waited[q].get(k, 0) < prev:
            self.waited[q][k] = prev
            waits.append((k, prev))
        self.dma_cnt[i] += 16
        tok = (k, self.dma_cnt[i])
        self.ops[q].append((waits, fn, (k, 16)))
        self._mark(tok, reads, writes)
        if is_output:
            self.out_tokens.append(tok)
        return tok

    def barrier(self):
        deps = {}
        for e in ["pe", "act", "dve", "pool"]:
            if self.cnt[e] > 0:
                deps[e] = self.cnt[e]
        for i in range(N_DMA_SEMS):
            if self.dma_cnt[i] > 0:
                deps[("dma", i)] = self.dma_cnt[i]
        for e in ENGS:
            d = dict(deps)
            waits = []
            wd = self.waited[e]
            for k, v in d.items():
                if wd.get(k, 0) >= v:
                    continue
                wd[k] = v
                waits.append((k, v))
            if waits:
                self.ops[e].append((waits, None, None))

    def finish(self):
        waits = []
        for k, v in self.out_tokens:
            if self.waited["sp"].get(k, 0) < v:
                self.waited["sp"][k] = v
                waits.append((k, v))
        self.ops["sp"].append((waits, None, None))

    def _plan_signals(self):
        needed = {e: set() for e in ["pe", "act", "dve", "pool"]}
        for e in ENGS:
            for waits, fn, inc in self.ops[e]:
                for k, v in waits:
                    if k in needed:
                        needed[k].add(v)
        self.sigmap = {}
        for e in needed:
            m = {}
            c = 0
            for idx in range(1, self.cnt[e] + 1):
                if idx in needed[e]:
                    c += 1
                    m[idx] = c
            self.sigmap[e] = m

    def replay(self, eng, e):
        idx = 0
        for waits, fn, inc in self.ops[eng]:
            for k, v in waits:
                if k in self.sigmap:
                    v = self.sigmap[k][v]
                e.wait_ge(self.sems[k], v)
            if fn is None:
                continue
            ins = fn(e)
            if inc[0] in self.sigmap:
                idx += 1
                if idx in self.sigmap[inc[0]]:
                    ins.then_inc(self.sems[inc[0]], 1)
            else:
                ins.then_inc(self.sems[inc[0]], inc[1])

    def run_block(self):
        self.finish()
        self._plan_signals()
        with self.nc.Block() as block:
            @block.tensor
            def _(e):
                self.replay("pe", e)

            @block.scalar
            def _(e):
                self.replay("act", e)

            @block.vector
            def _(e):
                self.replay("dve", e)

            @block.gpsimd
            def _(e):
                self.replay("pool", e)

            @block.sync
            def _(e):
                self.replay("sp", e)


class Arena:
    def __init__(self, nc, lo, hi, tag):
        self.nc, self.lo, self.hi, self.cur, self.tag = nc, lo, hi, lo, tag
        self.n = 0

    def take(self, name, shape, dt):
        per = 1
        for s_ in shape[1:]:
            per *= s_
        nbytes = per * (2 if dt == BF16 else 4)
        self.cur = (self.cur + 31) // 32 * 32
        assert self.cur + nbytes <= self.hi, (self.tag, name, self.cur, nbytes, self.hi)
        t = self.nc.alloc_sbuf_tensor_at("%s_%s" % (self.tag, name), list(shape), dt, offset=self.cur)
        self.cur += nbytes
        return t


def pe_fn(mms):
    def fn(e):
        last = None
        for (o, l, r, st, sp) in mms:
            last = e.matmul(o, lhsT=l, rhs=r, start=st, stop=sp)
        return last
    return fn


def copy_fn(eng, out, in_):
    if eng == "act":
        return lambda e: e.copy(out, in_)
    return lambda e: e.tensor_copy(out, in_)


def act_fn(out, in_, func, scale=1.0, bias=None):
    if bias is None:
        return lambda e: e.activation(out, in_, func, scale=scale)
    return lambda e: e.activation(out, in_, func, bias=bias, scale=scale)


def tt_fn(out, a, b, op):
    return lambda e: e.tensor_tensor(out, a, b, op)


def ts_fn(out, a, s1, s2, op0, op1=None):
    if op1 is None:
        return lambda e: e.tensor_scalar(out, a, s1, None, op0=op0)
    return lambda e: e.tensor_scalar(out, a, s1, s2, op0=op0, op1=op1)


def stt_fn(out, a, s, b, op0, op1):
    return lambda e: e.scalar_tensor_tensor(out, a, s, b, op0=op0, op1=op1)


def dma_fn(out, in_):
    return lambda e: e.dma_start(out=out, in_=in_)


def build_program(debug_stage=None):
    nc = bass.Bass("TRN2", target_bir_lowering=False)

    def din(name, shape, dt=F32):
        return nc.dram_tensor(name, list(shape), dt, kind="ExternalInput").ap()

    x_d = din("x", [S, D])
    w_hbc_d = din("w_hbc", [128, 8, 4, 384])
    w_att_d = din("w_att", [128, 8, 2, 640])
    w_gate_d = din("w_gate", [128, 8, 24])
    w_g2_d = din("w_g2", [128, 8, 8, 256])
    cw_d = din("cw", [128, 4, 3])
    wco_d = din("wco", [128, 4, 1024])
    wno_d = din("wno", [128, 4, 1024])
    wo_d = din("wo", [128, 8, 1024])
    wup_d = din("wup", [128, 8, 4096])
    wdn_d = din("wdn", [128, 32, 1024])
    w1_d = [din("w1k", [64, 32, 128]), din("w1v", [64, 32, 128])]
    w2_d = [din("w2k", [128, 64]), din("w2v", [128, 64])]
    peT_d = [din("peTk", [64, 32]), din("peTv", [64, 32])]
    lnp_d = din("lnp", [128, 4, 1024])
    ident_d = din("c_ident", [128, 128], BF16)
    tri_d = din("c_tri", [128, 128], BF16)
    anti_d = din("c_anti", [128, 128], BF16)
    cmask_d = din("c_cmask", [128, S], BF16)
    e30_d = din("c_e30", [32, S], BF16)
    maug_d = din("c_maug", [128, 33], BF16)
    perm_d = din("c_perm", [64, 64], BF16)
    selall_d = din("c_selall", [56, 24, 64], BF16)
    keep_d = din("c_keep", [128, 8, 32])
    forceb_d = din("c_forceb", [128, 8, 32])
    ct_d = din("c_ct", [64, S])
    st_d = din("c_st", [64, S])

    y_d = nc.dram_tensor("y", [S, D], F32, kind="ExternalOutput").ap()
    dbg_d = None
    if debug_stage is not None:
        dbg_d = nc.dram_tensor("dbg", [128, 16384], F32, kind="ExternalOutput").ap()

    P = Prog(nc)
    K = 1024
    base = SB_BASE

    pb = [nc.alloc_psum_tensor("pb%d" % i, [128, 512], F32) for i in range(8)]
    PB = [Buf("pb%d" % i, excl=True) for i in range(8)]

    class Rot:
        def __init__(self, ids):
            self.ids, self.i = ids, 0

        def __call__(self):
            b = self.ids[self.i]
            self.i = (self.i + 1) % len(self.ids)
            return b

    A0 = Arena(nc, base, base + 3 * K, "p")
    ident = A0.take("ident", [128, 128], BF16)
    triM = A0.take("tri", [128, 128], BF16)
    antiM = A0.take("anti", [128, 128], BF16)
    permM = A0.take("perm", [64, 64], BF16)
    maug = A0.take("maug", [128, 33], BF16)
    kcT = A0.take("kcT", [64, 2, 128], BF16)
    vcaug = A0.take("vcaug", [128, 2, 128], BF16)
    cw = A0.take("cw", [128, 4, 3], F32)
    B_const = Buf("const")
    B_kcT = [Buf("kcT0"), Buf("kcT1")]
    B_vc = [Buf("vc0"), Buf("vc1")]
    base = SB_BASE - 3 * K
    AX = Arena(nc, base + 6 * K, base + 38 * K, "x")
    xT = AX.take("xT", [128, 8, S], BF16)
    B_xT = [Buf("xT%d" % n) for n in range(16)]

    for (t, d) in [(ident, ident_d), (triM, tri_d), (antiM, anti_d), (permM, perm_d), (maug, maug_d), (cw, cw_d)]:
        P.dma("sp", dma_fn(t[:], d), writes=[B_const])
    P.op("dve", lambda e: e.memset(vcaug[:, :, 64:128], 1.0), writes=B_vc)

    TAc = Arena(nc, base + 70 * K, base + 112 * K, "tac")
    KA = TAc.take("KA", [96, S], BF16)
    sgT = TAc.take("sgT", [56, S], BF16)
    cmask = TAc.take("cmask", [128, S], BF16)
    CT = TAc.take("CT", [64, S], F32)
    ST = TAc.take("ST", [64, S], F32)
    selall = TAc.take("selall", [56, 24, 64], BF16)
    keep = TAc.take("keep", [128, 8, 32], F32)
    forceb = TAc.take("forceb", [128, 8, 32], F32)
    wA = TAc.take("wA", [128, 8, 512], BF16)
    wg = TAc.take("wg", [128, 8, 24], BF16)
    B_catt = Buf("catt")
    B_wA, B_wg = Buf(), Buf()
    for (t, d) in [(cmask, cmask_d), (selall, selall_d), (keep, keep_d), (forceb, forceb_d), (CT, ct_d), (ST, st_d)]:
        P.dma("sp", dma_fn(t[:], d), writes=[B_catt])
    P.dma("sp", dma_fn(KA[64:96, :], e30_d), writes=[B_catt])

    def dump(ap_list):
        col = 0
        for (ap, bufs, dt) in ap_list:
            p, n = ap.shape
            if dt == F32:
                P.dma("sp", dma_fn(dbg_d[0:p, col:col + n], ap), reads=bufs, is_output=True)
            else:
                P.dma("pool", dma_fn(dbg_d[0:p, col:col + n], ap), reads=bufs, is_output=True)
            col += n
        P.run_block()
        P.close()
        return nc

    T0 = Arena(nc, base + 38 * K, base + 100 * K, "t0")
    xb = T0.take("xb", [128, 16, D], BF16)
    B_xb = [Buf("xb%d" % c) for c in range(4)]
    x_v = x_d.rearrange("(n p) d -> p n d", p=128)
    for c in range(4):
        P.dma("pool", dma_fn(xb[:, 4 * c:4 * c + 4, :], x_v[:, 4 * c:4 * c + 4, :]), writes=[B_xb[c]])
    TC = Arena(nc, SB_END - 42 * K, SB_END, "tc")
    kcmpT = TC.take("kcmpT", [64, 4, S], BF16)
    w1 = [TC.take("w1k", [64, 32, 128], BF16), TC.take("w1v", [64, 32, 128], BF16)]
    w2 = [TC.take("w2k", [128, 64], BF16), TC.take("w2v", [128, 64], BF16)]
    peT = [TC.take("peTk", [64, 32], BF16), TC.take("peTv", [64, 32], BF16)]
    wck = [TC.take("wck0", [128, 8, 128], BF16), TC.take("wck1", [128, 8, 128], BF16)]
    B_w1 = [Buf(), Buf()]
    B_wck = [Buf(), Buf()]
    for g in range(2):
        P.dma("pool", dma_fn(wck[g][:], w_att_d[:, :, g, 512:640]), writes=[B_wck[g]])
    for kind in range(2):
        P.dma("pool", dma_fn(w1[kind][:], w1_d[kind]), writes=[B_w1[kind]])
        P.dma("pool", dma_fn(w2[kind][:], w2_d[kind]), writes=[B_w1[kind]])
        P.dma("pool", dma_fn(peT[kind][:], peT_d[kind]), writes=[B_w1[kind]])
    P.dma("pool", dma_fn(wg[:], w_gate_d), writes=[B_wg])
    P.dma("pool", dma_fn(wA[:], w_att_d[:, :, 0, 0:512]), writes=[B_wA])
    rot_all = Rot([0, 1, 2, 3, 4, 5, 6, 7])
    alt = 0
    for n in range(16):
        for half in range(2):
            b = rot_all()
            mms = []
            for j in range(4):
                kc = half * 4 + j
                mms.append((pb[b][:, j * 128:(j + 1) * 128], xb[:, n, kc * 128:(kc + 1) * 128], ident[:], True, True))
            P.op("pe", pe_fn(mms), reads=[B_xb[n // 4], B_const], writes=[PB[b]])
            eng = ["act", "dve"][alt % 2]
            alt += 1
            P.op(eng, copy_fn(eng, xT[:, half * 4:half * 4 + 4, n * 128:(n + 1) * 128],
                              pb[b][:, :].rearrange("p (a b) -> p a b", a=4)),
                 reads=[PB[b]], writes=[B_xT[n]])
    if debug_stage == "xT":
        tmpf = T0.take("dbgf", [128, 4096], F32)
        Bt = Buf()
        for kc in range(2):
            P.op("dve", copy_fn("dve", tmpf[:, kc * 2048:(kc + 1) * 2048], xT[:, kc, :]), reads=B_xT, writes=[Bt])
        return dump([(tmpf[:, :], [Bt], F32)])

    def proj_fm(w_tile, wcol0, m, tb, bank, B_w):
        mms = []
        for kc in range(8):
            mms.append((pb[bank][0:m, :], w_tile[:, kc, wcol0:wcol0 + m], xT[:, kc, tb * 512:(tb + 1) * 512],
                        kc == 0, kc == 7))
        P.op("pe", pe_fn(mms), reads=[B_w] + B_xT[4 * tb:4 * tb + 4], writes=[PB[bank]])

    bias_sb = TC.take("bias", [128, 1], F32)
    gx = TC.take("gx", [128, 128], F32)
    gx2 = TC.take("gx2", [128, 128], F32)
    gz = TC.take("gz", [128, 128], F32)
    gs = TC.take("gs", [128, 128], F32)
    hid = TC.take("hid", [128, 128], BF16)
    B_kcmp = [Buf() for _ in range(4)]
    B_bias, B_gx, B_gx2, B_gz, B_gs, B_hid = Buf(), Buf(), Buf(), Buf(), Buf(), Buf()
    rot4 = Rot([0, 1, 2, 3])
    alt = 0
    for g in range(2):
        for kind in range(2):
            idx = kind * 2 + g
            for tb in range(4):
                b = rot4()
                proj_fm(wck[g], kind * 64, 64, tb, b, B_wck[g])
                eng = ["act", "dve"][alt % 2]
                alt += 1
                P.op(eng, copy_fn(eng, kcmpT[:, idx, tb * 512:(tb + 1) * 512], pb[b][0:64, :]),
                     reads=[PB[b]], writes=[B_kcmp[idx]])
    for kind in range(2):
        for g in range(2):
            idx = kind * 2 + g
            bh, bbias, bo = 4, 5, 6
            mms = []
            for l in range(32):
                mms.append((pb[bh][:, 0:127], w1[kind][:, l, :], kcmpT[:, idx, l:l + 16 * 126 + 1:16], l == 0, l == 31))
            P.op("pe", pe_fn(mms), reads=[B_w1[kind], B_kcmp[idx]], writes=[PB[bh]])
            mms = []
            for l in range(32):
                mms.append((pb[bbias][:, 0:1], w1[kind][:, l, :], peT[kind][:, l:l + 1], l == 0, l == 31))
            P.op("pe", pe_fn(mms), reads=[B_w1[kind]], writes=[PB[bbias]])
            P.op("dve", copy_fn("dve", bias_sb[:, :], pb[bbias][:, 0:1]), reads=[PB[bbias]], writes=[B_bias])
            P.op("dve", ts_fn(gx[:, 0:127], pb[bh][:, 0:127], bias_sb[:, 0:1], None, ALU.add),
                 reads=[PB[bh], B_bias], writes=[B_gx])
            P.op("dve", tt_fn(gx2[:, 0:127], gx[:, 0:127], gx[:, 0:127], ALU.mult), reads=[B_gx], writes=[B_gx2])
            P.op("dve", ts_fn(gx2[:, 0:127], gx2[:, 0:127], 0.044715, 1.0, ALU.mult, ALU.add),
                 reads=[B_gx2], writes=[B_gx2])
            P.op("dve", tt_fn(gz[:, 0:127], gx2[:, 0:127], gx[:, 0:127], ALU.mult), reads=[B_gx2, B_gx], writes=[B_gz])
            P.op("act", act_fn(gs[:, 0:127], gz[:, 0:127], AF.Sigmoid, scale=1.5957691216057308),
                 reads=[B_gz], writes=[B_gs])
            P.op("dve", tt_fn(hid[:, 0:127], gx[:, 0:127], gs[:, 0:127], ALU.mult), reads=[B_gx, B_gs], writes=[B_hid])
            if kind == 0:
                P.op("pe", pe_fn([(pb[bo][0:64, 0:127], w2[0][:, :], hid[:, 0:127], True, True)]),
                     reads=[B_w1[0], B_hid], writes=[PB[bo]])
                P.op("dve", copy_fn("dve", kcT[:, g, 0:127], pb[bo][0:64, 0:127]), reads=[PB[bo]], writes=[B_kcT[g]])
            else:
                P.op("pe", pe_fn([(pb[bo][0:127, 0:64], hid[:, 0:127], w2[1][:, :], True, True)]),
                     reads=[B_w1[1], B_hid], writes=[PB[bo]])
                P.op("dve", copy_fn("dve", vcaug[0:127, g, 0:64], pb[bo][0:127, 0:64]), reads=[PB[bo]], writes=[B_vc[g]])
    if debug_stage == "cmp":
        tmpf = TC.take("dbgf", [128, 512], F32)
        Bt = Buf()
        P.op("dve", lambda e: e.memset(tmpf[:, :], 0.0), writes=[Bt])
        for g in range(2):
            P.op("dve", copy_fn("dve", tmpf[0:64, g * 128:(g + 1) * 128], kcT[:, g, :]), reads=[B_kcT[g]], writes=[Bt])
            P.op("dve", copy_fn("dve", tmpf[:, 256 + g * 128:256 + (g + 1) * 128], vcaug[:, g, :]), reads=[B_vc[g]], writes=[Bt])
        return dump([(tmpf[:, :], [Bt], F32)])
    P.barrier()

    AO = Arena(nc, base + 38 * K, base + 70 * K, "o")
    oT = AO.take("oT", [128, 4, S], BF16)
    B_oT = [Buf("oT%d" % h) for h in range(8)]
    TA = Arena(nc, base + 112 * K, SB_END, "ta")
    QA = TA.take("QA", [96, 4, S], BF16)
    qraw = TA.take("qraw", [64, 4, S], BF16)
    pcT = TA.take("pcT", [128, 4, S], BF16)
    kwin = TA.take("kwin", [64, S], BF16)
    Vtok = TA.take("Vtok", [128, 16, 2, 128], BF16)
    sgf = TA.take("sgf", [24, 512], F32)
    kraw = [TA.take("kraw%d" % i, [64, 512], BF16) for i in range(2)]
    rt1 = [TA.take("rt1_%d" % i, [64, 512], F32) for i in range(2)]
    rt2 = [TA.take("rt2_%d" % i, [64, 512], F32) for i in range(2)]
    pT = [TA.take("pT%d" % i, [128, 512], BF16) for i in range(3)]
    rden = TA.take("rden", [64, 512], F32)
    lnb = [TA.take("lnb%d" % i, [64, 512], F32) for i in range(2)]
    rdenA = [TA.take("rdenA%d" % i, [64, 512], F32) for i in range(2)]
    B_lnb, B_rdenA = [Buf(), Buf()], [Buf(), Buf()]
    dent = TA.take("dent", [64, 512], F32)
    nt1 = TA.take("nt1", [64, 512], F32)
    nt2 = TA.take("nt2", [64, 512], F32)
    nacc = [TA.take("nacc%d" % i, [64, 512], F32) for i in range(2)]
    rec4 = TA.take("rec4", [128, 4], F32)
    iacc = TA.take("iacc", [128, 32], F32)
    impm = TA.take("impm", [128, 32], F32)
    itmp = TA.take("itmp", [128, 32], F32)
    mx8 = TA.take("mx8", [128, 8], F32)
    m1 = TA.take("m1", [128, 32], BF16)

    B_QAq = [[Buf() for _ in range(4)] for _ in range(4)]
    B_QAm = [[Buf() for _ in range(4)] for _ in range(4)]
    B_qraw = [[Buf() for _ in range(4)] for _ in range(4)]
    B_pcT = [[Buf() for _ in range(4)] for _ in range(4)]
    B_KAk = [Buf() for _ in range(4)]
    B_kwin = [Buf() for _ in range(4)]
    B_Vtok = [Buf() for _ in range(4)]
    B_sg = [Buf() for _ in range(4)]
    B_sgf = Buf()
    B_kraw = [Buf(), Buf()]
    B_rt1 = [Buf(), Buf()]
    B_rt2 = [Buf(), Buf()]
    B_pT = [Buf(), Buf(), Buf()]
    B_rden, B_dent, B_nt1, B_nt2 = Buf(), Buf(), Buf(), Buf()
    B_nacc = [Buf(), Buf()]
    B_rec4, B_iacc, B_impm, B_itmp, B_mx8, B_m1 = Buf(), Buf(), Buf(), Buf(), Buf(), Buf()

    P.op("pool", lambda e: e.memset(QA[64:96, :, 0:1024], 0.0), writes=[B_QAm[hh][tb] for hh in range(4) for tb in range(2)])
    P.op("pool", lambda e: e.memset(Vtok[:, :, :, 64:128], 1.0), writes=B_Vtok)

    P.op("pool", lambda e: e.memset(sgT[:, :], 0.0), writes=B_sg)
    for tb in range(4):
        tbs = slice(tb * 512, (tb + 1) * 512)
        b = rot4()
        proj_fm(wg, 0, 24, tb, b, B_wg)
        P.op("act", act_fn(sgf[:, :], pb[b][0:24, :], AF.Sigmoid), reads=[PB[b]], writes=[B_sgf])
        P.op("dve", copy_fn("dve", sgT[0:24, tbs], sgf[:, :]), reads=[B_sgf], writes=[B_sg[tb]])
        P.op("dve", tt_fn(sgT[32:56, tbs], sgf[:, :], sgT[0:24, tbs], ALU.subtract), reads=[B_sgf, B_sg[tb]], writes=[B_sg[tb]])

    if debug_stage == "gates":
        tmpf = Arena(nc, base + 38 * K, base + 70 * K, "dbg").take("dbgf", [128, 4096], F32)
        Bt = Buf()
        P.op("dve", lambda e: e.memset(tmpf[:, :], 0.0), writes=[Bt])
        P.op("dve", copy_fn("dve", tmpf[0:24, 0:2048], sgT[0:24, :]), reads=B_sg, writes=[Bt])
        P.op("dve", copy_fn("dve", tmpf[0:24, 2048:4096], sgT[32:56, :]), reads=B_sg, writes=[Bt])
        return dump([(tmpf[:, :], [Bt], F32)])
    rot3 = Rot([0, 1] if KEEP_WARM else [0, 1, 2])
    rotG = Rot([6, 7])
    state = {"rope": 0, "pt": 0, "acc": 0, "rda": 0}

    def rope_a(job):
        wcol0, tb, raw_ap, B_raw, out_ap, B_out = job
        bq = rot_all()
        proj_fm(wA, wcol0, 64, tb, bq, B_wA)
        P.op("act", copy_fn("act", raw_ap, pb[bq][0:64, :]), reads=[PB[bq]], writes=[B_raw])
        return bq

    def rope_b(job, bq):
        wcol0, tb, raw_ap, B_raw, out_ap, B_out = job
        tbs = slice(tb * 512, (tb + 1) * 512)
        bs = rot_all()
        P.op("pe", pe_fn([(pb[bs][0:64, :], permM[:, :], raw_ap, True, True)]), reads=[B_const, B_raw], writes=[PB[bs]])
        i = state["rope"] % 2
        state["rope"] += 1
        P.op("dve", tt_fn(rt1[i][:, :], pb[bq][0:64, :], CT[:, tbs], ALU.mult), reads=[PB[bq], B_catt], writes=[B_rt1[i]])
        P.op("dve", tt_fn(rt2[i][:, :], pb[bs][0:64, :], ST[:, tbs], ALU.mult), reads=[PB[bs], B_catt], writes=[B_rt2[i]])
        P.op("pool", tt_fn(out_ap, rt1[i][:, :], rt2[i][:, :], ALU.add), reads=[B_rt1[i], B_rt2[i]], writes=[B_out])

    def rope_jobs(jobs):
        prev = None
        for job in jobs:
            bq = rope_a(job)
            if prev is not None:
                rope_b(*prev)
            prev = (job, bq)
        rope_b(*prev)

    def normalize(bank, h, hh, tb, br, clamp):
        tbs = slice(tb * 512, (tb + 1) * 512)
        if clamp:
            P.op("dve", ts_fn(dent[:, :], pb[bank][64:128, :], 1e-30, None, ALU.max), reads=[PB[bank]], writes=[B_dent])
            P.op("dve", lambda e: e.reciprocal(rden[:, :], dent[:, :]), reads=[B_dent], writes=[B_rden])
            rd, B_rd = rden, B_rden
        elif br == 0 or (br == 1 and tb >= 2):
            P.op("dve", lambda e: e.reciprocal(rden[:, :], pb[bank][64:128, :]), reads=[PB[bank]], writes=[B_rden])
            rd, B_rd = rden, B_rden
        else:
            k = state["rda"] % 2
            state["rda"] += 1
            P.op("act", act_fn(lnb[k][:, :], pb[bank][64:128, :], AF.Ln), reads=[PB[bank]], writes=[B_lnb[k]])
            P.op("act", act_fn(rdenA[k][:, :], lnb[k][:, :], AF.Exp, scale=-1.0), reads=[B_lnb[k]], writes=[B_rdenA[k]])
            rd, B_rd = rdenA[k], B_rdenA[k]
        P.op("dve", tt_fn(nt1[:, :], pb[bank][0:64, :], rd[:, :], ALU.mult), reads=[PB[bank], B_rd], writes=[B_nt1])
        bg = rotG()
        j = h * 3 + br
        P.op("pe", pe_fn([(pb[bg][0:64, :], selall[:, j, :], sgT[:, tbs], True, True)]),
             reads=[B_catt, B_sg[tb]], writes=[PB[bg]])
        a = state["acc"] % 2
        if br == 0:
            P.op("dve", tt_fn(nacc[a][:, :], pb[bg][0:64, :], nt1[:, :], ALU.mult), reads=[PB[bg], B_nt1], writes=[B_nacc[a]])
        else:
            P.op("dve", tt_fn(nt2[:, :], pb[bg][0:64, :], nt1[:, :], ALU.mult), reads=[PB[bg], B_nt1], writes=[B_nt2])
            if br == 1:
                P.op("pool", tt_fn(nacc[a][:, :], nacc[a][:, :], nt2[:, :], ALU.add), reads=[B_nacc[a], B_nt2], writes=[B_nacc[a]])
            else:
                po = 64 * (h % 2)
                P.op("dve", tt_fn(oT[po:po + 64, h // 2, tbs], nacc[a][:, :], nt2[:, :], ALU.add),
                     reads=[B_nacc[a], B_nt2], writes=[B_oT[h]])
                state["acc"] += 1

    def attn_branch(items, acc_bank, q_of, k_of, v_of, krows, reads_q, reads_k, reads_v):
        n = len(items)
        sb_of = [None] * n
        pt_of = [None] * n

        def emit_s(ii):
            kt, lo, hi, mtile, mlo = items[ii]
            b = rot3()
            sb_of[ii] = b
            mms = [(pb[b][:, lo:hi], k_of(kt), q_of(lo, hi), True, mtile is None)]
            if mtile is not None:
                mms.append((pb[b][:, mlo:mlo + 128], ident[:, :], mtile[:, :], False, True))
            P.op("pe", pe_fn(mms), reads=reads_q + [reads_k(kt), B_const], writes=[PB[b]])

        emit_s(0)
        if n > 1:
            emit_s(1)
        for ii in range(n):
            kt, lo, hi, mtile, mlo = items[ii]
            b = sb_of[ii]
            pi = state["pt"] % 3
            state["pt"] += 1
            P.op("act", act_fn(pT[pi][:, lo:hi], pb[b][:, lo:hi], AF.Exp, scale=0.125), reads=[PB[b]], writes=[B_pT[pi]])
            if ii + 2 < n:
                emit_s(ii + 2)
            mm_list = [(pb[acc_bank][:, lo:hi], v_of(kt), pT[pi][:, lo:hi], ii == 0, ii == n - 1)]
            if KEEP_WARM:
                mm_list.append((pb[2][:, 0:KEEP_WARM], ident[:, :], cmask[:, 0:KEEP_WARM], True, True))
            P.op("pe", pe_fn(mm_list), reads=[B_pT[pi], reads_v(kt), B_const, B_catt],
                 writes=[PB[acc_bank]] + ([PB[2]] if KEEP_WARM else []))

    for g in range(2):
        jobs = []
        for hh in range(4):
            for tb in range(4):
                tbs = slice(tb * 512, (tb + 1) * 512)
                jobs.append((hh * 64, tb, qraw[:, hh, tbs], B_qraw[hh][tb], QA[0:64, hh, tbs], B_QAq[hh][tb]))
        for tb in range(4):
            tbs = slice(tb * 512, (tb + 1) * 512)
            jobs.append((256, tb, kraw[tb % 2][:, :], B_kraw[tb % 2], KA[0:64, tbs], B_KAk[tb]))
        for tb in range(4):
            tbs = slice(tb * 512, (tb + 1) * 512)
            jobs.append((320, tb, kraw[tb % 2][:, :], B_kraw[tb % 2], kwin[:, tbs], B_kwin[tb]))
        rope_jobs(jobs)
        for t4 in range(4):
            b = rot4()
            mms = []
            for i in range(4):
                tt = 4 * t4 + i
                for kc in range(8):
                    mms.append((pb[b][:, i * 128:(i + 1) * 128], xT[:, kc, tt * 128:(tt + 1) * 128], wA[:, kc, 384:512],
                                kc == 0, kc == 7))
            P.op("pe", pe_fn(mms), reads=[B_wA] + B_xT[4 * t4:4 * t4 + 4], writes=[PB[b]])
            P.op("act", copy_fn("act", Vtok[:, 4 * t4:4 * t4 + 4, :, 0:64],
                                pb[b][:, :].rearrange("p (a s c) -> p a s c", a=4, s=2)),
                 reads=[PB[b]], writes=[B_Vtok[t4]])
        if g == 0:
            P.dma("pool", dma_fn(wA[:], w_att_d[:, :, 1, 0:512]), writes=[B_wA])
        if debug_stage == "proj" and g == 0:
            tmpf = Arena(nc, base + 38 * K, base + 70 * K, "dbg").take("dbgf", [128, 8192], F32)
            Bt = Buf()
            P.op("dve", lambda e: e.memset(tmpf[:, :], 0.0), writes=[Bt])
            allq = [B_QAq[hh][tb] for hh in range(4) for tb in range(4)]
            P.op("dve", copy_fn("dve", tmpf[0:64, 0:2048], QA[0:64, 1, :]), reads=allq, writes=[Bt])
            P.op("dve", copy_fn("dve", tmpf[0:64, 2048:4096], KA[0:64, :]), reads=B_KAk, writes=[Bt])
            P.op("dve", copy_fn("dve", tmpf[0:64, 4096:6144], kwin[:, :]), reads=B_kwin, writes=[Bt])
            P.op("dve", copy_fn("dve", tmpf[:, 6144:8192].rearrange("p (a c) -> p a c", a=16), Vtok[:, :, 0, :]),
                 reads=B_Vtok, writes=[Bt])
            return dump([(tmpf[:, :], [Bt], F32)])

        for hh in range(4):
            for tb in range(4):
                tbs = slice(tb * 512, (tb + 1) * 512)
                b = rot3()
                P.op("pe", pe_fn([(pb[b][0:127, :], kcT[:, g, 0:127], qraw[:, hh, tbs], True, False),
                                  (pb[b][0:127, :], ident[0:127, 0:127], cmask[0:127, tbs], False, True)]),
                     reads=[B_kcT[g], B_qraw[hh][tb], B_const, B_catt], writes=[PB[b]])
                P.op("act", act_fn(pcT[0:127, hh, tbs], pb[b][0:127, :], AF.Exp, scale=0.125),
                     reads=[PB[b]], writes=[B_pcT[hh][tb]])
        def topk_a(tt):
            tb = tt // 4
            bi = 6
            mms = []
            for hh in range(4):
                mms.append((pb[bi][:, hh * 33:(hh + 1) * 33], pcT[0:127, hh, tt * 128:(tt + 1) * 128], maug[0:127, :], True, True))
            P.op("pe", pe_fn(mms), reads=[B_pcT[hh][tb] for hh in range(4)] + [B_const], writes=[PB[bi]])
            P.op("dve", lambda e: e.reciprocal(rec4[:, :], pb[6][:, 32:132:33]), reads=[PB[bi]], writes=[B_rec4])
            P.op("dve", ts_fn(iacc[:, :], pb[bi][:, 0:32], rec4[:, 0:1], None, ALU.mult), reads=[PB[bi], B_rec4], writes=[B_iacc])
            for hh in range(1, 4):
                P.op("dve", stt_fn(iacc[:, :], pb[bi][:, hh * 33:hh * 33 + 32], rec4[:, hh:hh + 1], iacc[:, :], ALU.mult, ALU.add),
                     reads=[PB[bi], B_rec4, B_iacc], writes=[B_iacc])
            P.op("dve", tt_fn(impm[:, :], iacc[:, :], keep[:, tt - 8, :], ALU.mult), reads=[B_iacc, B_catt], writes=[B_impm])
            P.op("dve", tt_fn(impm[:, :], impm[:, :], forceb[:, tt - 8, :], ALU.add), reads=[B_impm, B_catt], writes=[B_impm])
            P.op("dve", lambda e: e.max(out=mx8[:, :], in_=impm[:, :]), reads=[B_impm], writes=[B_mx8])
            P.op("dve", lambda e: e.match_replace(out=itmp[:, :], in_to_replace=mx8[:, :], in_values=impm[:, :], imm_value=-1.0),
                 reads=[B_impm, B_mx8], writes=[B_itmp])
            P.op("dve", lambda e: e.max(out=mx8[:, :], in_=itmp[:, :]), reads=[B_itmp], writes=[B_mx8])
            P.op("dve", ts_fn(m1[:, :], impm[:, :], mx8[:, 7:8], 1.0, ALU.is_ge, ALU.subtract), reads=[B_impm, B_mx8], writes=[B_m1])

        def topk_b(tt):
            tb = tt // 4
            bm = 5
            tts = slice(tt * 128, (tt + 1) * 128)
            P.op("pe", pe_fn([(pb[bm][64:96, 0:128], m1[:, :], ident[:, :], True, True)]),
                 reads=[B_m1, B_const], writes=[PB[bm]])
            for hh in range(4):
                P.op("dve", copy_fn("dve", QA[64:96, hh, tts], pb[bm][64:96, 0:128]), reads=[PB[bm]], writes=[B_QAm[hh][tb]])

        if debug_stage == "topk":
            for tt in range(8, 16):
                topk_a(tt)
                topk_b(tt)
        if debug_stage == "topk" and g == 0:
            tmpf = Arena(nc, base + 38 * K, base + 70 * K, "dbg").take("dbgf", [128, 4096], F32)
            Bt = Buf()
            P.op("dve", lambda e: e.memset(tmpf[:, :], 0.0), writes=[Bt])
            P.op("dve", copy_fn("dve", tmpf[64:96, 0:2048], QA[64:96, 2, :]),
                 reads=[B_QAm[2][tb] for tb in range(4)], writes=[Bt])
            P.op("dve", copy_fn("dve", tmpf[0:127, 2048:4096], pcT[0:127, 1, :]),
                 reads=[B_pcT[1][tb] for tb in range(4)], writes=[Bt])
            return dump([(tmpf[:, :], [Bt], F32)])

        def unit(hh, tb):
            h = 4 * g + hh
            if True:
                tbs = slice(tb * 512, (tb + 1) * 512)
                q0 = tb * 512
                P.op("pe", pe_fn([(pb[5][:, :], vcaug[0:127, g, :], pcT[0:127, hh, tbs], True, True)]),
                     reads=[B_vc[g], B_pcT[hh][tb]], writes=[PB[5]])
                normalize(5, h, hh, tb, 0, clamp=(tb == 0))
                items = []
                for kt in range(0, 4 * tb + 4):
                    i = kt - 4 * tb
                    if i < 0:
                        items.append((kt, 0, 512, None, 0))
                    else:
                        items.append((kt, i * 128, 512, triM, i * 128))
                attn_branch(items, 3,
                            q_of=lambda lo, hi, hh=hh, q0=q0: QA[0:96, hh, q0 + lo:q0 + hi],
                            k_of=lambda kt: KA[0:96, kt * 128:(kt + 1) * 128],
                            v_of=lambda kt: Vtok[:, kt, 0, :],
                            krows=96,
                            reads_q=[B_QAq[hh][tb], B_QAm[hh][tb], B_catt],
                            reads_k=lambda kt: B_KAk[kt // 4],
                            reads_v=lambda kt: B_Vtok[kt // 4])
                normalize(3, h, hh, tb, 1, clamp=False)
                items = []
                for i in range(4):
                    items.append((4 * tb + i, i * 128, 512, triM, i * 128))
                if tb > 0:
                    for i in range(4):
                        items.append((4 * tb - 4 + i, 0, (i + 1) * 128, antiM, i * 128))
                attn_branch(items, 4,
                            q_of=lambda lo, hi, hh=hh, q0=q0: QA[0:64, hh, q0 + lo:q0 + hi],
                            k_of=lambda kt: kwin[:, kt * 128:(kt + 1) * 128],
                            v_of=lambda kt: Vtok[:, kt, 1, :],
                            krows=64,
                            reads_q=[B_QAq[hh][tb]],
                            reads_k=lambda kt: B_kwin[kt // 4],
                            reads_v=lambda kt: B_Vtok[kt // 4])
                normalize(4, h, hh, tb, 2, clamp=False)

        early = [(hh, tb) for tb in range(2) for hh in range(4)]
        for idx, tt in enumerate(range(8, 16)):
            topk_a(tt)
            unit(*early[idx])
            topk_b(tt)
        for tb in range(2, 4):
            for hh in range(4):
                unit(hh, tb)
    if debug_stage == "att":
        tmpf = Arena(nc, base + 70 * K, SB_END, "dbg").take("dbgf", [128, 16384], F32)
        Bt = Buf()
        P.barrier()
        P.op("dve", lambda e: e.memset(tmpf[:, :], 0.0), writes=[Bt])
        for h in range(8):
            P.op("dve", copy_fn("dve", tmpf[0:64, h * 2048:(h + 1) * 2048], oT[64 * (h % 2):64 * (h % 2) + 64, h // 2, :]), reads=[B_oT[h]], writes=[Bt])
        return dump([(tmpf[:, :], [Bt], F32)])
    P.barrier()

    AY = Arena(nc, base + 70 * K, base + 86 * K, "y")
    ycT = AY.take("ycT", [128, 4, S], BF16)
    B_yc = [Buf() for _ in range(4)]
    TM = Arena(nc, base + 118 * K, SB_END, "tm")
    whbc = [TM.take("whbc%d" % i, [128, 8, 384], BF16) for i in range(2)]
    B_whbc = [Buf(), Buf()]
    h_sb = TM.take("h_sb", [128, S], F32)
    u_sb = TM.take("u_sb", [128, S + 2], F32)
    y_sb = TM.take("y_sb", [128, S], F32)
    B_h, B_u, B_y = Buf(), Buf(), Buf()
    P.op("pool", lambda e: e.memset(u_sb[:, 0:2], 0.0), writes=[B_u])
    rot6 = Rot([0, 1, 2, 3, 4, 5])
    P.dma("pool", dma_fn(whbc[0][:], w_hbc_d[:, :, 0, :]), writes=[B_whbc[0]])
    TB = Arena(nc, base + 156 * K, SB_END, "tb")
    wco = TB.take("wco", [128, 4, D], BF16)
    wno = TB.take("wno", [128, 4, D], BF16)
    wg2 = [TB.take("wg2_%d" % i, [128, 8, 256], BF16) for i in range(2)]
    B_wco, B_wno = Buf(), Buf()
    B_wg2 = [Buf(), Buf()]
    P.dma("pool", dma_fn(whbc[1][:], w_hbc_d[:, :, 1, :]), writes=[B_whbc[1]])
    P.dma("pool", dma_fn(wco[:], wco_d), writes=[B_wco])
    P.dma("pool", dma_fn(wno[:], wno_d), writes=[B_wno])
    P.dma("pool", dma_fn(wg2[0][:], w_g2_d[:, :, 0, :]), writes=[B_wg2[0]])
    for j in range(4):
        wt = whbc[j % 2]
        Bw = B_whbc[j % 2]
        if 1 <= j and j + 1 < 4:
            P.dma("pool", dma_fn(whbc[(j + 1) % 2][:], w_hbc_d[:, :, j + 1, :]), writes=[B_whbc[(j + 1) % 2]])
        for tb in range(4):
            tbs = slice(tb * 512, (tb + 1) * 512)
            b = rot6()
            proj_fm(wt, 0, 128, tb, b, Bw)
            P.op("act", copy_fn("act", h_sb[:, tbs], pb[b][:, :]), reads=[PB[b]], writes=[B_h])
        for tb in range(4):
            tbs = slice(tb * 512, (tb + 1) * 512)
            b = rot6()
            proj_fm(wt, 256, 128, tb, b, Bw)
            P.op("dve", tt_fn(u_sb[:, 2 + tb * 512:2 + (tb + 1) * 512], pb[b][:, :], h_sb[:, tbs], ALU.mult),
                 reads=[PB[b], B_h], writes=[B_u])
        P.op("act", act_fn(y_sb[:, :], u_sb[:, 2:S + 2], AF.Identity, scale=cw[:, j, 2:3]), reads=[B_u, B_const], writes=[B_y])
        P.op("dve", stt_fn(y_sb[:, :], u_sb[:, 1:S + 1], cw[:, j, 1:2], y_sb[:, :], ALU.mult, ALU.add),
             reads=[B_u, B_const, B_y], writes=[B_y])
        P.op("dve", stt_fn(y_sb[:, :], u_sb[:, 0:S], cw[:, j, 0:1], y_sb[:, :], ALU.mult, ALU.add),
             reads=[B_u, B_const, B_y], writes=[B_y])
        for tb in range(4):
            tbs = slice(tb * 512, (tb + 1) * 512)
            b = rot6()
            proj_fm(wt, 128, 128, tb, b, Bw)
            P.op("dve", tt_fn(ycT[:, j, tbs], pb[b][:, :], y_sb[:, tbs], ALU.mult), reads=[PB[b], B_y], writes=[B_yc[j]])
    if debug_stage == "conv":
        tmpf = Arena(nc, base + 38 * K, base + 70 * K, "dbg").take("dbgf", [128, 8192], F32)
        Bt = Buf()
        P.barrier()
        for j in range(4):
            P.op("dve", copy_fn("dve", tmpf[:, j * 2048:(j + 1) * 2048], ycT[:, j, :]), reads=[B_yc[j]], writes=[Bt])
        return dump([(tmpf[:, :], [Bt], F32)])
    P.barrier()

    AM = Arena(nc, base + 86 * K, base + 118 * K, "m")
    mixT = AM.take("mixT", [128, 8, S], BF16)
    B_mix = [Buf() for _ in range(4)]
    sgc = [TB.take("sgc%d" % i, [128, 512], F32) for i in range(2)]
    sgn = [TB.take("sgn%d" % i, [128, 512], F32) for i in range(2)]
    ma = [TB.take("ma%d" % i, [128, 512], F32) for i in range(2)]
    mb = [TB.take("mb%d" % i, [128, 512], F32) for i in range(2)]
    B_sgc, B_sgn, B_ma, B_mb = [Buf(), Buf()], [Buf(), Buf()], [Buf(), Buf()], [Buf(), Buf()]
    rot8 = Rot([0, 1, 2, 3, 4, 5, 6, 7])
    it = 0
    for fc in range(8):
        wt = wg2[fc % 2]
        Bw = B_wg2[fc % 2]
        if fc + 1 < 8:
            P.dma("pool", dma_fn(wg2[(fc + 1) % 2][:], w_g2_d[:, :, fc + 1, :]), writes=[B_wg2[(fc + 1) % 2]])
        fcs = slice(fc * 128, (fc + 1) * 128)
        for tb in range(4):
            tbs = slice(tb * 512, (tb + 1) * 512)
            i = it % 2
            it += 1
            b1, b2, b3, b4 = rot8(), rot8(), rot8(), rot8()
            proj_fm(wt, 0, 128, tb, b1, Bw)
            P.op("act", act_fn(sgc[i][:, :], pb[b1][:, :], AF.Sigmoid), reads=[PB[b1]], writes=[B_sgc[i]])
            proj_fm(wt, 128, 128, tb, b2, Bw)
            P.op("act", act_fn(sgn[i][:, :], pb[b2][:, :], AF.Sigmoid), reads=[PB[b2]], writes=[B_sgn[i]])
            mms = [(pb[b3][:, :], wco[:, kc, fcs], ycT[:, kc, tbs], kc == 0, kc == 3) for kc in range(4)]
            P.op("pe", pe_fn(mms), reads=[B_wco] + B_yc, writes=[PB[b3]])
            mms = [(pb[b4][:, :], wno[:, hp, fcs], oT[:, hp, tbs], hp == 0, hp == 3) for hp in range(4)]
            P.op("pe", pe_fn(mms), reads=[B_wno] + B_oT, writes=[PB[b4]])
            P.op("dve", tt_fn(ma[i][:, :], pb[b3][:, :], sgc[i][:, :], ALU.mult), reads=[PB[b3], B_sgc[i]], writes=[B_ma[i]])
            P.op("dve", tt_fn(mb[i][:, :], pb[b4][:, :], sgn[i][:, :], ALU.mult), reads=[PB[b4], B_sgn[i]], writes=[B_mb[i]])
            P.op("pool", tt_fn(mixT[:, fc, tbs], ma[i][:, :], mb[i][:, :], ALU.add), reads=[B_ma[i], B_mb[i]], writes=[B_mix[tb]])
    if debug_stage == "mix":
        tmpf = Arena(nc, base + 38 * K, base + 70 * K, "dbg").take("dbgf", [128, 8192], F32)
        Bt = Buf()
        P.barrier()
        for j in range(4):
            P.op("dve", copy_fn("dve", tmpf[:, j * 2048:(j + 1) * 2048], mixT[:, j, :]), reads=B_mix, writes=[Bt])
        return dump([(tmpf[:, :], [Bt], F32)])
    P.barrier()

    AR = Arena(nc, base + 118 * K, base + 182 * K, "r")
    resid = AR.take("resid", [128, 16, D], F32)
    B_res = [Buf() for _ in range(16)]
    TCm = Arena(nc, base + 38 * K, base + 86 * K, "tcm")
    wo = TCm.take("wo", [128, 8, D], BF16)
    B_wo = Buf()
    P.dma("pool", dma_fn(wo[:], wo_d), writes=[B_wo])
    TL = Arena(nc, base + 182 * K, SB_END, "tl")
    lnp = TL.take("lnp", [128, 2, D], F32)
    B_lnp = Buf()
    P.dma("sp", dma_fn(lnp[:], lnp_d[:, 0:2, :]), writes=[B_lnp])
    xin = [TCm.take("xin%d" % i, [128, D], F32) for i in range(4)]
    B_xin = [Buf() for _ in range(4)]
    rbuf = [TCm.take("rbuf%d" % i, [128, D], F32) for i in range(2)]
    B_rbuf = [Buf(), Buf()]
    xn = [TL.take("xn%d" % i, [128, D], F32) for i in range(2)]
    B_xn = [Buf(), Buf()]
    obuf = [TL.take("obuf%d" % i, [128, D], F32) for i in range(2)]
    B_obuf = [Buf(), Buf()]
    rl = [TL.take("rl%d" % i, [128, 512], F32) for i in range(2)]
    B_rl = [Buf(), Buf()]
    x1b = [TCm.take("x1b%d" % i, [128, D], BF16) for i in range(2)]
    B_x1b = [Buf(), Buf()]
    stats4 = [TL.take("stats4_%d" % i, [128, 4, 12], F32) for i in range(2)]
    mv4 = [TL.take("mv4_%d" % i, [128, 4, 2], F32) for i in range(2)]
    rstd4 = [TL.take("rstd4_%d" % i, [128, 4], F32) for i in range(2)]
    nmr4 = [TL.take("nmr4_%d" % i, [128, 4], F32) for i in range(2)]
    ssum4 = [TL.take("ssum4_%d" % i, [128, 4], F32) for i in range(2)]
    ssq4 = [TL.take("ssq4_%d" % i, [128, 4], F32) for i in range(2)]
    B_ssum = [Buf(), Buf()]
    B_stats, B_mv, B_rstd, B_nmr = [Buf(), Buf()], [Buf(), Buf()], [Buf(), Buf()], [Buf(), Buf()]
    state["ln"] = 0

    def ln_stats_a(tiles, on_act=False, act_tiles=None):
        n = len(tiles)
        s_ = state["ln"] % 2
        state["ln"] += 1
        if act_tiles is not None:
            inv = 1.0 / D
            for j, (src_ap, B_src, dst_ap, B_dst, xi, variant) in enumerate(tiles):
                if j in act_tiles:
                    P.op("act", lambda e, j=j, src_ap=src_ap, xi=xi: e.activation(xn[xi][:, :], src_ap, AF.Identity, accum_out=ssum4[s_][:, j:j + 1]),
                         reads=[B_src], writes=[B_xn[xi], B_ssum[s_]])
                    P.op("act", lambda e, j=j, src_ap=src_ap, xi=xi: e.activation(xn[xi][:, :], src_ap, AF.Square, accum_out=ssq4[s_][:, j:j + 1]),
                         reads=[B_src], writes=[B_xn[xi], B_ssum[s_]])
            for j, (src_ap, B_src, dst_ap, B_dst, xi, variant) in enumerate(tiles):
                if j not in act_tiles:
                    P.op("dve", lambda e, j=j, src_ap=src_ap: e.bn_stats(stats4[s_][:, j, 0:6], src_ap[:, 0:512]),
                         reads=[B_src], writes=[B_stats[s_]])
                    P.op("dve", lambda e, j=j, src_ap=src_ap: e.bn_stats(stats4[s_][:, j, 6:12], src_ap[:, 512:1024]),
                         reads=[B_src], writes=[B_stats[s_]])
                    P.op("dve", lambda e, j=j: e.bn_aggr(mv4[s_][:, j, :], stats4[s_][:, j, :]), reads=[B_stats[s_]], writes=[B_mv[s_]])
                    P.op("dve", ts_fn(rstd4[s_][:, j:j + 1], mv4[s_][:, j, 1:2], LN_EPS, None, ALU.add), reads=[B_mv[s_]], writes=[B_rstd[s_]])
            for j in sorted(act_tiles):
                P.op("dve", ts_fn(mv4[s_][:, j, 0:1], ssum4[s_][:, j:j + 1], inv, None, ALU.mult), reads=[B_ssum[s_]], writes=[B_mv[s_]])
                P.op("dve", tt_fn(ssum4[s_][:, j:j + 1], mv4[s_][:, j, 0:1], mv4[s_][:, j, 0:1], ALU.mult), reads=[B_mv[s_]], writes=[B_ssum[s_]])
                P.op("dve", ts_fn(rstd4[s_][:, j:j + 1], ssq4[s_][:, j:j + 1], inv, LN_EPS, ALU.mult, ALU.add), reads=[B_ssum[s_]], writes=[B_rstd[s_]])
                P.op("dve", tt_fn(rstd4[s_][:, j:j + 1], rstd4[s_][:, j:j + 1], ssum4[s_][:, j:j + 1], ALU.subtract),
                     reads=[B_rstd[s_], B_ssum[s_]], writes=[B_rstd[s_]])
            return s_
        if on_act:
            for j, (src_ap, B_src, dst_ap, B_dst, xi, variant) in enumerate(tiles):
                P.op("act", lambda e, j=j, src_ap=src_ap, xi=xi: e.activation(xn[xi][:, :], src_ap, AF.Identity, accum_out=ssum4[s_][:, j:j + 1]),
                     reads=[B_src], writes=[B_xn[xi], B_ssum[s_]])
                P.op("act", lambda e, j=j, src_ap=src_ap, xi=xi: e.activation(xn[xi][:, :], src_ap, AF.Square, accum_out=ssq4[s_][:, j:j + 1]),
                     reads=[B_src], writes=[B_xn[xi], B_ssum[s_]])
            inv = 1.0 / D
            P.op("dve", ts_fn(mv4[s_][:, 0:n, 0], ssum4[s_][:, 0:n], inv, None, ALU.mult), reads=[B_ssum[s_]], writes=[B_mv[s_]])
            P.op("dve", tt_fn(ssum4[s_][:, 0:n], mv4[s_][:, 0:n, 0], mv4[s_][:, 0:n, 0], ALU.mult), reads=[B_mv[s_]], writes=[B_ssum[s_]])
            P.op("dve", ts_fn(rstd4[s_][:, 0:n], ssq4[s_][:, 0:n], inv, LN_EPS, ALU.mult, ALU.add), reads=[B_ssum[s_]], writes=[B_rstd[s_]])
            P.op("dve", tt_fn(rstd4[s_][:, 0:n], rstd4[s_][:, 0:n], ssum4[s_][:, 0:n], ALU.subtract),
                 reads=[B_rstd[s_], B_ssum[s_]], writes=[B_rstd[s_]])
            return s_
        for j, (src_ap, B_src, dst_ap, B_dst, xi, variant) in enumerate(tiles):
            P.op("dve", lambda e, j=j, src_ap=src_ap: e.bn_stats(stats4[s_][:, j, 0:6], src_ap[:, 0:512]),
                 reads=[B_src], writes=[B_stats[s_]])
            P.op("dve", lambda e, j=j, src_ap=src_ap: e.bn_stats(stats4[s_][:, j, 6:12], src_ap[:, 512:1024]),
                 reads=[B_src], writes=[B_stats[s_]])
            P.op("dve", lambda e, j=j: e.bn_aggr(mv4[s_][:, j, :], stats4[s_][:, j, :]), reads=[B_stats[s_]], writes=[B_mv[s_]])
        P.op("dve", ts_fn(rstd4[s_][:, 0:n], mv4[s_][:, 0:n, 1], LN_EPS, None, ALU.add), reads=[B_mv[s_]], writes=[B_rstd[s_]])
        return s_

    def ln_stats_b(tiles, s_):
        n = len(tiles)
        P.op("act", act_fn(rstd4[s_][:, 0:n], rstd4[s_][:, 0:n], AF.Sqrt), reads=[B_rstd[s_]], writes=[B_rstd[s_]])
        P.op("dve", lambda e: e.reciprocal(rstd4[s_][:, 0:n], rstd4[s_][:, 0:n]), reads=[B_rstd[s_]], writes=[B_rstd[s_]])
        if any(t[5] == "actpool" for t in tiles):
            P.op("dve", stt_fn(nmr4[s_][:, 0:n], mv4[s_][:, 0:n, 0], -1.0, rstd4[s_][:, 0:n], ALU.mult, ALU.mult),
                 reads=[B_mv[s_], B_rstd[s_]], writes=[B_nmr[s_]])
        return s_

    def ln_apply(tiles, s_, after=None):
        for j, (src_ap, B_src, dst_ap, B_dst, xi, variant) in enumerate(tiles):
            if variant == "dve":
                P.op("dve", stt_fn(xn[xi][:, :], src_ap, mv4[s_][:, j, 0:1], lnp[:, 0, :], ALU.subtract, ALU.mult),
                     reads=[B_src, B_mv[s_], B_lnp], writes=[B_xn[xi]])
                P.op("dve", stt_fn(dst_ap, xn[xi][:, :], rstd4[s_][:, j:j + 1], lnp[:, 1, :], ALU.mult, ALU.add),
                     reads=[B_xn[xi], B_rstd[s_], B_lnp], writes=[B_dst])
            else:
                P.op("act", act_fn(xn[xi][:, :], src_ap, AF.Identity, scale=rstd4[s_][:, j:j + 1], bias=nmr4[s_][:, j:j + 1]),
                     reads=[B_src, B_rstd[s_], B_nmr[s_]], writes=[B_xn[xi]])
                P.op("pool", tt_fn(xn[xi][:, :], xn[xi][:, :], lnp[:, 0, :], ALU.mult), reads=[B_xn[xi], B_lnp], writes=[B_xn[xi]])
                P.op("pool", tt_fn(dst_ap, xn[xi][:, :], lnp[:, 1, :], ALU.add), reads=[B_xn[xi], B_lnp], writes=[B_dst])
            if after is not None:
                after(j)

    def ln_stats(tiles):
        return ln_stats_b(tiles, ln_stats_a(tiles))

    def ln_block(tiles, after=None):
        ln_apply(tiles, ln_stats(tiles), after)

    x_t = x_d.rearrange("(n p) d -> n p d", p=128)

    wo_banks = {}
    rotW = Rot([0, 1, 2, 3, 4, 5])
    rotT = Rot([6, 7])

    def mixc_pe(tt):
        i = tt % 2
        tts = slice(tt * 128, (tt + 1) * 128)
        P.dma("sp", dma_fn(xin[tt % 4][:, :], x_t[tt]), writes=[B_xin[tt % 4]])
        wo_banks[tt] = []
        for half in range(2):
            hs = slice(half * 512, (half + 1) * 512)
            b = rotW()
            wo_banks[tt].append(b)
            mms = [(pb[b][:, :], mixT[:, kc, tts], wo[:, kc, hs], kc == 0, kc == 7) for kc in range(8)]
            P.op("pe", pe_fn(mms), reads=[B_wo] + B_mix, writes=[PB[b]])

    def mixc_evac(tt):
        i = tt % 2
        for half in range(2):
            hs = slice(half * 512, (half + 1) * 512)
            b = wo_banks[tt][half]
            P.op("dve", stt_fn(rbuf[i][:, hs], xin[tt % 4][:, hs], ALPHA, pb[b][:, :], ALU.mult, ALU.add),
                 reads=[B_xin[tt % 4], PB[b]], writes=[B_rbuf[i]])

    def mixc_tiles(tt):
        i = tt % 2
        return [(rbuf[i][:, :], B_rbuf[i], rbuf[i][:, :], B_rbuf[i], i, "dve")]

    def mixc_tail(tt):
        i = tt % 2
        tts = slice(tt * 128, (tt + 1) * 128)
        P.op("act", lambda e: e.mul(resid[:, tt, :], rbuf[i][:, :], ALPHA), reads=[B_rbuf[i]], writes=[B_res[tt]])
        P.op("act", copy_fn("act", x1b[i][:, :], rbuf[i][:, :]), reads=[B_rbuf[i]], writes=[B_x1b[i]])
        for half in range(2):
            b = rotT()
            mms = []
            for j in range(4):
                kc = half * 4 + j
                mms.append((pb[b][:, j * 128:(j + 1) * 128], x1b[i][:, kc * 128:(kc + 1) * 128], ident[:], True, True))
            P.op("pe", pe_fn(mms), reads=[B_x1b[i], B_const], writes=[PB[b]])
            P.op("act", copy_fn("act", xT[:, half * 4:half * 4 + 4, tts], pb[b][:, :].rearrange("p (a b) -> p a b", a=4)),
                 reads=[PB[b]], writes=[B_xT[tt]])

    TF = Arena(nc, base + 38 * K, base + 118 * K, "tf")
    wu = [TF.take("wu%d" % i, [128, 8, 1024], BF16) for i in range(2)]
    wd = [TF.take("wd%d" % i, [128, 8, 1024], BF16) for i in range(2)]
    hT = [TF.take("hT%d" % i, [128, 8, 512], BF16) for i in range(2)]
    B_wu, B_wd, B_hT = [Buf(), Buf()], [Buf(), Buf()], [Buf(), Buf()]

    mixc_pe(0)
    mixc_pe(1)
    mixc_pe(2)
    mixc_evac(0)
    ln_block(mixc_tiles(0))
    for tt in range(16):
        if tt + 3 < 16:
            mixc_pe(tt + 3)
            if tt + 3 == 15:
                P.dma("pool", dma_fn(wu[0][:], wup_d[:, :, 0:1024]), writes=[B_wu[0], B_wo])
        if tt + 1 < 16:
            mixc_evac(tt + 1)
            s_next = ln_stats(mixc_tiles(tt + 1))
        mixc_tail(tt)
        if tt + 1 < 16:
            ln_apply(mixc_tiles(tt + 1), s_next)
    if debug_stage == "ln1":
        P.barrier()
        return dump([(resid[:, tt, :], [B_res[tt]], F32) for tt in range(16)])
    P.barrier()

    P.dma("sp", dma_fn(lnp[:], lnp_d[:, 2:4, :]), writes=[B_lnp])
    rotU = Rot([0, 1, 2, 3])
    rotD = Rot([4, 5, 6, 7])

    def ln2_tiles(tb_):
        tiles = []
        for t4 in range(4):
            tt = 4 * tb_ + t4
            tiles.append((resid[:, tt, :], B_res[tt], obuf[tt % 2][:, :], B_obuf[tt % 2], tt % 2, "dve"))

        def store(j):
            tt = 4 * tb_ + j
            P.dma("sp", dma_fn(y_d[tt * 128:(tt + 1) * 128, :], obuf[tt % 2][:, :]), reads=[B_obuf[tt % 2]], is_output=True)

        return tiles, store

    def load_wu(q):
        P.dma("pool", dma_fn(wu[q % 2][:], wup_d[:, :, q * 1024:(q + 1) * 1024]), writes=[B_wu[q % 2]])

    def load_wd(q):
        P.dma("pool", dma_fn(wd[q % 2][:], wdn_d[:, q * 8:(q + 1) * 8, :]), writes=[B_wd[q % 2]])

    def up_group(blk, f):
        q, tb = divmod(blk, 4)
        i = blk % 2
        b = rotU()
        mms = [(pb[b][:, :], wu[q % 2][:, kc, f * 128:(f + 1) * 128], xT[:, kc, tb * 512:(tb + 1) * 512], kc == 0, kc == 7)
               for kc in range(8)]
        P.op("pe", pe_fn(mms), reads=[B_wu[q % 2]] + B_xT[4 * tb:4 * tb + 4], writes=[PB[b]])
        r = f % 2
        P.op("act", act_fn(rl[r][:, :], pb[b][:, :], AF.Relu), reads=[PB[b]], writes=[B_rl[r]])
        P.op("pool", tt_fn(hT[i][:, f, :], rl[r][:, :], rl[r][:, :], ALU.mult), reads=[B_rl[r]], writes=[B_hT[i]])

    def down_group(blk, j):
        q, tb = divmod(blk, 4)
        i = blk % 2
        t4, half = divmod(j, 2)
        tt = 4 * tb + t4
        hs = slice(half * 512, (half + 1) * 512)
        b = rotD()
        mms = [(pb[b][:, :], hT[i][:, f, t4 * 128:(t4 + 1) * 128], wd[q % 2][:, f, hs], f == 0, f == 7) for f in range(8)]
        P.op("pe", pe_fn(mms), reads=[B_hT[i], B_wd[q % 2]], writes=[PB[b]])
        P.op("dve", tt_fn(resid[:, tt, hs], pb[b][:, :], resid[:, tt, hs], ALU.add),
             reads=[PB[b], B_res[tt]], writes=[B_res[tt]])

    load_wd(0)
    pending = None
    for s_blk in range(17):
        if s_blk < 16:
            q, tb = divmod(s_blk, 4)
            if tb == 0 and q + 1 < 4:
                load_wu(q + 1)
            if tb == 1 and q + 1 < 4:
                load_wd(q + 1)
        for j in range(8):
            if s_blk < 16:
                up_group(s_blk, j)
            if s_blk >= 1:
                down_group(s_blk - 1, j)
            if j == 3 and pending is not None:
                tiles_p, s_p, store_p = pending
                ln_apply(tiles_p, ln_stats_b(tiles_p, s_p), after=store_p)
                pending = None
        if s_blk >= 1 and (s_blk - 1) // 4 == 3:
            tiles_p, store_p = ln2_tiles((s_blk - 1) % 4)
            if s_blk == 16:
                pending = (tiles_p, ln_stats_a(tiles_p, act_tiles={0, 1}), store_p)
            else:
                pending = (tiles_p, ln_stats_a(tiles_p, on_act=True), store_p)
    tiles_p, s_p, store_p = pending
    ln_apply(tiles_p, ln_stats_b(tiles_p, s_p), after=store_p)

    P.run_block()
    P.close()
    return nc


def _constants():
    bf = ml_dtypes.bfloat16
    c = {}
    c["c_ident"] = np.eye(128, dtype=np.float32).astype(bf)
    kl = np.arange(128)[:, None]
    ql = np.arange(128)[None, :]
    c["c_tri"] = np.where(kl <= ql, 0.0, -BIG).astype(np.float32).astype(bf)
    c["c_anti"] = np.where(kl > ql, 0.0, -BIG).astype(np.float32).astype(bf)
    n = np.arange(128)[:, None]
    t = np.arange(S)[None, :]
    c["c_cmask"] = np.where(16 * n + 31 <= t, 0.0, -BIG).astype(np.float32).astype(bf)
    j = np.arange(32)[:, None]
    c["c_e30"] = np.where((t // 64) == j, BIG, 0.0).astype(np.float32).astype(bf)
    nn = np.arange(128)[:, None] * 16
    jj = np.arange(32)[None, :] * 64
    m = ((nn < jj + 64) & (nn + 32 > jj)).astype(np.float32)
    m[127, :] = 0.0
    maug = np.concatenate([m, np.ones((128, 1), np.float32)], axis=1)
    c["c_maug"] = maug.astype(bf)
    perm = np.zeros((64, 64), np.float32)
    for mm_ in range(16):
        partner = mm_ + 8 if mm_ < 8 else mm_ - 8
        perm[partner, mm_] = 1.0
    c["c_perm"] = perm.astype(bf)
    sel = np.zeros((56, 24, 64), np.float32)
    for r in range(24):
        sel[r, r, :] = 1.0
        sel[32 + r, r, :] = 1.0
    c["c_selall"] = sel.astype(bf)
    keep = np.zeros((128, 8, 32), np.float32)
    forceb = np.zeros((128, 8, 32), np.float32)
    for i in range(8):
        tt = 8 + i
        tq = tt * 128 + np.arange(128)
        cur = tq // 64
        for p in range(128):
            cu = cur[p]
            for jb in range(32):
                if jb == 0:
                    forceb[p, i, jb] = 3e9
                elif jb == cu:
                    forceb[p, i, jb] = 2e9
                elif jb == cu - 1:
                    forceb[p, i, jb] = 1e9
                elif jb > cu:
                    forceb[p, i, jb] = -1.0
                else:
                    keep[p, i, jb] = 1.0
    c["c_keep"] = keep
    c["c_forceb"] = forceb
    inv = ROPE_THETA ** (-np.arange(0, 16, 2, dtype=np.float32) / np.float32(16.0))
    ang = np.arange(S, dtype=np.float32)[:, None] * inv[None, :].astype(np.float32)
    cos = np.cos(ang).astype(np.float32).T
    sin = np.sin(ang).astype(np.float32).T
    ct = np.ones((64, S), np.float32)
    st = np.zeros((64, S), np.float32)
    ct[0:8] = cos
    ct[8:16] = cos
    st[0:8] = -sin
    st[8:16] = sin
    c["c_ct"] = ct
    c["c_st"] = st
    return c


def _prep_weights(inp):
    f = lambda a: np.ascontiguousarray(a, dtype=np.float32)
    w_in = np.asarray(inp["w_in"])[0]
    wr = w_in.reshape(8, 128, 4888).transpose(1, 0, 2)
    out = {}
    hbc = np.stack([np.concatenate([wr[:, :, j * 128:(j + 1) * 128],
                                    wr[:, :, 512 + j * 128:512 + (j + 1) * 128],
                                    wr[:, :, 1024 + j * 128:1024 + (j + 1) * 128]], axis=2) for j in range(4)], axis=2)
    out["w_hbc"] = f(hbc)
    att = []
    for g in range(2):
        parts = [wr[:, :, 1536 + g * 256:1536 + (g + 1) * 256],
                 wr[:, :, 2304 + g * 64:2304 + (g + 1) * 64],
                 wr[:, :, 2560 + g * 64:2560 + (g + 1) * 64],
                 wr[:, :, 2432 + g * 64:2432 + (g + 1) * 64],
                 wr[:, :, 2688 + g * 64:2688 + (g + 1) * 64],
                 wr[:, :, 2048 + g * 64:2048 + (g + 1) * 64],
                 wr[:, :, 2176 + g * 64:2176 + (g + 1) * 64]]
        att.append(np.concatenate(parts, axis=2))
    out["w_att"] = f(np.stack(att, axis=2))
    out["w_gate"] = f(wr[:, :, 2816:2840])
    g2 = np.stack([np.concatenate([wr[:, :, 2840 + fc * 128:2840 + (fc + 1) * 128],
                                   wr[:, :, 3864 + fc * 128:3864 + (fc + 1) * 128]], axis=2) for fc in range(8)], axis=2)
    out["w_g2"] = f(g2)
    conv_w = np.asarray(inp["conv_w"])[0][:, 0, :]
    out["cw"] = f(conv_w.reshape(3, 4, 128).transpose(2, 1, 0))
    out["wco"] = f(np.asarray(inp["w_conv_out"])[0].reshape(4, 128, 1024).transpose(1, 0, 2))
    out["wno"] = f(np.asarray(inp["w_nsa_out"])[0].reshape(4, 128, 1024).transpose(1, 0, 2))
    out["wo"] = f(np.asarray(inp["w_o"])[0].reshape(8, 128, 1024).transpose(1, 0, 2))
    out["wup"] = f(np.asarray(inp["w_up"])[0].reshape(8, 128, 4096).transpose(1, 0, 2))
    out["wdn"] = f(np.asarray(inp["w_down"])[0].reshape(32, 128, 1024).transpose(1, 0, 2))
    out["w1k"] = f(np.asarray(inp["w_k_cmp1"])[0].transpose(1, 0, 2))
    out["w1v"] = f(np.asarray(inp["w_v_cmp1"])[0].transpose(1, 0, 2))
    out["w2k"] = f(np.asarray(inp["w_k_cmp2"])[0])
    out["w2v"] = f(np.asarray(inp["w_v_cmp2"])[0])
    out["peTk"] = f(np.asarray(inp["pe_k_cmp"])[0].T)
    out["peTv"] = f(np.asarray(inp["pe_v_cmp"])[0].T)
    lnp = np.stack([np.asarray(inp["ln1_g"])[0], np.asarray(inp["ln1_b"])[0],
                    np.asarray(inp["ln2_g"])[0], np.asarray(inp["ln2_b"])[0]], axis=0)
    out["lnp"] = f(np.broadcast_to(lnp[None], (128, 4, 1024)))
    return out


def kernel(**inputs):
    x = np.asarray(inputs["x"], dtype=np.float32)
    shared = _prep_weights(inputs)
    shared.update(_constants())
    nc = build_program(DEBUG_STAGE)
    in_maps = []
    for b in range(NCORES):
        m = dict(shared)
        m["x"] = np.ascontiguousarray(x[b])
        in_maps.append(m)
    res = run_bass_kernel_spmd(nc, in_maps, core_ids=list(range(NCORES)))
    if DEBUG_STAGE is not None:
        return np.stack([np.asarray(r["dbg"]) for r in res.results], axis=0)
    return np.stack([np.asarray(r["y"], dtype=np.float32) for r in res.results], axis=0)
```
